# Optimizing a Trainium2 kernel written in Bass

```python
import jax
import jax.numpy as jnp
from jax import lax
import numpy as np

D_MODEL = 1024
BATCH = 16
SEQ = 4096
DEPTH = 4

GRID_W = 64
CTX_LEN = 256
EPS = 1e-6

DN_HEADS = 4
DN_DK = 128
DN_DV = 128
DN_CONV = 4
DN_CHUNK = 64
HG_HEADS = 4
HG_DK = 128
HG_DV = 128
HG_CHUNK = 64
LRU_HEADS = 4
LRU_WIDTH = 512
LRU_CONV = 4
LRU_C = 8.0
ATT_Q_HEADS = 8
ATT_KV_HEADS = 2
ATT_HEAD_DIM = 64
ATT_BLOCK = 128
ROPE_THETA = 10000.0

N_BRANCH = 4
BRANCH_WIDTH = 512
D_FF = 4 * D_MODEL

IN_SPLITS = (
    DN_HEADS * DN_DK, DN_HEADS * DN_DK, DN_HEADS * DN_DV, DN_HEADS * DN_DV, 2 * DN_HEADS, 2 * DN_HEADS,
    HG_HEADS * HG_DK, HG_HEADS * HG_DK, HG_HEADS * HG_DK, HG_HEADS * HG_DV, HG_HEADS * HG_DV,
    LRU_WIDTH, LRU_WIDTH,
    ATT_Q_HEADS * ATT_HEAD_DIM, ATT_KV_HEADS * ATT_HEAD_DIM, ATT_KV_HEADS * ATT_HEAD_DIM,
    N_BRANCH * D_MODEL,
)
IN_WIDTH = sum(IN_SPLITS)

kernel_name = 'hybrid_dit_deltanet_hgrn2_rglru_gqa'


def _rms(x, w):
    xf = x.astype(jnp.float32)
    y = xf * lax.rsqrt(jnp.mean(xf * xf, axis=-1, keepdims=True) + EPS)
    return (y * w.astype(jnp.float32)).astype(x.dtype)


def _l2n(x):
    return x * lax.rsqrt(jnp.sum(x * x, axis=-1, keepdims=True) + EPS)


def _modulate(h, shift, scale):
    return h * (1.0 + scale) + shift


def _split_cols(u):
    idx = [int(i) for i in np.cumsum(IN_SPLITS)[:-1]]
    return jnp.split(u, idx, axis=-1)


def _dwconv(x, w):
    k = w.shape[0]
    t = x.shape[1]
    xp = jnp.pad(x, ((0, 0), ((k - 1) // 2, k // 2), (0, 0)))
    return sum(xp[:, j:j + t] * w[j] for j in range(k))


def _tflip(a, rev):
    return jnp.flip(a, axis=1) if rev else a


def _to_chunks(a, cs):
    bsz, t, h = a.shape[:3]
    a = a.reshape((bsz, t // cs, cs, h) + a.shape[3:])
    return jnp.moveaxis(a, (1, 3), (0, 2))


def _from_chunks(o):
    n, bsz, h, cs = o.shape[:4]
    return jnp.moveaxis(o, (0, 2), (1, 3)).reshape((bsz, n * cs, h) + o.shape[4:])


def _bidirectional(run, ctx_dirs, lat_dirs, s0):
    o_ctx, o_lat = [], []
    for d in range(2):
        rev = d == 1
        oc, s_ctx = run(tuple(_tflip(a, rev) for a in ctx_dirs[d]), s0)
        ol, _ = run(tuple(_tflip(a, rev) for a in lat_dirs[d]), s_ctx)
        o_ctx.append(_tflip(oc, rev))
        o_lat.append(_tflip(ol, rev))
    return o_ctx[0] + o_ctx[1], o_lat[0] + o_lat[1]


def _gdn_chunked(q, k, v, g, beta, s0):
    dv = v.shape[-1]
    q, k, v, g, beta = (_to_chunks(a, DN_CHUNK) for a in (q, k, v, g, beta))
    gam = jnp.cumsum(g, axis=-1)
    diff = gam[..., :, None] - gam[..., None, :]
    pos = jnp.arange(DN_CHUNK)
    incl = pos[:, None] >= pos[None, :]
    strict = pos[:, None] > pos[None, :]
    kb = k * beta[..., None]
    a_low = jnp.einsum('nbhid,nbhjd->nbhij', kb, k) * jnp.exp(jnp.where(strict, diff, -jnp.inf))
    eye = jnp.eye(DN_CHUNK, dtype=a_low.dtype)
    rhs = jnp.concatenate([v * beta[..., None], kb * jnp.exp(gam)[..., None]], axis=-1)
    sol = lax.linalg.triangular_solve(a_low + eye, rhs, left_side=True, lower=True, unit_diagonal=True)
    u, w = sol[..., :dv], sol[..., dv:]
    a_qk = jnp.einsum('nbhid,nbhjd->nbhij', q, k) * jnp.exp(jnp.where(incl, diff, -jnp.inf))
    q_dec = q * jnp.exp(gam)[..., None]
    k_dec = k * jnp.exp(gam[..., -1:] - gam)[..., None]
    c_dec = jnp.exp(gam[..., -1])[..., None, None]

    def step(s, xs):
        u_i, w_i, aqk_i, qd_i, kd_i, cd_i = xs
        v_new = u_i - jnp.einsum('bhck,bhkv->bhcv', w_i, s)
        o_i = jnp.einsum('bhck,bhkv->bhcv', qd_i, s) + jnp.einsum('bhcj,bhjv->bhcv', aqk_i, v_new)
        s = s * cd_i + jnp.einsum('bhck,bhcv->bhkv', kd_i, v_new)
        return s, o_i

    s_fin, o = lax.scan(step, s0, (u, w, a_qk, q_dec, k_dec, c_dec))
    return _from_chunks(o), s_fin


def _dn_prep(parts, conv_w, a_log, dt_bias):
    q, k, v, _, beta_raw, alpha_raw = parts
    bsz, t, _ = q.shape
    qkv = jax.nn.silu(_dwconv(jnp.concatenate([q, k, v], axis=-1), conv_w)).astype(jnp.float32)
    wk = DN_HEADS * DN_DK
    q = _l2n(qkv[..., :wk].reshape(bsz, t, DN_HEADS, DN_DK)) * (DN_DK ** -0.5)
    k = _l2n(qkv[..., wk:2 * wk].reshape(bsz, t, DN_HEADS, DN_DK))
    v = qkv[..., 2 * wk:].reshape(bsz, t, DN_HEADS, DN_DV)
    beta = jax.nn.sigmoid(beta_raw.astype(jnp.float32)).reshape(bsz, t, 2, DN_HEADS)
    g = -jnp.exp(a_log.astype(jnp.float32)) * jax.nn.softplus(
        alpha_raw.astype(jnp.float32).reshape(bsz, t, 2, DN_HEADS) + dt_bias.astype(jnp.float32))
    return [(q, k, v, g[:, :, d], beta[:, :, d]) for d in range(2)]


def _gated_out(o, gate, norm_w):
    bsz, t, h, dv = o.shape
    y = _rms(o, norm_w) * jax.nn.silu(gate.astype(jnp.float32)).reshape(bsz, t, h, dv)
    return y.reshape(bsz, t, h * dv).astype(gate.dtype)


def _deltanet_mixer(pc, pl, conv_w, a_log, dt_bias, norm_w, need_ctx):
    s0 = jnp.zeros((pc[0].shape[0], DN_HEADS, DN_DK, DN_DV), jnp.float32)
    oc, ol = _bidirectional(lambda xs, s: _gdn_chunked(*xs, s),
                            _dn_prep(pc, conv_w, a_log, dt_bias), _dn_prep(pl, conv_w, a_log, dt_bias), s0)
    y_ctx = _gated_out(oc, pc[3], norm_w) if need_ctx else None
    return y_ctx, _gated_out(ol, pl[3], norm_w)


def _hgrn2_chunked(q, k, v, logf, s0):
    qs, ks, vs = _to_chunks(q, HG_CHUNK), _to_chunks(k, HG_CHUNK), _to_chunks(v, HG_CHUNK)
    gs = jnp.cumsum(_to_chunks(logf, HG_CHUNK), axis=3)
    pos = jnp.arange(HG_CHUNK)
    incl = (pos[:, None] >= pos[None, :])[:, :, None]

    def step(s, xs):
        qc, kc, vc, gc = xs
        diff = gc[:, :, :, None, :] - gc[:, :, None, :, :]
        dec = jnp.exp(jnp.where(incl, diff, -jnp.inf))
        scores = jnp.einsum('bhtd,bhsd,bhtsd->bhts', qc, kc, dec)
        o = jnp.einsum('bhtd,bhde->bhte', qc * jnp.exp(gc), s) + jnp.einsum('bhts,bhse->bhte', scores, vc)
        g_last = gc[:, :, -1:, :]
        s = s * jnp.exp(g_last[:, :, 0, :])[..., None] + jnp.einsum(
            'bhsd,bhse->bhde', kc * jnp.exp(g_last - gc), vc)
        return s, o

    s_fin, o = lax.scan(step, s0, (qs, ks, vs, gs))
    return _from_chunks(o), s_fin


def _hgrn2_lower_bounds(p):
    sm = jax.nn.softmax(p.astype(jnp.float32), axis=1)
    cs = jnp.cumsum(sm, axis=1)
    return cs - cs[:, :1]


def _hgrn2_prep(parts, lb):
    q, f_fwd, f_bwd, i, _ = parts
    bsz, t, _ = q.shape
    shp = (bsz, t, HG_HEADS, HG_DK)
    q = jax.nn.silu(q.astype(jnp.float32)).reshape(shp)
    v = i.astype(jnp.float32).reshape(bsz, t, HG_HEADS, HG_DV)
    dirs = []
    for d, fr in enumerate((f_fwd, f_bwd)):
        f = lb[d] + (1.0 - lb[d]) * jax.nn.sigmoid(fr.astype(jnp.float32))
        dirs.append((q, (1.0 - f).reshape(shp), v, jnp.log(f).reshape(shp)))
    return dirs


def _hgrn2_mixer(pc, pl, lb, norm_w, need_ctx):
    s0 = jnp.zeros((pc[0].shape[0], HG_HEADS, HG_DK, HG_DV), jnp.float32)
    oc, ol = _bidirectional(lambda xs, s: _hgrn2_chunked(*xs, s), _hgrn2_prep(pc, lb), _hgrn2_prep(pl, lb), s0)
    y_ctx = _gated_out(oc, pc[4], norm_w) if need_ctx else None
    return y_ctx, _gated_out(ol, pl[4], norm_w)


def _linear_scan(xs, h0):
    a, b = xs

    def comb(l, r):
        return l[0] * r[0], r[0] * l[1] + r[1]

    a_cum, b_cum = lax.associative_scan(comb, (a, b), axis=1)
    h = a_cum * h0[:, None, :] + b_cum
    return h, h[:, -1]


def _rglru_prep(parts, conv_w, conv_b, w_a, b_a, w_x, b_x, lam):
    xb = parts[0]
    bsz, t, _ = xb.shape
    xc = (_dwconv(xb, conv_w) + conv_b).astype(jnp.float32)
    xh = xc.reshape(bsz, t, LRU_HEADS, LRU_WIDTH // LRU_HEADS)
    dirs = []
    for d in range(2):
        r = jax.nn.sigmoid(jnp.einsum('bthi,hij->bthj', xh, w_a[d].astype(jnp.float32)).reshape(bsz, t, LRU_WIDTH) + b_a[d])
        ig = jax.nn.sigmoid(jnp.einsum('bthi,hij->bthj', xh, w_x[d].astype(jnp.float32)).reshape(bsz, t, LRU_WIDTH) + b_x[d])
        log_a = -LRU_C * r * jax.nn.softplus(-lam[d].astype(jnp.float32))
        dirs.append((jnp.exp(log_a), jnp.sqrt(-jnp.expm1(2.0 * log_a)) * ig * xc))
    return dirs


def _rglru_mixer(pc, pl, conv_w, conv_b, w_a, b_a, w_x, b_x, lam, need_ctx):
    h0 = jnp.zeros((pc[0].shape[0], LRU_WIDTH), jnp.float32)
    prm = (conv_w, conv_b, w_a, b_a, w_x, b_x, lam)
    hc, hl = _bidirectional(_linear_scan, _rglru_prep(pc, *prm), _rglru_prep(pl, *prm), h0)

    def out(h, gb):
        return (h * jax.nn.gelu(gb.astype(jnp.float32))).astype(gb.dtype)

    y_ctx = out(hc, pc[1]) if need_ctx else None
    return y_ctx, out(hl, pl[1])


def _rope_2d_tables(rows):
    row_id = jnp.repeat(jnp.arange(rows), GRID_W).astype(jnp.float32)
    col_id = jnp.tile(jnp.arange(GRID_W), rows).astype(jnp.float32)
    axis_dim = ATT_HEAD_DIM // 2
    inv = ROPE_THETA ** (-jnp.arange(0, axis_dim, 2, dtype=jnp.float32) / axis_dim)
    ang = jnp.stack([row_id[:, None] * inv, col_id[:, None] * inv], axis=1)
    return jnp.cos(ang), jnp.sin(ang)


def _rope_2d(x, cos, sin):
    bsz, t, h, dh = x.shape
    xr = x.astype(jnp.float32).reshape(bsz, t, h, 2, 2, dh // 4)
    x1, x2 = xr[..., 0, :], xr[..., 1, :]
    c, s = cos[None, :, None], sin[None, :, None]
    out = jnp.stack([x1 * c - x2 * s, x2 * c + x1 * s], axis=-2)
    return out.reshape(bsz, t, h, dh).astype(x.dtype)


def _block_attention(q, keys, vals):
    bsz, t, hq, dh = q.shape
    grp = hq // ATT_KV_HEADS
    qb = jnp.moveaxis(q.reshape(bsz, t // ATT_BLOCK, ATT_BLOCK, ATT_KV_HEADS, grp, dh), 1, 0)
    scale = dh ** -0.5

    def one(qblk):
        s = jnp.einsum('bqkgd,bskd->bkgqs', qblk, keys).astype(jnp.float32) * scale
        p = jax.nn.softmax(s, axis=-1).astype(vals.dtype)
        return jnp.einsum('bkgqs,bskd->bqkgd', p, vals)

    o = lax.map(one, qb)
    return jnp.moveaxis(o, 0, 1).reshape(bsz, t, hq * dh)


def _gqa_mixer(pc, pl, qn_w, kn_w, cos, sin, need_ctx):
    def heads(parts):
        q, k, v = parts
        bsz, t, _ = q.shape
        q = _rms(q.reshape(bsz, t, ATT_Q_HEADS, ATT_HEAD_DIM), qn_w)
        k = _rms(k.reshape(bsz, t, ATT_KV_HEADS, ATT_HEAD_DIM), kn_w)
        return q, k, v.reshape(bsz, t, ATT_KV_HEADS, ATT_HEAD_DIM)

    cq, ck, cv = heads(pc)
    lq, lk, lv = heads(pl)
    lq, lk = _rope_2d(lq, cos, sin), _rope_2d(lk, cos, sin)
    y_lat = _block_attention(lq, jnp.concatenate([ck, lk], axis=1), jnp.concatenate([cv, lv], axis=1))
    y_ctx = _block_attention(cq, ck, cv) if need_ctx else None
    return y_ctx, y_lat


def _residual_update(h, ys, gate_raw, mods, norm2_w, w_branch, w_out, w1, w2):
    bsz, t, _ = gate_raw.shape
    g = gate_raw.reshape(bsz, t, N_BRANCH, D_MODEL)
    merged = sum(jax.nn.sigmoid(g[:, :, b]) * (ys[b] @ w_branch[b]) for b in range(N_BRANCH))
    h = h + mods[2] * (merged @ w_out)
    z = _modulate(_rms(h, norm2_w), mods[3], mods[4])
    return h + mods[5] * (jnp.square(jax.nn.relu(z @ w1)) @ w2)


def setup_inputs(seed: int = 0) -> dict:
    key = jax.random.key(seed)
    k = jax.random.split(key, 32)
    f32 = jnp.float32

    def nrm(i, shape, scale):
        return jax.random.normal(k[i], shape, f32) * scale

    hd = LRU_WIDTH // LRU_HEADS
    dt = jnp.exp(jax.random.uniform(k[10], (DEPTH, 2, DN_HEADS), f32, np.log(1e-3), np.log(1e-1)))
    a_c = jax.random.uniform(k[20], (DEPTH, 2, LRU_WIDTH), f32, 0.9, 0.999)
    s_l = a_c ** (1.0 / LRU_C)
    return {
        'x': nrm(0, (BATCH, SEQ, D_MODEL), 1.0),
        'c': nrm(1, (BATCH, D_MODEL), 1.0),
        'ctx': nrm(2, (BATCH, CTX_LEN, D_MODEL), 1.0),
        'c_ctx': nrm(3, (D_MODEL,), 1.0),
        'mod_w': nrm(4, (DEPTH, D_MODEL, 6 * D_MODEL), 0.5 * D_MODEL ** -0.5),
        'mod_b': nrm(5, (DEPTH, 6 * D_MODEL), 0.01),
        'norm1_w': 1.0 + nrm(6, (DEPTH, D_MODEL), 0.02),
        'norm2_w': 1.0 + nrm(7, (DEPTH, D_MODEL), 0.02),
        'w_in': nrm(8, (DEPTH, D_MODEL, IN_WIDTH), D_MODEL ** -0.5),
        'dn_conv_w': nrm(9, (DEPTH, DN_CONV, DN_HEADS * (2 * DN_DK + DN_DV)), DN_CONV ** -0.5),
        'dn_a_log': jnp.log(jax.random.uniform(k[11], (DEPTH, 2, DN_HEADS), f32, 1.0, 16.0)),
        'dn_dt_bias': dt + jnp.log(-jnp.expm1(-dt)),
        'dn_norm_w': 1.0 + nrm(12, (DEPTH, DN_DV), 0.02),
        'hg_lower_bounds': 1.0 + nrm(13, (2, DEPTH, HG_HEADS * HG_DK), 0.1),
        'hg_norm_w': 1.0 + nrm(14, (DEPTH, HG_DV), 0.02),
        'lru_conv_w': nrm(15, (DEPTH, LRU_CONV, LRU_WIDTH), LRU_CONV ** -0.5),
        'lru_conv_b': nrm(16, (DEPTH, LRU_WIDTH), 0.01),
        'lru_w_a': nrm(17, (DEPTH, 2, LRU_HEADS, hd, hd), hd ** -0.5),
        'lru_b_a': nrm(18, (DEPTH, 2, LRU_WIDTH), 0.01),
        'lru_w_x': nrm(19, (DEPTH, 2, LRU_HEADS, hd, hd), hd ** -0.5),
        'lru_b_x': nrm(21, (DEPTH, 2, LRU_WIDTH), 0.01),
        'lru_lambda': jnp.log(s_l) - jnp.log1p(-s_l),
        'att_q_norm_w': 1.0 + nrm(22, (DEPTH, ATT_HEAD_DIM), 0.02),
        'att_k_norm_w': 1.0 + nrm(23, (DEPTH, ATT_HEAD_DIM), 0.02),
        'w_branch': nrm(24, (DEPTH, N_BRANCH, BRANCH_WIDTH, D_MODEL), BRANCH_WIDTH ** -0.5),
        'w_out': nrm(25, (DEPTH, D_MODEL, D_MODEL), D_MODEL ** -0.5),
        'mlp_w1': nrm(26, (DEPTH, D_MODEL, D_FF), D_MODEL ** -0.5),
        'mlp_w2': nrm(27, (DEPTH, D_FF, D_MODEL), D_FF ** -0.5),
    }


def reference(x, c, ctx, c_ctx, mod_w, mod_b, norm1_w, norm2_w, w_in, dn_conv_w, dn_a_log, dn_dt_bias,
              dn_norm_w, hg_lower_bounds, hg_norm_w, lru_conv_w, lru_conv_b, lru_w_a, lru_b_a, lru_w_x,
              lru_b_x, lru_lambda, att_q_norm_w, att_k_norm_w, w_branch, w_out, mlp_w1, mlp_w2):
    rows = x.shape[1] // GRID_W
    cos, sin = _rope_2d_tables(rows)
    lb_all = _hgrn2_lower_bounds(hg_lower_bounds)
    silu_c = jax.nn.silu(c)
    silu_cc = jax.nn.silu(c_ctx)
    h_lat, h_ctx = x, ctx
    for layer in range(DEPTH):
        upd = layer < DEPTH - 1
        mod_lat = jnp.split((silu_c @ mod_w[layer] + mod_b[layer])[:, None, :], 6, axis=-1)
        mod_ctx = jnp.split(silu_cc @ mod_w[layer] + mod_b[layer], 6, axis=-1)
        pl = _split_cols(_modulate(_rms(h_lat, norm1_w[layer]), mod_lat[0], mod_lat[1]) @ w_in[layer])
        pc = _split_cols(_modulate(_rms(h_ctx, norm1_w[layer]), mod_ctx[0], mod_ctx[1]) @ w_in[layer])
        ya = _deltanet_mixer(pc[0:6], pl[0:6], dn_conv_w[layer], dn_a_log[layer], dn_dt_bias[layer],
                             dn_norm_w[layer], upd)
        yb = _hgrn2_mixer(pc[6:11], pl[6:11], lb_all[:, layer], hg_norm_w[layer], upd)
        yc = _rglru_mixer(pc[11:13], pl[11:13], lru_conv_w[layer], lru_conv_b[layer], lru_w_a[layer],
                          lru_b_a[layer], lru_w_x[layer], lru_b_x[layer], lru_lambda[layer], upd)
        yd = _gqa_mixer(pc[13:16], pl[13:16], att_q_norm_w[layer], att_k_norm_w[layer], cos, sin, upd)
        lw = (norm2_w[layer], w_branch[layer], w_out[layer], mlp_w1[layer], mlp_w2[layer])
        new_lat = _residual_update(h_lat, [ya[1], yb[1], yc[1], yd[1]], pl[16], mod_lat, *lw)
        if upd:
            h_ctx = _residual_update(h_ctx, [ya[0], yb[0], yc[0], yd[0]], pc[16], mod_ctx, *lw)
        h_lat = new_lat
    return h_lat
```

```python
import numpy as np
import concourse.bass as bass
import concourse.mybir as mybir
from concourse.bass_utils import run_bass_kernel_spmd

F32 = mybir.dt.float32
BF16 = mybir.dt.bfloat16
AF = mybir.ActivationFunctionType
ALU = mybir.AluOpType
AX = mybir.AxisListType

SAME_ENGINE_SYNC = True


class StopBuild(Exception):
    pass


class Trk:
    __slots__ = ("w", "r_eng", "r_dma", "excl")

    def __init__(self):
        self.excl = False
        self.w = None
        self.r_eng = {}
        self.r_dma = {}


class V:
    __slots__ = ("ap", "trks", "t")

    def __init__(self, ap, trks, t=None):
        self.ap = ap
        self.trks = trks
        self.t = t

    def __getitem__(self, idx):
        return V(self.ap[idx], self.trks, self.t)

    def rr(self, pat, **kw):
        return V(self.ap.rearrange(pat, **kw), self.trks, self.t)

    def bc(self, shape):
        return V(self.ap.broadcast_to(shape) if hasattr(self.ap, "broadcast_to") else self.ap.to_broadcast(shape), self.trks)


class T:
    def __init__(self, handle, name, is_dram=True):
        self.h = handle
        self.name = name
        self.trk = Trk()
        self.parts = {}
        self.is_dram = is_dram
        self.dsem = None

    def ap(self):
        h = self.h
        return h.ap() if hasattr(h, "ap") else h[:]

    def __getitem__(self, idx):
        return V(self.h[idx], [self.trk], self)

    def p(self, key, idx=None):
        if key not in self.parts:
            self.parts[key] = Trk()
        a = self.h[idx] if idx is not None else self.ap()
        return V(a, [self.parts[key]], self)


class MK:
    def __init__(self, nc, es):
        self.nc = nc
        self.es = es
        self.E = {"pe": nc.tensor, "act": nc.scalar, "dve": nc.vector, "pool": nc.gpsimd, "sp": nc.sync}
        self.esem = {}
        self.cnt = {}
        self.seen = {}
        self.clock = {}
        for e in ("pe", "act", "dve", "pool"):
            self.esem[e] = es.enter_context(nc.semaphore("es_" + e))
            self.cnt[e] = 0
            self.clock[e] = [None]
        for e in self.E:
            self.seen[e] = {}
        self.dsem = {}
        self.dcnt = {}
        self.nwait = 0
        self.ninst = 0
        self.sem_es = es
        self.sem_ptr = 0
        self.limit = None

    def sb(self, name, shape, dt=F32):
        self.uid = getattr(self, "uid", 0) + 1
        h = self.es.enter_context(self.nc.sbuf_tensor("%s_u%d" % (name, self.uid), list(shape), dt))
        return T(h, name, False)

    def ps(self, name, shape, dt=F32):
        h = self.es.enter_context(self.nc.psum_tensor(name, list(shape), dt))
        t = T(h, name, False)
        t.trk.excl = True
        return t

    def dram(self, name, shape, dt=F32, kind="Internal"):
        h = self.nc.dram_tensor(name, list(shape), dt, kind=kind)
        return T(h, name)

    def get_dsem(self, name):
        if name not in self.dsem:
            self.dsem[name] = self.sem_es.enter_context(self.nc.semaphore("ds_" + name))
            self.dcnt[name] = 0
        return self.dsem[name]

    def _wait_event(self, eng, ev):
        if ev is None:
            return
        seen = self.seen[eng]
        if ev[0] == "e":
            _, e2, k = ev
            if e2 == eng and (eng == "pe" or not SAME_ENGINE_SYNC):
                return
            if seen.get(e2, 0) >= k:
                return
            self.E[eng].wait_ge(self.esem[e2], k)
            self.nwait += 1
            seen[e2] = k
            clk = self.clock[e2][k]
            if clk:
                for kk, vv in clk.items():
                    if seen.get(kk, 0) < vv:
                        seen[kk] = vv
        else:
            _, sname, val = ev
            key = "d:" + sname
            if seen.get(key, 0) >= val:
                return
            self.E[eng].wait_ge(self.dsem[sname], val)
            self.nwait += 1
            seen[key] = val

    def _deps(self, eng, outs, ins):
        for v in ins:
            for t in v.trks:
                self._wait_event(eng, t.w)
                if t.excl:
                    for e2, k in t.r_eng.items():
                        if e2 != eng:
                            self._wait_event(eng, ("e", e2, k))
        for v in outs:
            for t in v.trks:
                self._wait_event(eng, t.w)
                for e2, k in t.r_eng.items():
                    self._wait_event(eng, ("e", e2, k))
                for s, val in t.r_dma.items():
                    self._wait_event(eng, ("d", s, val))

    def I(self, eng, meth, **kw):
        outs, ins = [], []
        args = {}
        for k, v in kw.items():
            if isinstance(v, V):
                (outs if k in ("out", "accum_out") else ins).append(v)
                args[k] = v.ap
            else:
                args[k] = v
        if self.limit is not None and self.ninst >= self.limit:
            return None
        self._deps(eng, outs, ins)
        inst = getattr(self.E[eng], meth)(**args)
        self.cnt[eng] += 1
        k = self.cnt[eng]
        inst.then_inc(self.esem[eng], 1)
        self.ninst += 1
        snap = dict(self.seen[eng])
        self.clock[eng].append(snap)
        ev = ("e", eng, k)
        for v in outs:
            for t in v.trks:
                t.w = ev
                t.r_eng = {}
                t.r_dma = {}
        for v in ins:
            for t in v.trks:
                if t.r_eng.get(eng, 0) < k:
                    t.r_eng[eng] = k
        return inst

    def dma(self, q, out, in_, sem=None, **kw):
        if sem is None:
            st = out.t if (out.t is not None and not out.t.is_dram) else in_.t
            if st.dsem is None:
                st.dsem = "D%d" % self.sem_ptr
                self.sem_ptr += 1
            sem = st.dsem
        if self.limit is not None and self.ninst >= self.limit:
            return None
        s = self.get_dsem(sem)
        self._deps(q, [out], [in_])
        inst = self.E[q].dma_start(out=out.ap, in_=in_.ap, **kw)
        self.dcnt[sem] += 16
        val = self.dcnt[sem]
        inst.then_inc(s, 16)
        self.ninst += 1
        ev = ("d", sem, val)
        for t in out.trks:
            t.w = ev
            t.r_eng = {}
            t.r_dma = {}
        for t in in_.trks:
            if t.r_dma.get(sem, 0) < val:
                t.r_dma[sem] = val
        return inst

    def barrier(self):
        for eng in self.E:
            for e2 in self.esem:
                if self.cnt[e2] > 0:
                    self._wait_event(eng, ("e", e2, self.cnt[e2]))
            for s, val in self.dcnt.items():
                if val > 0:
                    self._wait_event(eng, ("d", s, val))

    def finish(self, eng="sp"):
        for s, val in self.dcnt.items():
            if val > 0:
                self._wait_event(eng, ("d", s, val))

    def mm(self, out, lhsT, rhs, start=True, stop=True, **kw):
        return self.I("pe", "matmul", out=out, lhsT=lhsT, rhs=rhs, start=start, stop=stop, **kw)

    def tr(self, out, in_, identity):
        return self.I("pe", "transpose", out=out, in_=in_, identity=identity)

    def act(self, out, in_, func, eng="act", **kw):
        return self.I("act", "activation", out=out, in_=in_, func=func, **kw)

    def ts(self, eng, out, in0, s1, s2=None, op0=ALU.mult, op1=None, **kw):
        if op1 is None:
            return self.I(eng, "tensor_scalar", out=out, in0=in0, scalar1=s1, scalar2=s2, op0=op0, **kw)
        return self.I(eng, "tensor_scalar", out=out, in0=in0, scalar1=s1, scalar2=s2, op0=op0, op1=op1, **kw)

    def tt(self, eng, out, in0, in1, op):
        return self.I(eng, "tensor_tensor", out=out, in0=in0, in1=in1, op=op)

    def stt(self, out, in0, scalar, in1, op0, op1, eng="dve"):
        return self.I(eng, "scalar_tensor_tensor", out=out, in0=in0, scalar=scalar, in1=in1, op0=op0, op1=op1)

    def cp(self, eng, out, in_):
        if eng == "act":
            return self.I("act", "copy", out=out, in_=in_)
        return self.I(eng, "tensor_copy", out=out, in_=in_)

    def memset(self, eng, out, val):
        return self.I(eng, "memset", ap=None, out=out, constant=val) if False else self._memset(eng, out, val)

    def _memset(self, eng, out, val):
        if self.limit is not None and self.ninst >= self.limit:
            return None
        self._deps(eng, [out], [])
        inst = self.E[eng].memset(out.ap, val)
        self.cnt[eng] += 1
        k = self.cnt[eng]
        inst.then_inc(self.esem[eng], 1)
        self.ninst += 1
        self.clock[eng].append(dict(self.seen[eng]))
        for t in out.trks:
            t.w = ("e", eng, k)
            t.r_eng = {}
            t.r_dma = {}
        return inst

from contextlib import ExitStack, contextmanager

EPS = 1e-6
D = 1024
NFM = 72
NTM = 1296
DNQ, DNK, DNV, DNG, HGQ, HGFF, HGFB, HGG, LRX, LRG, MGB = 0, 4, 8, 12, 16, 20, 24, 28, 32, 36, 40
TM_HGI, TM_ATQ, TM_ATK, TM_ATV, TM_BETA, TM_ALPHA = 0, 512, 1024, 1152, 1280, 1288


def vus(v, axis):
    return V(v.ap.unsqueeze(axis), v.trks)


def vbc(v, shape):
    return V(v.ap.broadcast_to(list(shape)), v.trks)


class Cfg:
    def __init__(self, T=4096, C=256, NB=2, L=4, debug=False):
        self.T, self.C, self.NB, self.L = T, C, NB, L
        self.NT = T + C
        self.debug = debug
        self.groups = []
        t = 0
        while t < C:
            n = min(512, C - t)
            self.groups.append((t, n, 0))
            t += n
        while t < self.NT:
            n = min(512, self.NT - t)
            self.groups.append((t, n, 1))
            t += n
        self.ntile = self.NT // 128
        self.nch = self.NT // 64
        self.cch = C // 64

    def gi_of(self, t):
        for i, (t0, n, s) in enumerate(self.groups):
            if t0 <= t < t0 + n:
                return i
        raise ValueError

    def order(self, d):
        if d == 0:
            return list(range(self.nch))
        return list(range(self.cch - 1, -1, -1)) + list(range(self.nch - 1, self.cch - 1, -1))


class K:
    pass


@contextmanager
def scope(mk):
    old = mk.es
    ptr = mk.sem_ptr
    with ExitStack() as es:
        mk.es = es
        try:
            yield es
        finally:
            lim, mk.limit = mk.limit, None
            mk.barrier()
            mk.limit = lim
            mk.es = old
            mk.sem_ptr = ptr


def build(cfg):
    nc = bass.Bass("TRN2", target_bir_lowering=False)
    L, NB, NT, C = cfg.L, cfg.NB, cfg.NT, cfg.C
    root = ExitStack()
    mk = MK(nc, root)
    k = K()
    k.cfg, k.mk, k.nc = cfg, mk, nc
    T_ = T

    def din(name, shape):
        return T_(nc.dram_tensor(name, list(shape), F32, kind="ExternalInput"), name)

    k.x_in = din("x_in", [NB, NT, D])
    k.cT = din("cT", [128, 8, NB + 1])
    k.mod_w = din("mod_w", [L, 128, 8, 6 * D])
    k.mod_b = din("mod_b", [L, 128, 48])
    k.n1w = din("n1w", [L, 128, 8])
    k.n2w = din("n2w", [L, 128, 8])
    k.w_fm = din("w_fm", [L, NFM, 128, 8, 128])
    k.w_tm = din("w_tm", [L, 128, 8, NTM])
    k.dn_cw = din("dn_cw", [L, 128, 12, 4])
    k.dn_ab = din("dn_ab", [L, 64, 2, 8])
    k.dn_nw = din("dn_nw", [L, 128, 1])
    k.hg_nw = din("hg_nw", [L, 128, 1])
    k.hg_lb = din("hg_lb", [128, 2, L, 4])
    k.lru_cw = din("lru_cw", [L, 128, 4, 4])
    k.lru_cb = din("lru_cb", [L, 128, 4])
    k.lru_wa = din("lru_wa", [L, 128, 2, 4, 128])
    k.lru_wx = din("lru_wx", [L, 128, 2, 4, 128])
    k.lru_v = din("lru_v", [L, 128, 3, 2, 4])
    k.att_nw = din("att_nw", [L, 128, 2, 64])
    k.w_br = din("w_br", [L, 128, 4, 4, D])
    k.w_out = din("w_out", [L, 128, 8, D])
    k.w1 = din("w1", [L, 128, 8, 4 * D])
    k.w2 = din("w2", [L, 128, 32, D])
    k.c_ident = din("c_ident", [128, 128])
    k.c_masks = din("c_masks", [64, 4, 64])
    k.c_rope = din("c_rope", [cfg.T, 2, 2, 16])
    k.out = T_(nc.dram_tensor("out", [NB, cfg.T, D], F32, kind="ExternalOutput"), "out")

    kind = "ExternalOutput" if cfg.debug else "Internal"

    def dsc(name, shape):
        return T_(nc.dram_tensor(name, list(shape), F32, kind=kind), name)

    k.hT = [dsc(f"hT{b}", [D, NT]) for b in range(NB)]
    k.uT = [dsc(f"uT{b}", [NFM * 128, NT]) for b in range(NB)]
    k.utok = [dsc(f"utok{b}", [NT, NTM]) for b in range(NB)]
    k.osc = [[dsc(f"osc{b}_{i}", [NT, 512]) for i in range(4)] for b in range(NB)]
    k.yT = [[dsc(f"yT{b}_{i}", [512, NT]) for i in range(4)] for b in range(NB)]

    k.PS = [mk.ps(f"psb{i}", [128, 512]) for i in range(8)]
    k.ident = mk.sb("ident", [128, 128])
    k.identb = mk.sb("identb", [128, 128], BF16)
    k.ones = mk.sb("ones", [128, 128])
    k.masks = mk.sb("masks", [64, 4, 64])
    k.scT = mk.sb("scT", [128, 8, NB + 1])
    k.modT = mk.sb("modT", [128, 48, NB + 1])
    k.A1 = mk.sb("A1", [128, 8, NB + 1])
    k.A2 = mk.sb("A2", [128, 8, NB + 1])
    k.lb = mk.sb("lb", [128, 2, L, 4])
    mk.dma("sp", k.ident[:], k.c_ident[:, :])
    mk.dma("sp", k.masks[:], k.c_masks[:, :, :])
    mk.dma("sp", k.scT[:], k.cT[:, :, :])
    mk.dma("sp", k.lb[:], k.hg_lb[:, :, :, :])
    mk._memset("dve", k.ones[:], 1.0)
    mk.cp("dve", k.identb[:], k.ident[:])
    mk.act(k.scT[:], k.scT[:], AF.Silu)
    prep_lb(k)

    mk.limit = getattr(cfg, "limit", None)
    try:
        _stages(k, cfg, mk, L, NB)
    except StopBuild:
        pass
    mk.limit = None
    mk.barrier()
    root.close()
    return nc, mk


def _stages(k, cfg, mk, L, NB):
    stage_in(k)
    for l in range(L):
        need_ctx = l < L - 1
        stage_mods(k, l)
        for b in range(NB):
            stage_s1(k, l, b)
        if cfg.debug == "s1":
            break
        only = getattr(cfg, "only", None)
        for b in range(NB):
            if only is None or "lru" in only:
                stage_lru(k, l, b)
            if only is None or "att" in only:
                stage_att(k, l, b, need_ctx)
            if only is None or "dn" in only:
                stage_dn(k, l, b)
                if not getattr(cfg, "stop", ""):
                    stage_post(k, l, b, 0, k.dn_nw, DNG)
            if only is None or "hg" in only:
                stage_hg(k, l, b)
                stage_post(k, l, b, 1, k.hg_nw, HGG)
        if cfg.debug == "mix":
            break
        stage_s3a(k, l, need_ctx)
        stage_s3b(k, l, need_ctx)
    stage_out(k)


def prep_lb(k):
    mk, L = k.mk, k.cfg.L
    with scope(mk):
        e = mk.sb("lb_e", [128, 2, L, 4])
        s = mk.sb("lb_s", [128, 2, 4])
        mk.act(e[:], k.lb[:], AF.Exp)
        mk.cp("dve", s[:], e[:, :, 0, :])
        for l in range(1, L):
            mk.tt("dve", s[:], s[:], e[:, :, l, :], ALU.add)
        mk.I("dve", "reciprocal", out=s[:], in_=s[:])
        mk.tt("dve", e[:], e[:], vbc(vus(s[:], 2), [128, 2, L, 4]), ALU.mult)
        mk._memset("dve", k.lb[:, :, 0, :], 0.0)
        for l in range(1, L):
            mk.tt("dve", k.lb[:, :, l, :], k.lb[:, :, l - 1, :], e[:, :, l, :], ALU.add)


def hT_view(k, b):
    return V(k.hT[b].ap().rearrange("(kt p) t -> p kt t", p=128), [k.hT[b].trk])


def stage_in(k):
    mk, cfg = k.mk, k.cfg
    with scope(mk):
        xt = [mk.sb(f"in_x{i}", [128, D]) for i in range(2)]
        st = [mk.sb(f"in_s{i}", [128, 8, 128]) for i in range(2)]
        i = 0
        for b in range(cfg.NB):
            hv = hT_view(k, b)
            for ti in range(cfg.ntile):
                x, s = xt[i % 2], st[i % 2]
                mk.dma("sp", x[:], k.x_in[b, ti * 128:(ti + 1) * 128, :])
                for half in range(2):
                    ps = k.PS[half]
                    for q in range(4):
                        mk.tr(ps[:, q * 128:(q + 1) * 128], x[:, (half * 4 + q) * 128:(half * 4 + q + 1) * 128], k.ident[:])
                    mk.cp("act" if half else "dve", s[:, half * 4:half * 4 + 4, :], ps[:].rr("p (a b) -> p a b", a=4))
                mk.dma("sp", hv[:, :, ti * 128:(ti + 1) * 128], s[:])
                i += 1


def stage_out(k):
    mk, cfg = k.mk, k.cfg
    with scope(mk):
        ht = [mk.sb(f"out_h{i}", [128, 8, 128]) for i in range(2)]
        st = [mk.sb(f"out_s{i}", [128, D]) for i in range(2)]
        i = 0
        for b in range(cfg.NB):
            hv = hT_view(k, b)
            for ti in range(cfg.T // 128):
                h, s = ht[i % 2], st[i % 2]
                t0 = cfg.C + ti * 128
                mk.dma("sp", h[:], hv[:, :, t0:t0 + 128])
                for half in range(2):
                    ps = k.PS[half]
                    for q in range(4):
                        mk.tr(ps[:, q * 128:(q + 1) * 128], h[:, half * 4 + q, :], k.ident[:])
                    mk.cp("act" if half else "dve", s[:, half * 512:(half + 1) * 512], ps[:])
                mk.dma("sp", k.out[b, ti * 128:(ti + 1) * 128, :], s[:])
                i += 1


def stage_mods(k, l):
    mk, cfg = k.mk, k.cfg
    NJ = cfg.NB + 1
    with scope(mk):
        wp = [mk.sb(f"mod_w{i}", [128, 8, 512]) for i in range(2)]
        mb = mk.sb("mod_bt", [128, 48])
        nw = mk.sb("mod_nw", [128, 2, 8])
        mk.dma("sp", mb[:], k.mod_b[l])
        mk.dma("sp", nw[:, 0, :], k.n1w[l])
        mk.dma("sp", nw[:, 1, :], k.n2w[l])
        pm = k.PS[0]
        for pc in range(12):
            w = wp[pc % 2]
            mk.dma("sp", w[:], k.mod_w[l, :, :, pc * 512:(pc + 1) * 512])
            for j in range(4):
                o = (pc * 4 + j) * 4
                for kt in range(8):
                    mk.mm(pm[:, o:o + NJ], w[:, kt, j * 128:(j + 1) * 128], k.scT[:, kt, :], start=(kt == 0), stop=(kt == 7))
        mk.tt("dve", k.modT[:], pm[:, 0:192].rr("p (a b) -> p a b", b=4)[:, :, 0:NJ], vbc(vus(mb[:], 2), [128, 48, NJ]), ALU.add)
        for (A, w_i, sc0) in ((k.A1, 0, 8), (k.A2, 1, 32)):
            mk.ts("dve", A[:], k.modT[:, sc0:sc0 + 8, :], 1.0, None, op0=ALU.add)
            mk.tt("dve", A[:], A[:], vbc(vus(nw[:, w_i, :], 2), [128, 8, NJ]), ALU.mult)


def rms_mod(k, hg, n, j, A, sh0, zt_out, tmp, sq, rs, ps):
    mk = k.mk
    mk.act(sq[:, :, :n], hg[:, :, :n], AF.Square)
    for kt in range(8):
        mk.mm(ps[:, :n], k.ones[:], sq[:, kt, :n], start=(kt == 0), stop=(kt == 7))
    mk.act(rs[:, :n], ps[:, :n], AF.Sqrt, scale=1.0 / D, bias=EPS)
    mk.I("dve", "reciprocal", out=rs[:, :n], in_=rs[:, :n])
    mk.tt("dve", tmp[:, :, :n], hg[:, :, :n], vbc(vus(rs[:, :n], 1), [128, 8, n]), ALU.mult)
    for kt in range(8):
        if kt % 2 == 0:
            mk.ts("dve", zt_out(kt), tmp[:, kt, :n], A[:, kt, j:j + 1], k.modT[:, sh0 + kt, j:j + 1], op0=ALU.mult, op1=ALU.add)
        else:
            mk.act(zt_out(kt), tmp[:, kt, :n], AF.Identity, scale=A[:, kt, j:j + 1], bias=k.modT[:, sh0 + kt, j:j + 1])


def stage_s1(k, l, b):
    mk, cfg = k.mk, k.cfg
    NT = cfg.NT
    hv = hT_view(k, b)
    with scope(mk):
        ZT = mk.sb("s1_zt", [128, 8, NT], BF16)
        with scope(mk):
            hg = [mk.sb(f"s1_h{i}", [128, 8, 512]) for i in range(2)]
            sq = mk.sb("s1_sq", [128, 8, 512])
            tmp = mk.sb("s1_tmp", [128, 8, 512])
            rs = mk.sb("s1_rs", [128, 512])
            for gi, (t0, n, seg) in enumerate(cfg.groups):
                h = hg[gi % 2]
                mk.dma("sp", h[:, :, :n], hv[:, :, t0:t0 + n])
                j = cfg.NB if seg == 0 else b
                rms_mod(k, h, n, j, k.A1, 0, lambda kt: ZT.p(gi, (slice(None), kt, slice(t0, t0 + n))), tmp, sq, rs, k.PS[gi % 2])
        with scope(mk):
            wf = [mk.sb(f"s1_wf{i}", [128, 8, 128]) for i in range(2)]
            wb = [mk.sb(f"s1_wb{i}", [128, 8, 128], BF16) for i in range(2)]
            stg = [mk.sb(f"s1_st{i}", [128, NT]) for i in range(2)]
            ev = 0
            for blk in range(NFM):
                s = blk % 2
                mk.dma("sp", wf[s][:], k.w_fm[l, blk])
                mk.cp("pool", wb[s][:], wf[s][:])
                for gi, (t0, n, seg) in enumerate(cfg.groups):
                    ps = k.PS[ev % 4]
                    for kt in range(8):
                        mk.mm(ps[:, :n], wb[s][:, kt, :], ZT.p(gi, (slice(None), kt, slice(t0, t0 + n))), start=(kt == 0), stop=(kt == 7))
                    mk.cp("act" if ev % 2 else "dve", stg[s].p(gi, (slice(None), slice(t0, t0 + n))), ps[:, :n])
                    ev += 1
                allp = V(stg[s].ap(), [stg[s].parts[g] for g in range(len(cfg.groups))], stg[s])
                mk.dma("sp", k.uT[b].p(blk, (slice(blk * 128, (blk + 1) * 128), slice(None))), allp)
        with scope(mk):
            wtm = mk.sb("s1_wtm", [128, 8, NTM], BF16)
            wst = [mk.sb(f"s1_wst{i}", [128, 8, 432]) for i in range(2)]
            stt_ = [mk.sb(f"s1_stt{i}", [128, NTM]) for i in range(2)]
            for pc in range(3):
                mk.dma("sp", wst[pc % 2][:], k.w_tm[l, :, :, pc * 432:(pc + 1) * 432])
                mk.cp("pool", wtm.p(pc, (slice(None), slice(None), slice(pc * 432, (pc + 1) * 432))), wst[pc % 2][:])
            wall = lambda kt, c0, cn: V(wtm.h[:, kt, c0:c0 + cn], [wtm.parts[p] for p in range(3)])
            ev = 0
            for ti in range(cfg.ntile):
                gi = cfg.gi_of(ti * 128)
                s = ti % 2
                for (c0, cn) in ((0, 512), (512, 512), (1024, NTM - 1024)):
                    ps = k.PS[ev % 4]
                    for kt in range(8):
                        mk.mm(ps[:, :cn], ZT.p(gi, (slice(None), kt, slice(ti * 128, (ti + 1) * 128))), wall(kt, c0, cn), start=(kt == 0), stop=(kt == 7))
                    mk.cp("act" if ev % 2 else "dve", stt_[s][:, c0:c0 + cn], ps[:, :cn])
                    ev += 1
                mk.dma("sp", k.utok[b].p(ti, (slice(ti * 128, (ti + 1) * 128), slice(None))), stt_[s][:])


def uT_rows(k, b, blk):
    return k.uT[b].p(blk, (slice(blk * 128, (blk + 1) * 128), slice(None)))


def conv4(mk, out, x, w, segs):
    for (s0, sn) in segs:
        e = s0 + sn
        mk.ts("dve", out[:, s0:e], x[:, s0:e], w[:, 1:2], None, op0=ALU.mult)
        mk.stt(out[:, s0 + 1:e], x[:, s0:e - 1], w[:, 0:1], out[:, s0 + 1:e], ALU.mult, ALU.add)
        mk.stt(out[:, s0:e - 1], x[:, s0 + 1:e], w[:, 2:3], out[:, s0:e - 1], ALU.mult, ALU.add)
        mk.stt(out[:, s0:e - 2], x[:, s0 + 2:e], w[:, 3:4], out[:, s0:e - 2], ALU.mult, ALU.add)


def softplus(mk, out, x, t1):
    mk.act(t1, x, AF.Abs)
    mk.act(t1, t1, AF.Exp, scale=-1.0)
    mk.act(t1, t1, AF.Ln, bias=1.0)
    mk.stt(out, x, 0.0, t1, ALU.max, ALU.add)


def stage_lru(k, l, b):
    mk, cfg = k.mk, k.cfg
    NT, C = cfg.NT, cfg.C
    segs = [(0, C), (C, cfg.T)]
    with scope(mk):
        cw = mk.sb("lr_cw", [128, 4, 4]); cb = mk.sb("lr_cb", [128, 4])
        wa = mk.sb("lr_wa", [128, 2, 4, 128]); wx = mk.sb("lr_wx", [128, 2, 4, 128])
        vv = mk.sb("lr_v", [128, 3, 2, 4]); csp = mk.sb("lr_csp", [128, 2, 4]); t1 = mk.sb("lr_t1", [128, 2, 4])
        mk.dma("sp", cw[:], k.lru_cw[l]); mk.dma("sp", cb[:], k.lru_cb[l])
        mk.dma("sp", wa[:], k.lru_wa[l]); mk.dma("sp", wx[:], k.lru_wx[l])
        mk.dma("sp", vv[:], k.lru_v[l])
        mk.ts("dve", csp[:], vv[:, 2], -1.0, None, op0=ALU.mult)
        softplus(mk, csp[:], csp[:], t1[:])
        mk.ts("dve", csp[:], csp[:], -8.0, None, op0=ALU.mult)
        xb = mk.sb("lr_xb", [128, NT]); gb = mk.sb("lr_gb", [128, NT]); xc = mk.sb("lr_xc", [128, NT])
        R = mk.sb("lr_r", [128, NT]); IG = mk.sb("lr_ig", [128, NT]); A = mk.sb("lr_a", [128, NT])
        H = [mk.sb(f"lr_h{d}", [128, NT]) for d in range(2)]
        ev = 0
        for h in range(4):
            mk.dma("sp", xb[:], uT_rows(k, b, LRX + h))
            mk.dma("sp", gb[:], uT_rows(k, b, LRG + h))
            conv4(mk, xc, xb, cw[:, h, :], segs)
            mk.ts("dve", xc[:], xc[:], cb[:, h:h + 1], None, op0=ALU.add)
            for d in range(2):
                for (t0, n, seg) in cfg.groups:
                    pa, px = k.PS[ev % 2], k.PS[2 + ev % 2]
                    ev += 1
                    mk.mm(pa[:, :n], wa[:, d, h, :], xc[:, t0:t0 + n])
                    mk.mm(px[:, :n], wx[:, d, h, :], xc[:, t0:t0 + n])
                    mk.act(R[:, t0:t0 + n], pa[:, :n], AF.Sigmoid, bias=vv[:, 0, d, h:h + 1])
                    mk.act(IG[:, t0:t0 + n], px[:, :n], AF.Sigmoid, bias=vv[:, 1, d, h:h + 1])
                mk.act(A[:], R[:], AF.Exp, scale=csp[:, d, h:h + 1])
                mk.tt("dve", R[:], A[:], A[:], ALU.mult)
                mk.ts("dve", R[:], R[:], -1.0, 1.0, op0=ALU.mult, op1=ALU.add)
                mk.ts("dve", R[:], R[:], 0.0, None, op0=ALU.max)
                mk.act(R[:], R[:], AF.Sqrt)
                mk.tt("dve", R[:], R[:], IG[:], ALU.mult)
                mk.tt("dve", R[:], R[:], xc[:], ALU.mult)
                if d == 0:
                    mk.I("dve", "tensor_tensor_scan", out=H[0][:], data0=A[:], data1=R[:], initial=0.0, op0=ALU.mult, op1=ALU.add)
                else:
                    mk.I("dve", "tensor_tensor_scan", out=H[1][:, C - 1::-1] if False else V(H[1].h[:, 0:C][:, ::-1], [H[1].trk]),
                         data0=V(A.h[:, 0:C][:, ::-1], [A.trk]), data1=V(R.h[:, 0:C][:, ::-1], [R.trk]),
                         initial=0.0, op0=ALU.mult, op1=ALU.add)
                    mk.I("dve", "tensor_tensor_scan", out=V(H[1].h[:, C:NT][:, ::-1], [H[1].trk]),
                         data0=V(A.h[:, C:NT][:, ::-1], [A.trk]), data1=V(R.h[:, C:NT][:, ::-1], [R.trk]),
                         initial=H[1][:, 0:1], op0=ALU.mult, op1=ALU.add)
            mk.tt("dve", H[0][:], H[0][:], H[1][:], ALU.add)
            mk.act(gb[:], gb[:], AF.Gelu)
            mk.tt("dve", H[0][:], H[0][:], gb[:], ALU.mult)
            mk.dma("sp", k.yT[b][2].p(h, (slice(h * 128, (h + 1) * 128), slice(None))), H[0][:])


def stage_att(k, l, b, need_ctx):
    mk, cfg = k.mk, k.cfg
    NT, C, ntile = cfg.NT, cfg.C, cfg.ntile
    with scope(mk):
        QT = mk.sb("at_qt", [64, 8, NT], BF16)
        KT = mk.sb("at_kt", [64, 2, NT], BF16)
        VA = mk.sb("at_va", [128, ntile, 2, 65], BF16)
        nw = mk.sb("at_nw", [128, 2, 64])
        mk.dma("sp", nw[:], k.att_nw[l])
        mk._memset("pool", VA[:], 1.0)
        with scope(mk):
            a_ = [mk.sb(f"at_a{i}", [128, 768]) for i in range(2)]
            cs_ = [mk.sb(f"at_cs{i}", [128, 2, 2, 16]) for i in range(2)]
            sq = mk.sb("at_sq", [128, 640]); ss = mk.sb("at_ss", [128, 10])
            qn = mk.sb("at_qn", [128, 640]); qr = mk.sb("at_qr", [128, 640])
            tt_ = [mk.sb(f"at_t{i}", [128, 10, 2, 16]) for i in range(4)]
            for ti in range(ntile):
                a = a_[ti % 2]
                mk.dma("sp", a[:], k.utok[b].p(ti, (slice(ti * 128, (ti + 1) * 128), slice(TM_ATQ, TM_ATQ + 768))))
                mk.tt("dve", sq[:], a[:, 0:640], a[:, 0:640], ALU.mult)
                mk.I("dve", "tensor_reduce", out=ss[:], in_=sq[:].rr("p (h d) -> p h d", d=64), axis=AX.X, op=ALU.add)
                mk.act(ss[:], ss[:], AF.Sqrt, scale=1.0 / 64, bias=EPS)
                mk.I("dve", "reciprocal", out=ss[:], in_=ss[:])
                mk.tt("dve", qn[:].rr("p (h d) -> p h d", d=64), a[:, 0:640].rr("p (h d) -> p h d", d=64), vbc(vus(ss[:], 2), [128, 10, 64]), ALU.mult)
                mk.tt("pool", qn[:, 0:512].rr("p (h d) -> p h d", d=64), qn[:, 0:512].rr("p (h d) -> p h d", d=64), vbc(vus(nw[:, 0, :], 1), [128, 8, 64]), ALU.mult)
                mk.tt("pool", qn[:, 512:640].rr("p (h d) -> p h d", d=64), qn[:, 512:640].rr("p (h d) -> p h d", d=64), vbc(vus(nw[:, 1, :], 1), [128, 2, 64]), ALU.mult)
                if ti * 128 >= C:
                    cs = cs_[ti % 2]
                    tl = ti * 128 - C
                    mk.dma("sp", cs[:], k.c_rope[tl:tl + 128])
                    xv = qn[:].rr("p (h a f r) -> p h a f r", h=10, a=2, f=2)
                    ov = qr[:].rr("p (h a f r) -> p h a f r", h=10, a=2, f=2)
                    x1, x2 = xv[:, :, :, 0, :], xv[:, :, :, 1, :]
                    cc = vbc(vus(cs[:, 0], 1), [128, 10, 2, 16]); sn = vbc(vus(cs[:, 1], 1), [128, 10, 2, 16])
                    mk.tt("dve", tt_[0][:], x1, cc, ALU.mult)
                    mk.tt("pool", tt_[1][:], x2, sn, ALU.mult)
                    mk.tt("dve", tt_[2][:], x2, cc, ALU.mult)
                    mk.tt("pool", tt_[3][:], x1, sn, ALU.mult)
                    mk.tt("dve", ov[:, :, :, 0, :], tt_[0][:], tt_[1][:], ALU.subtract)
                    mk.tt("dve", ov[:, :, :, 1, :], tt_[2][:], tt_[3][:], ALU.add)
                    src = qr
                else:
                    src = qn
                for hh in range(10):
                    ps = k.PS[hh // 4]
                    o = (hh % 4) * 128
                    mk.tr(ps[0:64, o:o + 128], src[:, hh * 64:(hh + 1) * 64], k.ident[:])
                mk.cp("act", QT[:, 0:4, ti * 128:(ti + 1) * 128], k.PS[0][0:64, :].rr("p (a b) -> p a b", a=4))
                mk.cp("dve", QT[:, 4:8, ti * 128:(ti + 1) * 128], k.PS[1][0:64, :].rr("p (a b) -> p a b", a=4))
                mk.cp("act", KT[:, 0:2, ti * 128:(ti + 1) * 128], k.PS[2][0:64, 0:256].rr("p (a b) -> p a b", a=2))
                mk.cp("pool", VA[:, ti, :, 0:64], a[:, 640:768].rr("p (g d) -> p g d", g=2))
        with scope(mk):
            OT = [mk.sb(f"at_ot{i}", [128, 512]) for i in range(2)]
            PT = [mk.sb(f"at_pt{i}", [128, 512], BF16) for i in range(2)]
            YS = [mk.sb(f"at_ys{i}", [128, 4, 128]) for i in range(2)]
            rc = mk.sb("at_rc", [128, 8])
            yv = V(k.yT[b][3].ap().rearrange("(ft p) t -> p ft t", p=128), [k.yT[b][3].trk])
            it = 0
            qtiles = list(range(ntile)) if need_ctx else list(range(C // 128, ntile))
            for qn_i, qi in enumerate(qtiles):
                keyt = list(range(C // 128)) if qi * 128 < C else list(range(ntile))
                ot = OT[qn_i % 2]
                for g in range(2):
                    for kt in keyt:
                        S = k.PS[it % 2]; pt = PT[it % 2]; it += 1
                        mk.mm(S[:, :], KT[:, g, kt * 128:(kt + 1) * 128], QT[:, 4 * g:4 * g + 4, qi * 128:(qi + 1) * 128])
                        mk.act(pt[:], S[:], AF.Exp, scale=0.125)
                        for hq in range(4):
                            mk.mm(k.PS[2 + hq][:, 0:65], pt[:, hq * 128:(hq + 1) * 128], VA[:, kt, g, :], start=(kt == keyt[0]), stop=(kt == keyt[-1]))
                    for hq in range(4):
                        hh = 4 * g + hq
                        mk.I("dve", "reciprocal", out=rc[:, hh:hh + 1], in_=k.PS[2 + hq][:, 64:65])
                        mk.ts("dve", ot[:, hh * 64:(hh + 1) * 64], k.PS[2 + hq][:, 0:64], rc[:, hh:hh + 1], None, op0=ALU.mult)
                for ft in range(4):
                    mk.tr(k.PS[6][:, ft * 128:(ft + 1) * 128], ot[:, ft * 128:(ft + 1) * 128], k.ident[:])
                ys = YS[qn_i % 2]
                mk.cp("act", ys[:], k.PS[6][:].rr("p (a b) -> p a b", a=4))
                mk.dma("sp", yv[:, :, qi * 128:(qi + 1) * 128], ys[:])


def stage_dn(k, l, b):
    mk, cfg = k.mk, k.cfg
    NT, C, nch = cfg.NT, cfg.C, cfg.nch
    segs = [(0, C), (C, cfg.T)]
    LE, GE, GT, LT = 0, 1, 2, 3
    with scope(mk):
        graw = mk.sb("dn_graw", [64, nch, 16]); ab = mk.sb("dn_ab", [64, 2, 8])
        beta = mk.sb("dn_beta", [64, nch, 8]); g = mk.sb("dn_g", [64, nch, 8]); t1 = mk.sb("dn_t1", [64, nch, 8])
        gam = mk.sb("dn_gam", [64, nch, 8]); erg = mk.sb("dn_erg", [64, nch, 8]); cdec = mk.sb("dn_cdec", [128, nch, 8])
        eg = mk.sb("dn_eg", [64, nch, 8]); bg = mk.sb("dn_bg", [64, nch, 8]); ea = mk.sb("dn_ea", [64, 8])
        cw = mk.sb("dn_cw", [128, 12, 4])
        mk.dma("sp", cw[:], k.dn_cw[l])
        mk.dma("sp", ab[:], k.dn_ab[l])
        gsrc = k.utok[b].h[:, TM_BETA:TM_BETA + 16].rearrange("(n c) f -> c n f", c=64)
        for n0 in range(0, nch, 16):
            n1 = min(nch, n0 + 16)
            mk.dma("sp", graw[:, n0:n1, :], V(gsrc[:, n0:n1, :], [k.utok[b].trk] + list(k.utok[b].parts.values())))
        mk.act(beta[:], graw[:, :, 0:8], AF.Sigmoid)
        mk.tt("dve", g[:], graw[:, :, 8:16], vbc(vus(ab[:, 1, :], 1), [64, nch, 8]), ALU.add)
        softplus(mk, g[:], g[:], t1[:])
        mk.act(ea[:], ab[:, 0, :], AF.Exp)
        mk.tt("dve", g[:], g[:], vbc(vus(ea[:], 1), [64, nch, 8]), ALU.mult)
        mk.ts("dve", g[:], g[:], -1.0, None, op0=ALU.mult)
        for d in range(2):
            cs = slice(d * 4, d * 4 + 4)
            n4 = nch * 4
            mk.mm(k.PS[0][0:64, 0:n4], k.masks[:, LE if d == 0 else GE, :], g[:, :, cs])
            mk.cp("dve", gam[:, :, cs], k.PS[0][0:64, 0:n4].rr("p (n f) -> p n f", f=4))
            mk.mm(k.PS[1][0:64, 0:n4], k.masks[:, GT if d == 0 else LT, :], g[:, :, cs])
            mk.act(erg[:, :, cs], k.PS[1][0:64, 0:n4].rr("p (n f) -> p n f", f=4), AF.Exp)
            mk.mm(k.PS[2][:, 0:n4], k.ones[0:64, :], g[:, :, cs])
            mk.act(cdec[:, :, cs], k.PS[2][:, 0:n4].rr("p (n f) -> p n f", f=4), AF.Exp)
        mk.act(eg[:], gam[:], AF.Exp)
        mk.tt("dve", bg[:], beta[:], eg[:], ALU.mult)
        raw = mk.sb("dn_raw", [128, NT])
        sq = mk.sb("dn_sq", [128, 512]); rs = mk.sb("dn_rs", [128, 512])
        stop = getattr(cfg, "stop", "")
        if stop == "dn_prep":
            return
        for hp in range(2):
            with scope(mk):
                Q = [mk.sb(f"dn_q{i}", [128, NT]) for i in range(2)]
                Kt = [mk.sb(f"dn_k{i}", [128, NT]) for i in range(2)]
                Vt = [mk.sb(f"dn_v{i}", [128, NT]) for i in range(2)]
                for hh in range(2):
                    h = 2 * hp + hh
                    for (dst, blk0, wi) in ((Q[hh], DNQ, 0), (Kt[hh], DNK, 1), (Vt[hh], DNV, 2)):
                        mk.dma("sp", raw[:], uT_rows(k, b, blk0 + h))
                        conv4(mk, dst, raw, cw[:, wi * 4 + h, :], segs)
                        mk.act(dst[:], dst[:], AF.Silu)
                    for (dst, scl) in ((Q[hh], 128.0 ** -0.5), (Kt[hh], 1.0)):
                        for gi, (t0, n, seg) in enumerate(cfg.groups):
                            ps = k.PS[gi % 2]
                            mk.act(sq[:, :n], dst[:, t0:t0 + n], AF.Square)
                            mk.mm(ps[:, :n], k.ones[:], sq[:, :n])
                            mk.act(rs[:, :n], ps[:, :n], AF.Sqrt, bias=EPS)
                            mk.I("dve", "reciprocal", out=rs[:, :n], in_=rs[:, :n])
                            mk.stt(dst[:, t0:t0 + n], dst[:, t0:t0 + n], scl, rs[:, :n], ALU.mult, ALU.mult)
                if stop == "dn_a":
                    continue
                S = mk.sb("dn_S", [128, 4, 128])
                mk._memset("dve", S[:], 0.0)
                GD = mk.sb("dn_gd", [64, 4, 64]); NM = mk.sb("dn_nm", [64, 4, 64]); Dm = mk.sb("dn_dm", [64, 4, 64])
                Ds = mk.sb("dn_ds", [64, 4, 64]); Di = mk.sb("dn_di", [64, 4, 64])
                AL = mk.sb("dn_al", [64, 4, 64]); AQ = mk.sb("dn_aq", [64, 4, 64]); AQT = mk.sb("dn_aqt", [64, 4, 64])
                BB = mk.sb("dn_bb", [64, 2, 4, 64]); X = mk.sb("dn_x", [64, 4, 64])
                vb = mk.sb("dn_vb", [64, 4, 128]); kbd = mk.sb("dn_kbd", [64, 4, 128]); kd = mk.sb("dn_kd", [64, 4, 128])
                U = mk.sb("dn_u", [64, 4, 128]); WT = mk.sb("dn_wt", [128, 4, 64]); VN = mk.sb("dn_vn", [64, 4, 128])
                O2 = mk.sb("dn_o2", [64, 4, 128]); Os = [mk.sb(f"dn_o{i}", [64, 4, 128]) for i in range(2)]
                P = k.PS
                id64 = k.ident[0:64, 0:64]
                orders = [cfg.order(0), cfg.order(1)]
                for s in range(nch):
                    info = []
                    for c in range(4):
                        d, hh = c // 2, c % 2
                        n = orders[d][s]
                        info.append((d, hh, n, n * 64, d * 4 + 2 * hp + hh))
                    for c, (d, hh, n, t0, col) in enumerate(info):
                        cc = slice(c * 64, (c + 1) * 64)
                        kT = Kt[hh][:, t0:t0 + 64]
                        mk.ts("pool", GD[:, c, :], k.masks[:, LE if d == 0 else GE, :], g[:, n, col:col + 1], None, op0=ALU.mult)
                        mk.mm(P[0][0:64, cc], k.ones[0:64, 0:64], GD[:, c, :])
                        mk.mm(P[0][0:64, 256 + c * 64:256 + (c + 1) * 64], kT, kT)
                        mk.mm(P[1][0:64, cc], Q[hh][:, t0:t0 + 64], kT)
                        mk.tr(P[3][0:64, c * 128:(c + 1) * 128], kT, k.ident[:])
                        mk.tr(P[4][0:64, c * 128:(c + 1) * 128], Vt[hh][:, t0:t0 + 64], k.ident[:])
                        mk.ts("dve", NM[:, c, :], P[0][0:64, cc], gam[:, n, col:col + 1], 0.0, op0=ALU.subtract, op1=ALU.max)
                    if stop == "dn_p1":
                        continue
                    mk.act(Dm[:], NM[:], AF.Exp, scale=-1.0)
                    for d in range(2):
                        dsl = slice(2 * d, 2 * d + 2)
                        mk.tt("dve", Ds[:, dsl, :], Dm[:, dsl, :], vbc(vus(k.masks[:, GT if d == 0 else LT, :], 1), [64, 2, 64]), ALU.mult)
                        mk.tt("dve", Di[:, dsl, :], Dm[:, dsl, :], vbc(vus(k.masks[:, GE if d == 0 else LE, :], 1), [64, 2, 64]), ALU.mult)
                    for c, (d, hh, n, t0, col) in enumerate(info):
                        mk.stt(AL[:, c, :], P[0][0:64, 256 + c * 64:256 + (c + 1) * 64], beta[:, n, col:col + 1], Ds[:, c, :], ALU.mult, ALU.mult)
                    mk.tt("dve", AQ[:], P[1][0:64, 0:256].rr("p (a b) -> p a b", a=4), Di[:], ALU.mult)
                    for c in range(4):
                        mk.mm(P[2][0:64, c * 64:(c + 1) * 64], AL[:, c, :], id64)
                        mk.mm(P[2][0:64, 256 + c * 64:256 + (c + 1) * 64], AQ[:, c, :], id64)
                    mk.ts("dve", BB[:, 0], P[2][0:64, 0:256].rr("p (a b) -> p a b", a=4), -1.0, None, op0=ALU.mult)
                    mk.ts("dve", BB[:, 1], AL[:], -1.0, None, op0=ALU.mult)
                    mk.cp("act", AQT[:], P[2][0:64, 256:512].rr("p (a b) -> p a b", a=4))
                    mk.tt("dve", X[:], BB[:, 0], vbc(vus(id64, 1), [64, 4, 64]), ALU.add)
                    if stop == "dn_p2":
                        continue
                    for lev in range(5):
                        for c in range(4):
                            mk.mm(P[5][0:64, c * 64:(c + 1) * 64], BB[:, 1, c, :], BB[:, 0, c, :])
                            mk.mm(P[5][0:64, 256 + c * 64:256 + (c + 1) * 64], BB[:, 0, c, :], BB[:, 1, c, :])
                        mk.cp("act" if lev % 2 else "dve", BB[:], P[5][0:64, :].rr("p (t a b) -> p t a b", t=2, a=4))
                        for c in range(4):
                            mk.mm(P[6][0:64, c * 64:(c + 1) * 64], BB[:, 1, c, :], X[:, c, :])
                        mk.tt("dve", X[:], X[:], P[6][0:64, 0:256].rr("p (a b) -> p a b", a=4), ALU.add)
                    for c, (d, hh, n, t0, col) in enumerate(info):
                        mk.act(vb[:, c, :], P[4][0:64, c * 128:(c + 1) * 128], AF.Identity, scale=beta[:, n, col:col + 1])
                        mk.ts("dve", kbd[:, c, :], P[3][0:64, c * 128:(c + 1) * 128], bg[:, n, col:col + 1], None, op0=ALU.mult)
                        mk.act(kd[:, c, :], P[3][0:64, c * 128:(c + 1) * 128], AF.Identity, scale=erg[:, n, col:col + 1])
                    for c in range(4):
                        mk.mm(P[6][0:64, c * 128:(c + 1) * 128], X[:, c, :], vb[:, c, :])
                        mk.mm(P[1][:, c * 64:(c + 1) * 64], kbd[:, c, :], X[:, c, :])
                    if stop == "dn_p3":
                        continue
                    mk.cp("act", U[:], P[6][0:64, :].rr("p (a b) -> p a b", a=4))
                    mk.cp("dve", WT[:], P[1][:, 0:256].rr("p (a b) -> p a b", a=4))
                    for c in range(4):
                        mk.mm(P[7][0:64, c * 128:(c + 1) * 128], WT[:, c, :], S[:, c, :])
                    mk.tt("dve", VN[:], U[:], P[7][0:64, :].rr("p (a b) -> p a b", a=4), ALU.subtract)
                    for c, (d, hh, n, t0, col) in enumerate(info):
                        mk.mm(P[3][0:64, c * 128:(c + 1) * 128], Q[hh][:, t0:t0 + 64], S[:, c, :])
                        mk.mm(P[4][0:64, c * 128:(c + 1) * 128], AQT[:, c, :], VN[:, c, :])
                        mk.mm(P[5][:, c * 128:(c + 1) * 128], kd[:, c, :], VN[:, c, :])
                    mk.cp("act", O2[:], P[4][0:64, :].rr("p (a b) -> p a b", a=4))
                    Ot = Os[s % 2]
                    for c, (d, hh, n, t0, col) in enumerate(info):
                        mk.stt(Ot[:, c, :], P[3][0:64, c * 128:(c + 1) * 128], eg[:, n, col:col + 1], O2[:, c, :], ALU.mult, ALU.add)
                        mk.stt(S[:, c, :], S[:, c, :], cdec[:, n, col:col + 1], P[5][:, c * 128:(c + 1) * 128], ALU.mult, ALU.add)
                    if stop == "dn_p4":
                        continue
                    for d in range(2):
                        t0 = orders[d][s] * 64
                        mk.dma("sp", k.osc[b][d][t0:t0 + 64, hp * 256:(hp + 1) * 256], Ot[:, 2 * d:2 * d + 2, :].rr("p a b -> p (a b)"))


def stage_hg(k, l, b):
    mk, cfg = k.mk, k.cfg
    NT, C, nch = cfg.NT, cfg.C, cfg.nch
    LE, GE = 0, 1
    P = k.PS
    with scope(mk):
        rmask = mk.sb("hg_rm", [128, 2, NT], BF16)
        mk._memset("pool", rmask[:], 1.0)
        mk._memset("pool", rmask[:, 0, :].rr("p (n c) -> p n c", c=64)[:, :, 0:1], 0.0)
        mk._memset("pool", rmask[:, 1, :].rr("p (n c) -> p n c", c=64)[:, :, 63:64], 0.0)
        q = mk.sb("hg_q", [128, NT]); fr = mk.sb("hg_fr", [128, NT]); Kf = mk.sb("hg_kf", [128, NT])
        gc = mk.sb("hg_gc", [128, NT]); tmp = fr; QG = mk.sb("hg_qg", [128, NT])
        QTL = mk.sb("hg_qtl", [128, NT], BF16); KT4 = mk.sb("hg_kt4", [128, nch, 4, 64], BF16); KD = mk.sb("hg_kd", [128, NT])
        REF = mk.sb("hg_ref", [128, nch, 4]); EL = mk.sb("hg_el", [128, nch]); oml = mk.sb("hg_oml", [128, 1])
        S = mk.sb("hg_S", [128, 128])
        vt = [mk.sb(f"hg_vt{i}", [64, 128]) for i in range(2)]; vtb = [mk.sb(f"hg_vtb{i}", [64, 128], BF16) for i in range(2)]
        SCT = mk.sb("hg_sct", [64, 64], BF16); kdt = mk.sb("hg_kdt", [64, 128], BF16)
        Os = [mk.sb(f"hg_o{i}", [64, 128]) for i in range(2)]
        v3 = lambda t: t[:].rr("p (n c) -> p n c", c=64)
        for h in range(4):
            mk.dma("sp", q[:], uT_rows(k, b, HGQ + h))
            mk.act(q[:], q[:], AF.Silu)
            for d in range(2):
                mk.dma("sp", fr[:], uT_rows(k, b, (HGFF if d == 0 else HGFB) + h))
                lbc = k.lb[:, d, l, h:h + 1]
                mk.ts("dve", oml[:], lbc, -1.0, 1.0, op0=ALU.mult, op1=ALU.add)
                mk.act(fr[:], fr[:], AF.Sigmoid)
                mk.ts("dve", fr[:], fr[:], oml[:, 0:1], lbc, op0=ALU.mult, op1=ALU.add)
                mk.ts("dve", Kf[:], fr[:], -1.0, 1.0, op0=ALU.mult, op1=ALU.add)
                mk.act(fr[:], fr[:], AF.Ln)
                if d == 0:
                    mk.I("dve", "tensor_tensor_scan", out=gc[:], data0=rmask[:, 0, :], data1=fr[:], initial=0.0, op0=ALU.mult, op1=ALU.add)
                else:
                    mk.I("dve", "tensor_tensor_scan", out=V(gc.h[:, ::-1], [gc.trk]), data0=V(rmask.h[:, 1, ::-1], [rmask.trk]),
                         data1=V(fr.h[:, ::-1], [fr.trk]), initial=0.0, op0=ALU.mult, op1=ALU.add)
                last = 63 if d == 0 else 0
                mk.act(tmp[:], gc[:], AF.Exp)
                mk.cp("dve", EL[:], v3(tmp)[:, :, last])
                mk.tt("dve", QG[:], q[:], tmp[:], ALU.mult)
                mk._memset("pool", REF[:], 0.0)
                if d == 0:
                    mk.cp("pool", REF[:, :, 1:4], v3(gc)[:, :, 15:63:16])
                else:
                    mk.cp("pool", REF[:, :, 0:3], v3(gc)[:, :, 16:64:16])
                g4 = gc[:].rr("p (n j r) -> p n j r", j=4, r=16)
                t4 = tmp[:].rr("p (n j r) -> p n j r", j=4, r=16)
                mk.tt("dve", t4, g4, vbc(vus(REF[:], 3), [128, nch, 4, 16]), ALU.subtract)
                mk.act(tmp[:], tmp[:], AF.Exp)
                mk.tt("dve", QTL[:], q[:], tmp[:], ALU.mult)
                for J in range(4):
                    mk.tt("dve", v3(tmp), vbc(REF[:, :, J:J + 1], [128, nch, 64]), v3(gc), ALU.subtract)
                    mk.ts("dve", tmp[:], tmp[:], 60.0, None, op0=ALU.min)
                    mk.act(tmp[:], tmp[:], AF.Exp)
                    mk.tt("dve", KT4[:, :, J, :], v3(tmp), v3(Kf), ALU.mult)
                mk.tt("dve", v3(tmp), vbc(v3(gc)[:, :, last:last + 1], [128, nch, 64]), v3(gc), ALU.subtract)
                mk.act(tmp[:], tmp[:], AF.Exp)
                mk.tt("dve", KD[:], tmp[:], Kf[:], ALU.mult)
                mk._memset("dve", S[:], 0.0)
                order = cfg.order(d)
                for s, n in enumerate(order):
                    t0 = n * 64
                    v_, vb_, O_ = vt[s % 2], vtb[s % 2], Os[s % 2]
                    mk.dma("sp", v_[:], V(k.utok[b].h[t0:t0 + 64, TM_HGI + h * 128:TM_HGI + (h + 1) * 128], [k.utok[b].parts[t0 // 128]]))
                    mk.cp("pool", vb_[:], v_[:])
                    for J in range(4):
                        mk.mm(P[0][0:64, 16 * J:16 * J + 16], KT4[:, n, J, :], QTL[:, t0 + 16 * J:t0 + 16 * J + 16])
                    mk.tt("dve", SCT[:], P[0][0:64, 0:64], k.masks[:, LE if d == 0 else GE, :], ALU.mult)
                    mk.tr(P[1][0:64, 0:128], KD[:, t0:t0 + 64], k.ident[:])
                    mk.cp("act", kdt[:], P[1][0:64, 0:128])
                    mk.mm(P[2][0:64, 0:128], QG[:, t0:t0 + 64], S[:], start=True, stop=False)
                    mk.mm(P[2][0:64, 0:128], SCT[:], vb_[:], start=False, stop=True)
                    mk.cp("act", O_[:], P[2][0:64, 0:128])
                    mk.dma("sp", k.osc[b][2 + d][t0:t0 + 64, h * 128:(h + 1) * 128], O_[:])
                    mk.mm(P[3][:, 0:128], kdt[:], vb_[:])
                    mk.stt(S[:], S[:], EL[:, n:n + 1], P[3][:, 0:128], ALU.mult, ALU.add)


def stage_post(k, l, b, which, nwT, gblk):
    mk, cfg = k.mk, k.cfg
    with scope(mk):
        nw = mk.sb("po_nw", [128, 1])
        mk.dma("sp", nw[:], nwT[l])
        of = [mk.sb(f"po_of{i}", [128, 512]) for i in range(2)]; ob = [mk.sb(f"po_ob{i}", [128, 512]) for i in range(2)]
        G = [mk.sb(f"po_g{i}", [128, 4, 128]) for i in range(2)]; Y = [mk.sb(f"po_y{i}", [128, 4, 128]) for i in range(2)]
        sq = mk.sb("po_sq", [128, 512]); ss = mk.sb("po_ss", [128, 4])
        uv = V(k.uT[b].ap().rearrange("(x p) t -> p x t", p=128), [k.uT[b].trk] + list(k.uT[b].parts.values()))
        yv = V(k.yT[b][which].ap().rearrange("(ft p) t -> p ft t", p=128), [k.yT[b][which].trk])
        for ti in range(cfg.ntile):
            s = ti % 2
            rows = slice(ti * 128, (ti + 1) * 128)
            mk.dma("sp", of[s][:], k.osc[b][2 * which][rows, :])
            mk.dma("sp", ob[s][:], k.osc[b][2 * which + 1][rows, :])
            mk.dma("sp", G[s][:], uv[:, gblk:gblk + 4, rows])
            o = of[s]
            mk.tt("dve", o[:], o[:], ob[s][:], ALU.add)
            mk.tt("pool", sq[:], o[:], o[:], ALU.mult)
            mk.I("dve", "tensor_reduce", out=ss[:], in_=sq[:].rr("p (h d) -> p h d", d=128), axis=AX.X, op=ALU.add)
            mk.act(ss[:], ss[:], AF.Sqrt, scale=1.0 / 128, bias=EPS)
            mk.I("dve", "reciprocal", out=ss[:], in_=ss[:])
            mk.tt("dve", o[:].rr("p (h d) -> p h d", d=128), o[:].rr("p (h d) -> p h d", d=128), vbc(vus(ss[:], 2), [128, 4, 128]), ALU.mult)
            ps = k.PS[s]
            for h in range(4):
                mk.tr(ps[:, h * 128:(h + 1) * 128], o[:, h * 128:(h + 1) * 128], k.ident[:])
            mk.act(G[s][:], G[s][:], AF.Silu)
            mk.stt(Y[s][:].rr("p a b -> p (a b)"), ps[:], nw[:, 0:1], G[s][:].rr("p a b -> p (a b)"), ALU.mult, ALU.mult)
            mk.dma("sp", yv[:, :, rows], Y[s][:])


def load_cast(mk, dst_fn, src_fn, npieces, stage_tiles, semL):
    for pc in range(npieces):
        st = stage_tiles[pc % 2]
        sv = src_fn(pc)
        mk.dma("sp", V(st.h[tuple(slice(0, x) for x in sv.ap.shape)], [st.trk], st), sv)
        mk.cp("pool" if pc % 2 else "dve", dst_fn(pc), V(st.h[tuple(slice(0, x) for x in sv.ap.shape)], [st.trk], st))


def stage_s3a(k, l, need_ctx):
    mk, cfg = k.mk, k.cfg
    NB = cfg.NB
    with scope(mk):
        wbr = mk.sb("a_wbr", [128, 4, 4, D], BF16)
        wo = mk.sb("a_wo", [128, 8, D], BF16)
        with scope(mk):
            stg = [mk.sb(f"a_stg{i}", [128, 4, D]) for i in range(2)]
            load_cast(mk, lambda pc: wbr[:, pc], lambda pc: k.w_br[l, :, pc], 4, stg, 1)
            load_cast(mk, lambda pc: wo[:, pc * 4:(pc + 1) * 4, :], lambda pc: k.w_out[l, :, pc * 4:(pc + 1) * 4, :], 2, stg, 1)
        yt = [mk.sb(f"a_yt{i}", [128, 4, 512]) for i in range(2)]
        ytb = mk.sb("a_ytb", [128, 4, 4, 512], BF16)
        G = [mk.sb(f"a_g{i}", [128, 4, 512]) for i in range(2)]
        merged = mk.sb("a_mg", [128, 8, 512], BF16)
        macc = mk.sb("a_macc", [128, 512]); mt = mk.sb("a_mt", [128, 512])
        hg = [mk.sb(f"a_h{i}", [128, 8, 512]) for i in range(2)]
        it = 0
        ev = 0
        for b in range(NB):
            hv = hT_view(k, b)
            uv = V(k.uT[b].ap().rearrange("(x p) t -> p x t", p=128), [k.uT[b].trk] + list(k.uT[b].parts.values()))
            for gi, (t0, n, seg) in enumerate(cfg.groups):
                if seg == 0 and not need_ctx:
                    continue
                j = NB if seg == 0 else b
                h_ = hg[it % 2]
                it += 1
                mk.dma("sp", h_[:, :, :n], hv[:, :, t0:t0 + n])
                for bi in range(4):
                    y_ = yt[bi % 2]
                    yv = V(k.yT[b][bi].ap().rearrange("(ft p) t -> p ft t", p=128), [k.yT[b][bi].trk] + list(k.yT[b][bi].parts.values()))
                    mk.dma("sp", y_[:, :, :n], yv[:, :, t0:t0 + n])
                    mk.cp("pool" if bi % 2 else "dve", ytb[:, bi, :, :n], y_[:, :, :n])
                for c in range(8):
                    g_ = G[c % 2]
                    mk.dma("sp", g_[:, :, :n], uv[:, MGB + c:MGB + c + 25:8, t0:t0 + n])
                    mk.act(g_[:, :, :n], g_[:, :, :n], AF.Sigmoid)
                    for bi in range(4):
                        ps = k.PS[ev % 4]
                        ev += 1
                        for ft in range(4):
                            mk.mm(ps[:, :n], wbr[:, bi, ft, c * 128:(c + 1) * 128], ytb[:, bi, ft, :n], start=(ft == 0), stop=(ft == 3))
                        if bi == 0:
                            mk.tt("dve", macc[:, :n], ps[:, :n], g_[:, 0, :n], ALU.mult)
                        else:
                            mk.tt("dve", mt[:, :n], ps[:, :n], g_[:, bi, :n], ALU.mult)
                            mk.tt("pool", macc[:, :n] if bi < 3 else merged[:, c, :n], macc[:, :n], mt[:, :n], ALU.add)
                for c in range(8):
                    ps = k.PS[4 + c % 4]
                    for kt in range(8):
                        mk.mm(ps[:, :n], wo[:, kt, c * 128:(c + 1) * 128], merged[:, kt, :n], start=(kt == 0), stop=(kt == 7))
                    mk.stt(h_[:, c, :n], ps[:, :n], k.modT[:, 16 + c, j:j + 1], h_[:, c, :n], ALU.mult, ALU.add)
                mk.dma("sp", hv[:, :, t0:t0 + n], h_[:, :, :n])


def stage_s3b(k, l, need_ctx):
    mk, cfg = k.mk, k.cfg
    NB = cfg.NB
    with scope(mk):
        w1b = mk.sb("b_w1", [128, 8, 4 * D], BF16)
        w2b = mk.sb("b_w2", [128, 32, D], BF16)
        with scope(mk):
            stg = [mk.sb(f"b_stg{i}", [128, 8, 512]) for i in range(2)]
            load_cast(mk, lambda pc: w1b[:, :, pc * 512:(pc + 1) * 512], lambda pc: k.w1[l, :, :, pc * 512:(pc + 1) * 512], 8, stg, 1)
            stg2 = [V(stg[i].h[:].rearrange("p a b -> p (a b)").rearrange("p (a b) -> p a b", a=4), [stg[i].trk], stg[i]) for i in range(2)]
            for pc in range(8):
                sv = k.w2[l, :, pc * 4:(pc + 1) * 4, :]
                mk.dma("sp", stg2[pc % 2], sv)
                mk.cp("pool" if pc % 2 else "dve", w2b[:, pc * 4:(pc + 1) * 4, :], stg2[pc % 2])
        G = 256
        hg = [mk.sb(f"b_h{i}", [128, 8, G]) for i in range(2)]
        sq = mk.sb("b_sq", [128, 8, G]); tmp = mk.sb("b_tmp", [128, 8, G]); rs = mk.sb("b_rs", [128, G])
        z2 = mk.sb("b_z2", [128, 8, G], BF16)
        hid = mk.sb("b_hid", [128, 32, G], BF16)
        rl = [mk.sb(f"b_rl{i}", [128, G]) for i in range(2)]
        it = 0
        ev = 0
        for b in range(NB):
            hv = hT_view(k, b)
            subs = []
            for (t0, n, seg) in cfg.groups:
                if seg == 0 and not need_ctx:
                    continue
                for o in range(0, n, G):
                    subs.append((t0 + o, min(G, n - o), seg))
            for (t0, n, seg) in subs:
                j = NB if seg == 0 else b
                h_ = hg[it % 2]
                it += 1
                mk.dma("sp", h_[:, :, :n], hv[:, :, t0:t0 + n])
                rms_mod(k, h_, n, j, k.A2, 24, lambda kt: z2[:, kt, :n], tmp, sq, rs, k.PS[7])
                for fc in range(32):
                    ps = k.PS[ev % 4]
                    r_ = rl[ev % 2]
                    ev += 1
                    for kt in range(8):
                        mk.mm(ps[:, :n], w1b[:, kt, fc * 128:(fc + 1) * 128], z2[:, kt, :n], start=(kt == 0), stop=(kt == 7))
                    mk.act(r_[:, :n], ps[:, :n], AF.Relu)
                    mk.tt("pool" if ev % 2 else "dve", hid[:, fc, :n], r_[:, :n], r_[:, :n], ALU.mult)
                for c in range(8):
                    ps = k.PS[4 + c % 2]
                    for ft in range(32):
                        mk.mm(ps[:, :n], w2b[:, ft, c * 128:(c + 1) * 128], hid[:, ft, :n], start=(ft == 0), stop=(ft == 31))
                    mk.stt(h_[:, c, :n], ps[:, :n], k.modT[:, 40 + c, j:j + 1], h_[:, c, :n], ALU.mult, ALU.add)
                mk.dma("sp", hv[:, :, t0:t0 + n], h_[:, :, :n])


_CACHE = {}


def host_consts(cfg):
    i = np.arange(64)
    kk, ff = i[:, None], i[None, :]
    masks = np.stack([(kk <= ff), (kk >= ff), (kk > ff), (kk < ff)], axis=1).astype(np.float32)
    rows = cfg.T // 64
    row_id = np.repeat(np.arange(rows), 64).astype(np.float32)
    col_id = np.tile(np.arange(64), rows).astype(np.float32)
    inv = (10000.0 ** (-np.arange(0, 32, 2, dtype=np.float32) / 32.0)).astype(np.float32)
    ang = np.stack([row_id[:, None] * inv, col_id[:, None] * inv], axis=1).astype(np.float32)
    rope = np.stack([np.cos(ang), np.sin(ang)], axis=1).astype(np.float32)
    return {"c_ident": np.eye(128, dtype=np.float32), "c_masks": np.ascontiguousarray(masks), "c_rope": np.ascontiguousarray(rope)}


def fm_cols():
    offs = np.cumsum([0, 512, 512, 512, 512, 8, 8, 512, 512, 512, 512, 512, 512, 512, 512, 128, 128, 4096])
    (dq, dk, dv, dg, dbeta, dalpha, hq, hff, hfb, hi, hgate, lx, lg, aq, ak, av, mg) = offs[:17]
    fm = []
    for base in (dq, dk, dv, dg, hq, hff, hfb, hgate, lx, lg):
        fm.append(np.arange(base, base + 512))
    fm.append(np.arange(mg, mg + 4096))
    fm = np.concatenate(fm)
    tm = np.concatenate([np.arange(hi, hi + 512), np.arange(aq, aq + 512), np.arange(ak, ak + 128), np.arange(av, av + 128),
                         np.arange(dbeta, dbeta + 8), np.arange(dalpha, dalpha + 8)])
    assert fm.size == NFM * 128 and tm.size == NTM
    return fm, tm


def pk(w):
    r = w.shape[0] // 128
    return np.ascontiguousarray(w.reshape(r, 128, *w.shape[1:]).swapaxes(0, 1))


def host_weights(inp, L):
    f = np.float32
    fm, tm = fm_cols()
    W = {}
    W["mod_w"] = np.stack([pk(inp["mod_w"][l]) for l in range(L)])
    W["mod_b"] = np.stack([np.ascontiguousarray(inp["mod_b"][l].reshape(48, 128).T) for l in range(L)])
    W["n1w"] = np.stack([np.ascontiguousarray(inp["norm1_w"][l].reshape(8, 128).T) for l in range(L)])
    W["n2w"] = np.stack([np.ascontiguousarray(inp["norm2_w"][l].reshape(8, 128).T) for l in range(L)])
    wfm = []
    for l in range(L):
        w = inp["w_in"][l][:, fm]
        w = w.reshape(8, 128, NFM, 128).transpose(2, 1, 0, 3)
        wfm.append(np.ascontiguousarray(w))
    W["w_fm"] = np.stack(wfm)
    W["w_tm"] = np.stack([pk(np.ascontiguousarray(inp["w_in"][l][:, tm])) for l in range(L)])
    W["dn_cw"] = np.stack([np.ascontiguousarray(inp["dn_conv_w"][l].reshape(4, 12, 128).transpose(2, 1, 0)) for l in range(L)])
    W["dn_ab"] = np.stack([np.broadcast_to(np.stack([inp["dn_a_log"][l].reshape(8), inp["dn_dt_bias"][l].reshape(8)])[None], (64, 2, 8)).copy() for l in range(L)])
    W["dn_nw"] = np.ascontiguousarray(inp["dn_norm_w"][:L].reshape(L, 128, 1))
    W["hg_nw"] = np.ascontiguousarray(inp["hg_norm_w"][:L].reshape(L, 128, 1))
    W["hg_lb"] = np.ascontiguousarray(inp["hg_lower_bounds"][:, :L].reshape(2, L, 4, 128).transpose(3, 0, 1, 2))
    W["lru_cw"] = np.stack([np.ascontiguousarray(inp["lru_conv_w"][l].reshape(4, 4, 128).transpose(2, 1, 0)) for l in range(L)])
    W["lru_cb"] = np.stack([np.ascontiguousarray(inp["lru_conv_b"][l].reshape(4, 128).T) for l in range(L)])
    W["lru_wa"] = np.stack([np.ascontiguousarray(inp["lru_w_a"][l].transpose(2, 0, 1, 3)) for l in range(L)])
    W["lru_wx"] = np.stack([np.ascontiguousarray(inp["lru_w_x"][l].transpose(2, 0, 1, 3)) for l in range(L)])
    W["lru_v"] = np.stack([np.ascontiguousarray(np.stack([inp["lru_b_a"][l], inp["lru_b_x"][l], inp["lru_lambda"][l]]).reshape(3, 2, 4, 128).transpose(3, 0, 1, 2)) for l in range(L)])
    W["att_nw"] = np.stack([np.broadcast_to(np.stack([inp["att_q_norm_w"][l], inp["att_k_norm_w"][l]])[None], (128, 2, 64)).copy() for l in range(L)])
    W["w_br"] = np.stack([np.ascontiguousarray(inp["w_branch"][l].reshape(4, 4, 128, D).transpose(2, 0, 1, 3)) for l in range(L)])
    W["w_out"] = np.stack([pk(inp["w_out"][l]) for l in range(L)])
    W["w1"] = np.stack([pk(inp["mlp_w1"][l]) for l in range(L)])
    W["w2"] = np.stack([pk(inp["mlp_w2"][l]) for l in range(L)])
    return {k_: np.ascontiguousarray(v, dtype=f) for k_, v in W.items()}


def run(inp, cfg, n_cores):
    key = (cfg.T, cfg.C, cfg.NB, cfg.L, cfg.debug)
    if key not in _CACHE:
        _CACHE[key] = build(cfg)
    nc, mk = _CACHE[key]
    inp = {k_: np.asarray(v, dtype=np.float32) for k_, v in inp.items()}
    W = host_weights(inp, cfg.L)
    W.update(host_consts(cfg))
    in_maps = []
    NB = cfg.NB
    for ci in range(n_cores):
        bs = slice(ci * NB, (ci + 1) * NB)
        m = dict(W)
        m["x_in"] = np.ascontiguousarray(np.concatenate([inp["ctx"][bs], inp["x"][bs]], axis=1))
        cv = np.concatenate([inp["c"][bs], inp["c_ctx"][None]], axis=0)
        m["cT"] = np.ascontiguousarray(cv.T.reshape(8, 128, NB + 1).transpose(1, 0, 2))
        in_maps.append(m)
    res = run_bass_kernel_spmd(nc, in_maps, core_ids=list(range(n_cores)))
    return res


def kernel(**inputs):
    cfg = Cfg(T=4096, C=256, NB=2, L=4)
    res = run(inputs, cfg, 8)
    return np.concatenate([r["out"] for r in res.results], axis=0).astype(np.float32)
```

```python
import numpy as np
import concourse.bass as bass
import concourse.mybir as mybir
from concourse.bass_utils import run_bass_kernel_spmd

F32 = mybir.dt.float32
BF16 = mybir.dt.bfloat16
AF = mybir.ActivationFunctionType
ALU = mybir.AluOpType
AX = mybir.AxisListType

SAME_ENGINE_SYNC = True


class StopBuild(Exception):
    pass


class Trk:
    __slots__ = ("w", "r_eng", "r_dma", "excl")

    def __init__(self):
        self.excl = False
        self.w = None
        self.r_eng = {}
        self.r_dma = {}


class V:
    __slots__ = ("ap", "trks", "t")

    def __init__(self, ap, trks, t=None):
        self.ap = ap
        self.trks = trks
        self.t = t

    def __getitem__(self, idx):
        return V(self.ap[idx], self.trks, self.t)

    def rr(self, pat, **kw):
        return V(self.ap.rearrange(pat, **kw), self.trks, self.t)

    def bc(self, shape):
        return V(self.ap.broadcast_to(shape) if hasattr(self.ap, "broadcast_to") else self.ap.to_broadcast(shape), self.trks)


class T:
    def __init__(self, handle, name, is_dram=True):
        self.h = handle
        self.name = name
        self.trk = Trk()
        self.parts = {}
        self.is_dram = is_dram
        self.dsem = None

    def ap(self):
        h = self.h
        return h.ap() if hasattr(h, "ap") else h[:]

    def __getitem__(self, idx):
        return V(self.h[idx], [self.trk], self)

    def p(self, key, idx=None):
        if key not in self.parts:
            self.parts[key] = Trk()
        a = self.h[idx] if idx is not None else self.ap()
        return V(a, [self.parts[key]], self)


class MK:
    def __init__(self, nc, es):
        self.nc = nc
        self.es = es
        self.E = {"pe": nc.tensor, "act": nc.scalar, "dve": nc.vector, "pool": nc.gpsimd, "sp": nc.sync}
        self.esem = {}
        self.cnt = {}
        self.seen = {}
        self.clock = {}
        for e in ("pe", "act", "dve", "pool"):
            self.esem[e] = es.enter_context(nc.semaphore("es_" + e))
            self.cnt[e] = 0
            self.clock[e] = [None]
        for e in self.E:
            self.seen[e] = {}
        self.dsem = {}
        self.dcnt = {}
        self.nwait = 0
        self.ninst = 0
        self.sem_es = es
        self.sem_ptr = 0
        self.limit = None

    def sb(self, name, shape, dt=F32):
        self.uid = getattr(self, "uid", 0) + 1
        h = self.es.enter_context(self.nc.sbuf_tensor("%s_u%d" % (name, self.uid), list(shape), dt))
        return T(h, name, False)

    def ps(self, name, shape, dt=F32):
        h = self.es.enter_context(self.nc.psum_tensor(name, list(shape), dt))
        t = T(h, name, False)
        t.trk.excl = True
        return t

    def dram(self, name, shape, dt=F32, kind="Internal"):
        h = self.nc.dram_tensor(name, list(shape), dt, kind=kind)
        return T(h, name)

    def get_dsem(self, name):
        if name not in self.dsem:
            self.dsem[name] = self.sem_es.enter_context(self.nc.semaphore("ds_" + name))
            self.dcnt[name] = 0
        return self.dsem[name]

    def _wait_event(self, eng, ev):
        if ev is None:
            return
        seen = self.seen[eng]
        if ev[0] == "e":
            _, e2, k = ev
            if e2 == eng and (eng == "pe" or not SAME_ENGINE_SYNC):
                return
            if seen.get(e2, 0) >= k:
                return
            self.E[eng].wait_ge(self.esem[e2], k)
            self.nwait += 1
            seen[e2] = k
            clk = self.clock[e2][k]
            if clk:
                for kk, vv in clk.items():
                    if seen.get(kk, 0) < vv:
                        seen[kk] = vv
        else:
            _, sname, val = ev
            key = "d:" + sname
            if seen.get(key, 0) >= val:
                return
            self.E[eng].wait_ge(self.dsem[sname], val)
            self.nwait += 1
            seen[key] = val

    def _deps(self, eng, outs, ins):
        for v in ins:
            for t in v.trks:
                self._wait_event(eng, t.w)
                if t.excl:
                    for e2, k in t.r_eng.items():
                        if e2 != eng:
                            self._wait_event(eng, ("e", e2, k))
        for v in outs:
            for t in v.trks:
                self._wait_event(eng, t.w)
                for e2, k in t.r_eng.items():
                    self._wait_event(eng, ("e", e2, k))
                for s, val in t.r_dma.items():
                    self._wait_event(eng, ("d", s, val))

    def I(self, eng, meth, **kw):
        outs, ins = [], []
        args = {}
        for k, v in kw.items():
            if isinstance(v, V):
                (outs if k in ("out", "accum_out") else ins).append(v)
                args[k] = v.ap
            else:
                args[k] = v
        if self.limit is not None and self.ninst >= self.limit:
            return None
        self._deps(eng, outs, ins)
        inst = getattr(self.E[eng], meth)(**args)
        self.cnt[eng] += 1
        k = self.cnt[eng]
        inst.then_inc(self.esem[eng], 1)
        self.ninst += 1
        snap = dict(self.seen[eng])
        self.clock[eng].append(snap)
        ev = ("e", eng, k)
        for v in outs:
            for t in v.trks:
                t.w = ev
                t.r_eng = {}
                t.r_dma = {}
        for v in ins:
            for t in v.trks:
                if t.r_eng.get(eng, 0) < k:
                    t.r_eng[eng] = k
        return inst

    def dma(self, q, out, in_, sem=None, **kw):
        if sem is None:
            st = out.t if (out.t is not None and not out.t.is_dram) else in_.t
            if st.dsem is None:
                st.dsem = "D%d" % self.sem_ptr
                self.sem_ptr += 1
            sem = st.dsem
        if self.limit is not None and self.ninst >= self.limit:
            return None
        s = self.get_dsem(sem)
        self._deps(q, [out], [in_])
        inst = self.E[q].dma_start(out=out.ap, in_=in_.ap, **kw)
        self.dcnt[sem] += 16
        val = self.dcnt[sem]
        inst.then_inc(s, 16)
        self.ninst += 1
        ev = ("d", sem, val)
        for t in out.trks:
            t.w = ev
            t.r_eng = {}
            t.r_dma = {}
        for t in in_.trks:
            if t.r_dma.get(sem, 0) < val:
                t.r_dma[sem] = val
        return inst

    def barrier(self):
        for eng in self.E:
            for e2 in self.esem:
                if self.cnt[e2] > 0:
                    self._wait_event(eng, ("e", e2, self.cnt[e2]))
            for s, val in self.dcnt.items():
                if val > 0:
                    self._wait_event(eng, ("d", s, val))

    def finish(self, eng="sp"):
        for s, val in self.dcnt.items():
            if val > 0:
                self._wait_event(eng, ("d", s, val))

    def mm(self, out, lhsT, rhs, start=True, stop=True, **kw):
        return self.I("pe", "matmul", out=out, lhsT=lhsT, rhs=rhs, start=start, stop=stop, **kw)

    def tr(self, out, in_, identity):
        return self.I("pe", "transpose", out=out, in_=in_, identity=identity)

    def act(self, out, in_, func, eng="act", **kw):
        return self.I("act", "activation", out=out, in_=in_, func=func, **kw)

    def ts(self, eng, out, in0, s1, s2=None, op0=ALU.mult, op1=None, **kw):
        if op1 is None:
            return self.I(eng, "tensor_scalar", out=out, in0=in0, scalar1=s1, scalar2=s2, op0=op0, **kw)
        return self.I(eng, "tensor_scalar", out=out, in0=in0, scalar1=s1, scalar2=s2, op0=op0, op1=op1, **kw)

    def tt(self, eng, out, in0, in1, op):
        return self.I(eng, "tensor_tensor", out=out, in0=in0, in1=in1, op=op)

    def stt(self, out, in0, scalar, in1, op0, op1, eng="dve"):
        return self.I(eng, "scalar_tensor_tensor", out=out, in0=in0, scalar=scalar, in1=in1, op0=op0, op1=op1)

    def cp(self, eng, out, in_):
        if eng == "act":
            return self.I("act", "copy", out=out, in_=in_)
        return self.I(eng, "tensor_copy", out=out, in_=in_)

    def memset(self, eng, out, val):
        return self.I(eng, "memset", ap=None, out=out, constant=val) if False else self._memset(eng, out, val)

    def _memset(self, eng, out, val):
        if self.limit is not None and self.ninst >= self.limit:
            return None
        self._deps(eng, [out], [])
        inst = self.E[eng].memset(out.ap, val)
        self.cnt[eng] += 1
        k = self.cnt[eng]
        inst.then_inc(self.esem[eng], 1)
        self.ninst += 1
        self.clock[eng].append(dict(self.seen[eng]))
        for t in out.trks:
            t.w = ("e", eng, k)
            t.r_eng = {}
            t.r_dma = {}
        return inst

from contextlib import ExitStack, contextmanager

EPS = 1e-6
STQ = "pool"
D = 1024
NFM = 72
NTM = 1296
DNQ, DNK, DNV, DNG, HGQ, HGFF, HGFB, HGG, LRX, LRG, MGB = 0, 4, 8, 12, 16, 20, 24, 28, 32, 36, 40
TM_HGI, TM_ATQ, TM_ATK, TM_ATV, TM_BETA, TM_ALPHA = 0, 512, 1024, 1152, 1280, 1288


def vus(v, axis):
    return V(v.ap.unsqueeze(axis), v.trks)


def vbc(v, shape):
    return V(v.ap.broadcast_to(list(shape)), v.trks)


class Cfg:
    def __init__(self, T=4096, C=256, NB=2, L=4, debug=False):
        self.T, self.C, self.NB, self.L = T, C, NB, L
        self.NT = T + C
        self.debug = debug
        self.groups = []
        t = 0
        while t < C:
            n = min(512, C - t)
            self.groups.append((t, n, 0))
            t += n
        while t < self.NT:
            n = min(512, self.NT - t)
            self.groups.append((t, n, 1))
            t += n
        self.ntile = self.NT // 128
        self.nch = self.NT // 64
        self.cch = C // 64

    def gi_of(self, t):
        for i, (t0, n, s) in enumerate(self.groups):
            if t0 <= t < t0 + n:
                return i
        raise ValueError

    def order(self, d):
        if d == 0:
            return list(range(self.nch))
        return list(range(self.cch - 1, -1, -1)) + list(range(self.nch - 1, self.cch - 1, -1))


class K:
    pass


@contextmanager
def scope(mk):
    old = mk.es
    ptr = mk.sem_ptr
    with ExitStack() as es:
        mk.es = es
        try:
            yield es
        finally:
            lim, mk.limit = mk.limit, None
            mk.barrier()
            mk.limit = lim
            mk.es = old
            mk.sem_ptr = ptr


def build(cfg):
    nc = bass.Bass("TRN2", target_bir_lowering=False)
    L, NB, NT, C = cfg.L, cfg.NB, cfg.NT, cfg.C
    root = ExitStack()
    mk = MK(nc, root)
    k = K()
    k.cfg, k.mk, k.nc = cfg, mk, nc
    T_ = T

    def din(name, shape):
        return T_(nc.dram_tensor(name, list(shape), F32, kind="ExternalInput"), name)

    k.x_in = din("x_in", [NB, NT, D])
    k.cT = din("cT", [128, 8, NB + 1])
    k.mod_w = din("mod_w", [L, 128, 8, 6 * D])
    k.mod_b = din("mod_b", [L, 128, 48])
    k.n1w = din("n1w", [L, 128, 8])
    k.n2w = din("n2w", [L, 128, 8])
    k.w_fm = din("w_fm", [L, NFM, 128, 8, 128])
    k.w_tm = din("w_tm", [L, 128, 8, NTM])
    k.dn_cw = din("dn_cw", [L, 128, 12, 4])
    k.dn_ab = din("dn_ab", [L, 64, 2, 8])
    k.dn_nw = din("dn_nw", [L, 128, 1])
    k.hg_nw = din("hg_nw", [L, 128, 1])
    k.hg_lb = din("hg_lb", [128, 2, L, 4])
    k.lru_cw = din("lru_cw", [L, 128, 4, 4])
    k.lru_cb = din("lru_cb", [L, 128, 4])
    k.lru_wa = din("lru_wa", [L, 128, 2, 4, 128])
    k.lru_wx = din("lru_wx", [L, 128, 2, 4, 128])
    k.lru_v = din("lru_v", [L, 128, 3, 2, 4])
    k.att_nw = din("att_nw", [L, 128, 2, 64])
    k.w_br = din("w_br", [L, 128, 4, 4, D])
    k.w_out = din("w_out", [L, 128, 8, D])
    k.w1 = din("w1", [L, 128, 8, 4 * D])
    k.w2 = din("w2", [L, 128, 32, D])
    k.c_ident = din("c_ident", [128, 128])
    k.c_masks = din("c_masks", [64, 4, 64])
    k.c_rope = din("c_rope", [cfg.T, 2, 2, 16])
    k.out = T_(nc.dram_tensor("out", [NB, cfg.T, D], F32, kind="ExternalOutput"), "out")

    kind = "ExternalOutput" if cfg.debug else "Internal"

    def dsc(name, shape):
        return T_(nc.dram_tensor(name, list(shape), F32, kind=kind), name)

    k.hT = [dsc(f"hT{b}", [D, NT]) for b in range(NB)]
    k.uT = [dsc(f"uT{b}", [NFM * 128, NT]) for b in range(NB)]
    k.utok = [dsc(f"utok{b}", [NT, NTM]) for b in range(NB)]
    k.osc = [[dsc(f"osc{b}_{i}", [NT, 512]) for i in range(4)] for b in range(NB)]
    k.yT = [[dsc(f"yT{b}_{i}", [512, NT]) for i in range(4)] for b in range(NB)]

    k.PS = [mk.ps(f"psb{i}", [128, 512]) for i in range(8)]
    k.ident = mk.sb("ident", [128, 128])
    k.identb = mk.sb("identb", [128, 128], BF16)
    k.ones = mk.sb("ones", [128, 128])
    k.masks = mk.sb("masks", [64, 4, 64])
    k.scT = mk.sb("scT", [128, 8, NB + 1])
    k.modT = mk.sb("modT", [128, 48, NB + 1])
    k.A1 = mk.sb("A1", [128, 8, NB + 1])
    k.A2 = mk.sb("A2", [128, 8, NB + 1])
    k.lb = mk.sb("lb", [128, 2, L, 4])
    mk.dma("sp", k.ident[:], k.c_ident[:, :])
    mk.dma("sp", k.masks[:], k.c_masks[:, :, :])
    mk.dma("sp", k.scT[:], k.cT[:, :, :])
    mk.dma("sp", k.lb[:], k.hg_lb[:, :, :, :])
    mk._memset("dve", k.ones[:], 1.0)
    mk.cp("dve", k.identb[:], k.ident[:])
    mk.act(k.scT[:], k.scT[:], AF.Silu)
    prep_lb(k)

    mk.limit = getattr(cfg, "limit", None)
    try:
        _stages(k, cfg, mk, L, NB)
    except StopBuild:
        pass
    mk.limit = None
    mk.barrier()
    root.close()
    return nc, mk


def _stages(k, cfg, mk, L, NB):
    stage_in(k)
    for l in range(L):
        need_ctx = l < L - 1
        stage_mods(k, l)
        for b in range(NB):
            stage_s1(k, l, b)
        if cfg.debug == "s1":
            break
        only = getattr(cfg, "only", None)
        for b in range(NB):
            if only is None or "lru" in only:
                stage_lru(k, l, b)
            if only is None or "att" in only:
                stage_att(k, l, b, need_ctx)
            if only is None or "dn" in only:
                stage_dn(k, l, b)
                if not getattr(cfg, "stop", ""):
                    stage_post(k, l, b, 0, k.dn_nw, DNG)
            if only is None or "hg" in only:
                stage_hg(k, l, b)
                stage_post(k, l, b, 1, k.hg_nw, HGG)
        if cfg.debug == "mix":
            break
        stage_s3a(k, l, need_ctx)
        stage_s3b(k, l, need_ctx)
    stage_out(k)


def prep_lb(k):
    mk, L = k.mk, k.cfg.L
    with scope(mk):
        e = mk.sb("lb_e", [128, 2, L, 4])
        s = mk.sb("lb_s", [128, 2, 4])
        mk.act(e[:], k.lb[:], AF.Exp)
        mk.cp("dve", s[:], e[:, :, 0, :])
        for l in range(1, L):
            mk.tt("dve", s[:], s[:], e[:, :, l, :], ALU.add)
        mk.I("dve", "reciprocal", out=s[:], in_=s[:])
        mk.tt("dve", e[:], e[:], vbc(vus(s[:], 2), [128, 2, L, 4]), ALU.mult)
        mk._memset("dve", k.lb[:, :, 0, :], 0.0)
        for l in range(1, L):
            mk.tt("dve", k.lb[:, :, l, :], k.lb[:, :, l - 1, :], e[:, :, l, :], ALU.add)


def hT_view(k, b):
    return V(k.hT[b].ap().rearrange("(kt p) t -> p kt t", p=128), [k.hT[b].trk])


def stage_in(k):
    mk, cfg = k.mk, k.cfg
    with scope(mk):
        xt = [mk.sb(f"in_x{i}", [128, D]) for i in range(2)]
        st = [mk.sb(f"in_s{i}", [128, 8, 128]) for i in range(2)]
        i = 0
        for b in range(cfg.NB):
            hv = hT_view(k, b)
            for ti in range(cfg.ntile):
                x, s = xt[i % 2], st[i % 2]
                mk.dma("sp", x[:], k.x_in[b, ti * 128:(ti + 1) * 128, :])
                for half in range(2):
                    ps = k.PS[half]
                    for q in range(4):
                        mk.tr(ps[:, q * 128:(q + 1) * 128], x[:, (half * 4 + q) * 128:(half * 4 + q + 1) * 128], k.ident[:])
                    mk.cp("act" if half else "dve", s[:, half * 4:half * 4 + 4, :], ps[:].rr("p (a b) -> p a b", a=4))
                mk.dma(STQ, hv[:, :, ti * 128:(ti + 1) * 128], s[:])
                i += 1


def stage_out(k):
    mk, cfg = k.mk, k.cfg
    with scope(mk):
        ht = [mk.sb(f"out_h{i}", [128, 8, 128]) for i in range(2)]
        st = [mk.sb(f"out_s{i}", [128, D]) for i in range(2)]
        i = 0
        for b in range(cfg.NB):
            hv = hT_view(k, b)
            for ti in range(cfg.T // 128):
                h, s = ht[i % 2], st[i % 2]
                t0 = cfg.C + ti * 128
                mk.dma("sp", h[:], hv[:, :, t0:t0 + 128])
                for half in range(2):
                    ps = k.PS[half]
                    for q in range(4):
                        mk.tr(ps[:, q * 128:(q + 1) * 128], h[:, half * 4 + q, :], k.ident[:])
                    mk.cp("act" if half else "dve", s[:, half * 512:(half + 1) * 512], ps[:])
                mk.dma(STQ, k.out[b, ti * 128:(ti + 1) * 128, :], s[:])
                i += 1


def stage_mods(k, l):
    mk, cfg = k.mk, k.cfg
    NJ = cfg.NB + 1
    with scope(mk):
        wp = [mk.sb(f"mod_w{i}", [128, 8, 512]) for i in range(2)]
        mb = mk.sb("mod_bt", [128, 48])
        nw = mk.sb("mod_nw", [128, 2, 8])
        mk.dma("sp", mb[:], k.mod_b[l])
        mk.dma("sp", nw[:, 0, :], k.n1w[l])
        mk.dma("sp", nw[:, 1, :], k.n2w[l])
        pm = k.PS[0]
        for pc in range(12):
            w = wp[pc % 2]
            mk.dma("sp", w[:], k.mod_w[l, :, :, pc * 512:(pc + 1) * 512])
            for j in range(4):
                o = (pc * 4 + j) * 4
                for kt in range(8):
                    mk.mm(pm[:, o:o + NJ], w[:, kt, j * 128:(j + 1) * 128], k.scT[:, kt, :], start=(kt == 0), stop=(kt == 7))
        mk.tt("dve", k.modT[:], pm[:, 0:192].rr("p (a b) -> p a b", b=4)[:, :, 0:NJ], vbc(vus(mb[:], 2), [128, 48, NJ]), ALU.add)
        for (A, w_i, sc0) in ((k.A1, 0, 8), (k.A2, 1, 32)):
            mk.ts("dve", A[:], k.modT[:, sc0:sc0 + 8, :], 1.0, None, op0=ALU.add)
            mk.tt("dve", A[:], A[:], vbc(vus(nw[:, w_i, :], 2), [128, 8, NJ]), ALU.mult)


def rms_mod(k, hg, n, j, A, sh0, zt_out, tmp, sq, rs, ps):
    mk = k.mk
    mk.act(sq[:, :, :n], hg[:, :, :n], AF.Square)
    for kt in range(8):
        mk.mm(ps[:, :n], k.ones[:], sq[:, kt, :n], start=(kt == 0), stop=(kt == 7))
    mk.act(rs[:, :n], ps[:, :n], AF.Sqrt, scale=1.0 / D, bias=EPS)
    mk.I("dve", "reciprocal", out=rs[:, :n], in_=rs[:, :n])
    mk.tt("dve", tmp[:, :, :n], hg[:, :, :n], vbc(vus(rs[:, :n], 1), [128, 8, n]), ALU.mult)
    for kt in range(8):
        if kt % 2 == 0:
            mk.ts("dve", zt_out(kt), tmp[:, kt, :n], A[:, kt, j:j + 1], k.modT[:, sh0 + kt, j:j + 1], op0=ALU.mult, op1=ALU.add)
        else:
            mk.act(zt_out(kt), tmp[:, kt, :n], AF.Identity, scale=A[:, kt, j:j + 1], bias=k.modT[:, sh0 + kt, j:j + 1])


def stage_s1(k, l, b):
    mk, cfg = k.mk, k.cfg
    NT = cfg.NT
    hv = hT_view(k, b)
    with scope(mk):
        ZT = mk.sb("s1_zt", [128, 8, NT], BF16)
        with scope(mk):
            hg = [mk.sb(f"s1_h{i}", [128, 8, 512]) for i in range(2)]
            sq = mk.sb("s1_sq", [128, 8, 512])
            tmp = mk.sb("s1_tmp", [128, 8, 512])
            rs = mk.sb("s1_rs", [128, 512])
            for gi, (t0, n, seg) in enumerate(cfg.groups):
                h = hg[gi % 2]
                mk.dma("sp", h[:, :, :n], hv[:, :, t0:t0 + n])
                j = cfg.NB if seg == 0 else b
                rms_mod(k, h, n, j, k.A1, 0, lambda kt: ZT.p(gi, (slice(None), kt, slice(t0, t0 + n))), tmp, sq, rs, k.PS[gi % 2])
        with scope(mk):
            wf = [mk.sb(f"s1_wf{i}", [128, 8, 128]) for i in range(2)]
            wb = [mk.sb(f"s1_wb{i}", [128, 8, 128], BF16) for i in range(2)]
            stg = [mk.sb(f"s1_st{i}", [128, NT]) for i in range(2)]
            ev = 0
            for blk in range(NFM):
                s = blk % 2
                mk.dma("sp", wf[s][:], k.w_fm[l, blk])
                mk.cp("pool", wb[s][:], wf[s][:])
                for gi, (t0, n, seg) in enumerate(cfg.groups):
                    ps = k.PS[ev % 4]
                    for kt in range(8):
                        mk.mm(ps[:, :n], wb[s][:, kt, :], ZT.p(gi, (slice(None), kt, slice(t0, t0 + n))), start=(kt == 0), stop=(kt == 7))
                    mk.cp("act" if ev % 2 else "dve", stg[s].p(gi, (slice(None), slice(t0, t0 + n))), ps[:, :n])
                    ev += 1
                allp = V(stg[s].ap(), [stg[s].parts[g] for g in range(len(cfg.groups))], stg[s])
                mk.dma(STQ, k.uT[b].p(blk, (slice(blk * 128, (blk + 1) * 128), slice(None))), allp)
        with scope(mk):
            wtm = mk.sb("s1_wtm", [128, 8, NTM], BF16)
            wst = [mk.sb(f"s1_wst{i}", [128, 8, 432]) for i in range(2)]
            stt_ = [mk.sb(f"s1_stt{i}", [128, NTM]) for i in range(2)]
            for pc in range(3):
                mk.dma("sp", wst[pc % 2][:], k.w_tm[l, :, :, pc * 432:(pc + 1) * 432])
                mk.cp("pool", wtm.p(pc, (slice(None), slice(None), slice(pc * 432, (pc + 1) * 432))), wst[pc % 2][:])
            wall = lambda kt, c0, cn: V(wtm.h[:, kt, c0:c0 + cn], [wtm.parts[p] for p in range(3)])
            ev = 0
            for ti in range(cfg.ntile):
                gi = cfg.gi_of(ti * 128)
                s = ti % 2
                for (c0, cn) in ((0, 512), (512, 512), (1024, NTM - 1024)):
                    ps = k.PS[ev % 4]
                    for kt in range(8):
                        mk.mm(ps[:, :cn], ZT.p(gi, (slice(None), kt, slice(ti * 128, (ti + 1) * 128))), wall(kt, c0, cn), start=(kt == 0), stop=(kt == 7))
                    mk.cp("act" if ev % 2 else "dve", stt_[s][:, c0:c0 + cn], ps[:, :cn])
                    ev += 1
                mk.dma(STQ, k.utok[b].p(ti, (slice(ti * 128, (ti + 1) * 128), slice(None))), stt_[s][:])


def uT_rows(k, b, blk):
    return k.uT[b].p(blk, (slice(blk * 128, (blk + 1) * 128), slice(None)))


def conv4(mk, out, x, w, segs):
    for (s0, sn) in segs:
        e = s0 + sn
        mk.ts("dve", out[:, s0:e], x[:, s0:e], w[:, 1:2], None, op0=ALU.mult)
        mk.stt(out[:, s0 + 1:e], x[:, s0:e - 1], w[:, 0:1], out[:, s0 + 1:e], ALU.mult, ALU.add)
        mk.stt(out[:, s0:e - 1], x[:, s0 + 1:e], w[:, 2:3], out[:, s0:e - 1], ALU.mult, ALU.add)
        mk.stt(out[:, s0:e - 2], x[:, s0 + 2:e], w[:, 3:4], out[:, s0:e - 2], ALU.mult, ALU.add)


def softplus(mk, out, x, t1):
    mk.act(t1, x, AF.Abs)
    mk.act(t1, t1, AF.Exp, scale=-1.0)
    mk.act(t1, t1, AF.Ln, bias=1.0)
    mk.stt(out, x, 0.0, t1, ALU.max, ALU.add)


def stage_lru(k, l, b):
    mk, cfg = k.mk, k.cfg
    NT, C = cfg.NT, cfg.C
    segs = [(0, C), (C, cfg.T)]
    with scope(mk):
        cw = mk.sb("lr_cw", [128, 4, 4]); cb = mk.sb("lr_cb", [128, 4])
        wa = mk.sb("lr_wa", [128, 2, 4, 128]); wx = mk.sb("lr_wx", [128, 2, 4, 128])
        vv = mk.sb("lr_v", [128, 3, 2, 4]); csp = mk.sb("lr_csp", [128, 2, 4]); t1 = mk.sb("lr_t1", [128, 2, 4])
        mk.dma("sp", cw[:], k.lru_cw[l]); mk.dma("sp", cb[:], k.lru_cb[l])
        mk.dma("sp", wa[:], k.lru_wa[l]); mk.dma("sp", wx[:], k.lru_wx[l])
        mk.dma("sp", vv[:], k.lru_v[l])
        mk.ts("dve", csp[:], vv[:, 2], -1.0, None, op0=ALU.mult)
        softplus(mk, csp[:], csp[:], t1[:])
        mk.ts("dve", csp[:], csp[:], -8.0, None, op0=ALU.mult)
        xb = mk.sb("lr_xb", [128, NT]); gb = mk.sb("lr_gb", [128, NT]); xc = mk.sb("lr_xc", [128, NT])
        R = mk.sb("lr_r", [128, NT]); IG = mk.sb("lr_ig", [128, NT]); A = mk.sb("lr_a", [128, NT])
        H = [mk.sb(f"lr_h{d}", [128, NT]) for d in range(2)]
        ev = 0
        for h in range(4):
            mk.dma("sp", xb[:], uT_rows(k, b, LRX + h))
            mk.dma("sp", gb[:], uT_rows(k, b, LRG + h))
            conv4(mk, xc, xb, cw[:, h, :], segs)
            mk.ts("dve", xc[:], xc[:], cb[:, h:h + 1], None, op0=ALU.add)
            for d in range(2):
                for (t0, n, seg) in cfg.groups:
                    pa, px = k.PS[ev % 2], k.PS[2 + ev % 2]
                    ev += 1
                    mk.mm(pa[:, :n], wa[:, d, h, :], xc[:, t0:t0 + n])
                    mk.mm(px[:, :n], wx[:, d, h, :], xc[:, t0:t0 + n])
                    mk.act(R[:, t0:t0 + n], pa[:, :n], AF.Sigmoid, bias=vv[:, 0, d, h:h + 1])
                    mk.act(IG[:, t0:t0 + n], px[:, :n], AF.Sigmoid, bias=vv[:, 1, d, h:h + 1])
                mk.act(A[:], R[:], AF.Exp, scale=csp[:, d, h:h + 1])
                mk.tt("dve", R[:], A[:], A[:], ALU.mult)
                mk.ts("dve", R[:], R[:], -1.0, 1.0, op0=ALU.mult, op1=ALU.add)
                mk.ts("dve", R[:], R[:], 0.0, None, op0=ALU.max)
                mk.act(R[:], R[:], AF.Sqrt)
                mk.tt("dve", R[:], R[:], IG[:], ALU.mult)
                mk.tt("dve", R[:], R[:], xc[:], ALU.mult)
                if d == 0:
                    mk.I("dve", "tensor_tensor_scan", out=H[0][:], data0=A[:], data1=R[:], initial=0.0, op0=ALU.mult, op1=ALU.add)
                else:
                    mk.I("dve", "tensor_tensor_scan", out=H[1][:, C - 1::-1] if False else V(H[1].h[:, 0:C][:, ::-1], [H[1].trk]),
                         data0=V(A.h[:, 0:C][:, ::-1], [A.trk]), data1=V(R.h[:, 0:C][:, ::-1], [R.trk]),
                         initial=0.0, op0=ALU.mult, op1=ALU.add)
                    mk.I("dve", "tensor_tensor_scan", out=V(H[1].h[:, C:NT][:, ::-1], [H[1].trk]),
                         data0=V(A.h[:, C:NT][:, ::-1], [A.trk]), data1=V(R.h[:, C:NT][:, ::-1], [R.trk]),
                         initial=H[1][:, 0:1], op0=ALU.mult, op1=ALU.add)
            mk.tt("dve", H[0][:], H[0][:], H[1][:], ALU.add)
            mk.act(gb[:], gb[:], AF.Gelu)
            mk.tt("dve", H[0][:], H[0][:], gb[:], ALU.mult)
            mk.dma(STQ, k.yT[b][2].p(h, (slice(h * 128, (h + 1) * 128), slice(None))), H[0][:])


def stage_att(k, l, b, need_ctx):
    mk, cfg = k.mk, k.cfg
    NT, C, ntile = cfg.NT, cfg.C, cfg.ntile
    with scope(mk):
        QT = mk.sb("at_qt", [64, 8, NT], BF16)
        KT = mk.sb("at_kt", [64, 2, NT], BF16)
        VA = mk.sb("at_va", [128, ntile, 2, 65], BF16)
        nw = mk.sb("at_nw", [128, 2, 64])
        mk.dma("sp", nw[:], k.att_nw[l])
        mk._memset("pool", VA[:], 1.0)
        with scope(mk):
            a_ = [mk.sb(f"at_a{i}", [128, 768]) for i in range(2)]
            cs_ = [mk.sb(f"at_cs{i}", [128, 2, 2, 16]) for i in range(2)]
            sq = mk.sb("at_sq", [128, 640]); ss = mk.sb("at_ss", [128, 10])
            qn = mk.sb("at_qn", [128, 640]); qr = mk.sb("at_qr", [128, 640])
            tt_ = [mk.sb(f"at_t{i}", [128, 10, 2, 16]) for i in range(4)]
            for ti in range(ntile):
                a = a_[ti % 2]
                mk.dma("sp", a[:], k.utok[b].p(ti, (slice(ti * 128, (ti + 1) * 128), slice(TM_ATQ, TM_ATQ + 768))))
                mk.tt("dve", sq[:], a[:, 0:640], a[:, 0:640], ALU.mult)
                mk.I("dve", "tensor_reduce", out=ss[:], in_=sq[:].rr("p (h d) -> p h d", d=64), axis=AX.X, op=ALU.add)
                mk.act(ss[:], ss[:], AF.Sqrt, scale=1.0 / 64, bias=EPS)
                mk.I("dve", "reciprocal", out=ss[:], in_=ss[:])
                mk.tt("dve", qn[:].rr("p (h d) -> p h d", d=64), a[:, 0:640].rr("p (h d) -> p h d", d=64), vbc(vus(ss[:], 2), [128, 10, 64]), ALU.mult)
                mk.tt("pool", qn[:, 0:512].rr("p (h d) -> p h d", d=64), qn[:, 0:512].rr("p (h d) -> p h d", d=64), vbc(vus(nw[:, 0, :], 1), [128, 8, 64]), ALU.mult)
                mk.tt("pool", qn[:, 512:640].rr("p (h d) -> p h d", d=64), qn[:, 512:640].rr("p (h d) -> p h d", d=64), vbc(vus(nw[:, 1, :], 1), [128, 2, 64]), ALU.mult)
                if ti * 128 >= C:
                    cs = cs_[ti % 2]
                    tl = ti * 128 - C
                    mk.dma("sp", cs[:], k.c_rope[tl:tl + 128])
                    xv = qn[:].rr("p (h a f r) -> p h a f r", h=10, a=2, f=2)
                    ov = qr[:].rr("p (h a f r) -> p h a f r", h=10, a=2, f=2)
                    x1, x2 = xv[:, :, :, 0, :], xv[:, :, :, 1, :]
                    cc = vbc(vus(cs[:, 0], 1), [128, 10, 2, 16]); sn = vbc(vus(cs[:, 1], 1), [128, 10, 2, 16])
                    mk.tt("dve", tt_[0][:], x1, cc, ALU.mult)
                    mk.tt("pool", tt_[1][:], x2, sn, ALU.mult)
                    mk.tt("dve", tt_[2][:], x2, cc, ALU.mult)
                    mk.tt("pool", tt_[3][:], x1, sn, ALU.mult)
                    mk.tt("dve", ov[:, :, :, 0, :], tt_[0][:], tt_[1][:], ALU.subtract)
                    mk.tt("dve", ov[:, :, :, 1, :], tt_[2][:], tt_[3][:], ALU.add)
                    src = qr
                else:
                    src = qn
                for hh in range(10):
                    ps = k.PS[hh // 4]
                    o = (hh % 4) * 128
                    mk.tr(ps[0:64, o:o + 128], src[:, hh * 64:(hh + 1) * 64], k.ident[:])
                mk.cp("act", QT[:, 0:4, ti * 128:(ti + 1) * 128], k.PS[0][0:64, :].rr("p (a b) -> p a b", a=4))
                mk.cp("dve", QT[:, 4:8, ti * 128:(ti + 1) * 128], k.PS[1][0:64, :].rr("p (a b) -> p a b", a=4))
                mk.cp("act", KT[:, 0:2, ti * 128:(ti + 1) * 128], k.PS[2][0:64, 0:256].rr("p (a b) -> p a b", a=2))
                mk.cp("pool", VA[:, ti, :, 0:64], a[:, 640:768].rr("p (g d) -> p g d", g=2))
        with scope(mk):
            OT = [mk.sb(f"at_ot{i}", [128, 512]) for i in range(2)]
            PT = [mk.sb(f"at_pt{i}", [128, 512], BF16) for i in range(2)]
            YS = [mk.sb(f"at_ys{i}", [128, 4, 128]) for i in range(2)]
            rc = mk.sb("at_rc", [128, 8])
            yv = V(k.yT[b][3].ap().rearrange("(ft p) t -> p ft t", p=128), [k.yT[b][3].trk])
            it = 0
            qtiles = list(range(ntile)) if need_ctx else list(range(C // 128, ntile))
            for qn_i, qi in enumerate(qtiles):
                keyt = list(range(C // 128)) if qi * 128 < C else list(range(ntile))
                ot = OT[qn_i % 2]
                items = [(g, kt) for g in range(2) for kt in keyt]

                def front(i, base=it):
                    g, kt = items[i]
                    S = k.PS[(base + i) % 2]; pt = PT[(base + i) % 2]
                    mk.mm(S[:, :], KT[:, g, kt * 128:(kt + 1) * 128], QT[:, 4 * g:4 * g + 4, qi * 128:(qi + 1) * 128])
                    mk.act(pt[:], S[:], AF.Exp, scale=0.125)

                def back(i, base=it):
                    g, kt = items[i]
                    pt = PT[(base + i) % 2]
                    for hq in range(4):
                        mk.mm(k.PS[2 + hq][:, 0:65], pt[:, hq * 128:(hq + 1) * 128], VA[:, kt, g, :], start=(kt == keyt[0]), stop=(kt == keyt[-1]))
                    if kt == keyt[-1]:
                        for hq in range(4):
                            hh = 4 * g + hq
                            mk.I("dve", "reciprocal", out=rc[:, hh:hh + 1], in_=k.PS[2 + hq][:, 64:65])
                            mk.ts("dve", ot[:, hh * 64:(hh + 1) * 64], k.PS[2 + hq][:, 0:64], rc[:, hh:hh + 1], None, op0=ALU.mult)

                front(0)
                for i in range(len(items)):
                    if i + 1 < len(items):
                        front(i + 1)
                    back(i)
                it += len(items)
                for ft in range(4):
                    mk.tr(k.PS[6][:, ft * 128:(ft + 1) * 128], ot[:, ft * 128:(ft + 1) * 128], k.ident[:])
                ys = YS[qn_i % 2]
                mk.cp("act", ys[:], k.PS[6][:].rr("p (a b) -> p a b", a=4))
                mk.dma(STQ, yv[:, :, qi * 128:(qi + 1) * 128], ys[:])


def stage_dn(k, l, b):
    mk, cfg = k.mk, k.cfg
    NT, C, nch = cfg.NT, cfg.C, cfg.nch
    segs = [(0, C), (C, cfg.T)]
    LE, GE, GT, LT = 0, 1, 2, 3
    with scope(mk):
        graw = mk.sb("dn_graw", [64, nch, 16]); ab = mk.sb("dn_ab", [64, 2, 8])
        beta = mk.sb("dn_beta", [64, nch, 8]); g = mk.sb("dn_g", [64, nch, 8]); t1 = mk.sb("dn_t1", [64, nch, 8])
        gam = mk.sb("dn_gam", [64, nch, 8]); erg = mk.sb("dn_erg", [64, nch, 8]); cdec = mk.sb("dn_cdec", [128, nch, 8])
        eg = mk.sb("dn_eg", [64, nch, 8]); bg = mk.sb("dn_bg", [64, nch, 8]); ea = mk.sb("dn_ea", [64, 8])
        cw = mk.sb("dn_cw", [128, 12, 4])
        mk.dma("sp", cw[:], k.dn_cw[l])
        mk.dma("sp", ab[:], k.dn_ab[l])
        gsrc = k.utok[b].h[:, TM_BETA:TM_BETA + 16].rearrange("(n c) f -> c n f", c=64)
        for n0 in range(0, nch, 16):
            n1 = min(nch, n0 + 16)
            mk.dma("sp", graw[:, n0:n1, :], V(gsrc[:, n0:n1, :], [k.utok[b].trk] + list(k.utok[b].parts.values())))
        mk.act(beta[:], graw[:, :, 0:8], AF.Sigmoid)
        mk.tt("dve", g[:], graw[:, :, 8:16], vbc(vus(ab[:, 1, :], 1), [64, nch, 8]), ALU.add)
        softplus(mk, g[:], g[:], t1[:])
        mk.act(ea[:], ab[:, 0, :], AF.Exp)
        mk.tt("dve", g[:], g[:], vbc(vus(ea[:], 1), [64, nch, 8]), ALU.mult)
        mk.ts("dve", g[:], g[:], -1.0, None, op0=ALU.mult)
        for d in range(2):
            cs = slice(d * 4, d * 4 + 4)
            n4 = nch * 4
            mk.mm(k.PS[0][0:64, 0:n4], k.masks[:, LE if d == 0 else GE, :], g[:, :, cs])
            mk.cp("dve", gam[:, :, cs], k.PS[0][0:64, 0:n4].rr("p (n f) -> p n f", f=4))
            mk.mm(k.PS[1][0:64, 0:n4], k.masks[:, GT if d == 0 else LT, :], g[:, :, cs])
            mk.act(erg[:, :, cs], k.PS[1][0:64, 0:n4].rr("p (n f) -> p n f", f=4), AF.Exp)
            mk.mm(k.PS[2][:, 0:n4], k.ones[0:64, :], g[:, :, cs])
            mk.act(cdec[:, :, cs], k.PS[2][:, 0:n4].rr("p (n f) -> p n f", f=4), AF.Exp)
        mk.act(eg[:], gam[:], AF.Exp)
        mk.tt("dve", bg[:], beta[:], eg[:], ALU.mult)
        raw = mk.sb("dn_raw", [128, NT])
        sq = mk.sb("dn_sq", [128, 512]); rs = mk.sb("dn_rs", [128, 512])
        stop = getattr(cfg, "stop", "")
        if stop == "dn_prep":
            return
        for hp in range(2):
            with scope(mk):
                Q = [mk.sb(f"dn_q{i}", [128, NT]) for i in range(2)]
                Kt = [mk.sb(f"dn_k{i}", [128, NT]) for i in range(2)]
                Vt = [mk.sb(f"dn_v{i}", [128, NT]) for i in range(2)]
                for hh in range(2):
                    h = 2 * hp + hh
                    for (dst, blk0, wi) in ((Q[hh], DNQ, 0), (Kt[hh], DNK, 1), (Vt[hh], DNV, 2)):
                        mk.dma("sp", raw[:], uT_rows(k, b, blk0 + h))
                        conv4(mk, dst, raw, cw[:, wi * 4 + h, :], segs)
                        mk.act(dst[:], dst[:], AF.Silu)
                    for (dst, scl) in ((Q[hh], 128.0 ** -0.5), (Kt[hh], 1.0)):
                        for gi, (t0, n, seg) in enumerate(cfg.groups):
                            ps = k.PS[gi % 2]
                            mk.act(sq[:, :n], dst[:, t0:t0 + n], AF.Square)
                            mk.mm(ps[:, :n], k.ones[:], sq[:, :n])
                            mk.act(rs[:, :n], ps[:, :n], AF.Sqrt, bias=EPS)
                            mk.I("dve", "reciprocal", out=rs[:, :n], in_=rs[:, :n])
                            mk.stt(dst[:, t0:t0 + n], dst[:, t0:t0 + n], scl, rs[:, :n], ALU.mult, ALU.mult)
                if stop == "dn_a":
                    continue
                S = mk.sb("dn_S", [128, 4, 128])
                mk._memset("dve", S[:], 0.0)
                GD = mk.sb("dn_gd", [64, 4, 64]); NM = mk.sb("dn_nm", [64, 4, 64]); Dm = mk.sb("dn_dm", [64, 4, 64])
                Ds = mk.sb("dn_ds", [64, 4, 64]); Di = mk.sb("dn_di", [64, 4, 64])
                AL = mk.sb("dn_al", [64, 4, 64]); AQ = mk.sb("dn_aq", [64, 4, 64]); AQT = mk.sb("dn_aqt", [64, 4, 64])
                BB = mk.sb("dn_bb", [64, 2, 4, 64]); X = mk.sb("dn_x", [64, 4, 64])
                vb = mk.sb("dn_vb", [64, 4, 128]); kbd = mk.sb("dn_kbd", [64, 4, 128]); kd = mk.sb("dn_kd", [64, 4, 128])
                U = mk.sb("dn_u", [64, 4, 128]); WT = mk.sb("dn_wt", [128, 4, 64]); VN = mk.sb("dn_vn", [64, 4, 128])
                O2 = mk.sb("dn_o2", [64, 4, 128]); Os = [mk.sb(f"dn_o{i}", [64, 4, 128]) for i in range(2)]
                P = k.PS
                id64 = k.ident[0:64, 0:64]
                orders = [cfg.order(0), cfg.order(1)]
                for s in range(nch):
                    info = []
                    for c in range(4):
                        d, hh = c // 2, c % 2
                        n = orders[d][s]
                        info.append((d, hh, n, n * 64, d * 4 + 2 * hp + hh))
                    for c, (d, hh, n, t0, col) in enumerate(info):
                        cc = slice(c * 64, (c + 1) * 64)
                        kT = Kt[hh][:, t0:t0 + 64]
                        mk.ts("pool", GD[:, c, :], k.masks[:, LE if d == 0 else GE, :], g[:, n, col:col + 1], None, op0=ALU.mult)
                        mk.mm(P[0][0:64, cc], k.ones[0:64, 0:64], GD[:, c, :])
                        mk.mm(P[0][0:64, 256 + c * 64:256 + (c + 1) * 64], kT, kT)
                        mk.mm(P[1][0:64, cc], Q[hh][:, t0:t0 + 64], kT)
                        mk.tr(P[3][0:64, c * 128:(c + 1) * 128], kT, k.ident[:])
                        mk.tr(P[4][0:64, c * 128:(c + 1) * 128], Vt[hh][:, t0:t0 + 64], k.ident[:])
                        mk.ts("dve", NM[:, c, :], P[0][0:64, cc], gam[:, n, col:col + 1], 0.0, op0=ALU.subtract, op1=ALU.max)
                    if stop == "dn_p1":
                        continue
                    mk.act(Dm[:], NM[:], AF.Exp, scale=-1.0)
                    for d in range(2):
                        dsl = slice(2 * d, 2 * d + 2)
                        mk.tt("dve", Ds[:, dsl, :], Dm[:, dsl, :], vbc(vus(k.masks[:, GT if d == 0 else LT, :], 1), [64, 2, 64]), ALU.mult)
                        mk.tt("dve", Di[:, dsl, :], Dm[:, dsl, :], vbc(vus(k.masks[:, GE if d == 0 else LE, :], 1), [64, 2, 64]), ALU.mult)
                    for c, (d, hh, n, t0, col) in enumerate(info):
                        mk.stt(AL[:, c, :], P[0][0:64, 256 + c * 64:256 + (c + 1) * 64], beta[:, n, col:col + 1], Ds[:, c, :], ALU.mult, ALU.mult)
                    mk.tt("dve", AQ[:], P[1][0:64, 0:256].rr("p (a b) -> p a b", a=4), Di[:], ALU.mult)
                    for c in range(4):
                        mk.mm(P[2][0:64, c * 64:(c + 1) * 64], AL[:, c, :], id64)
                        mk.mm(P[2][0:64, 256 + c * 64:256 + (c + 1) * 64], AQ[:, c, :], id64)
                    mk.ts("dve", BB[:, 0], P[2][0:64, 0:256].rr("p (a b) -> p a b", a=4), -1.0, None, op0=ALU.mult)
                    mk.ts("dve", BB[:, 1], AL[:], -1.0, None, op0=ALU.mult)
                    mk.cp("act", AQT[:], P[2][0:64, 256:512].rr("p (a b) -> p a b", a=4))
                    mk.tt("dve", X[:], BB[:, 0], vbc(vus(id64, 1), [64, 4, 64]), ALU.add)
                    if stop == "dn_p2":
                        continue
                    for lev in range(5):
                        for c in range(4):
                            mk.mm(P[5][0:64, c * 64:(c + 1) * 64], BB[:, 1, c, :], BB[:, 0, c, :])
                            mk.mm(P[5][0:64, 256 + c * 64:256 + (c + 1) * 64], BB[:, 0, c, :], BB[:, 1, c, :])
                        mk.cp("act" if lev % 2 else "dve", BB[:], P[5][0:64, :].rr("p (t a b) -> p t a b", t=2, a=4))
                        for c in range(4):
                            mk.mm(P[6][0:64, c * 64:(c + 1) * 64], BB[:, 1, c, :], X[:, c, :])
                        mk.tt("dve", X[:], X[:], P[6][0:64, 0:256].rr("p (a b) -> p a b", a=4), ALU.add)
                    for c, (d, hh, n, t0, col) in enumerate(info):
                        mk.act(vb[:, c, :], P[4][0:64, c * 128:(c + 1) * 128], AF.Identity, scale=beta[:, n, col:col + 1])
                        mk.ts("dve", kbd[:, c, :], P[3][0:64, c * 128:(c + 1) * 128], bg[:, n, col:col + 1], None, op0=ALU.mult)
                        mk.act(kd[:, c, :], P[3][0:64, c * 128:(c + 1) * 128], AF.Identity, scale=erg[:, n, col:col + 1])
                    for c in range(4):
                        mk.mm(P[6][0:64, c * 128:(c + 1) * 128], X[:, c, :], vb[:, c, :])
                        mk.mm(P[1][:, c * 64:(c + 1) * 64], kbd[:, c, :], X[:, c, :])
                    if stop == "dn_p3":
                        continue
                    mk.cp("act", U[:], P[6][0:64, :].rr("p (a b) -> p a b", a=4))
                    mk.cp("dve", WT[:], P[1][:, 0:256].rr("p (a b) -> p a b", a=4))
                    for c in range(4):
                        mk.mm(P[7][0:64, c * 128:(c + 1) * 128], WT[:, c, :], S[:, c, :])
                    mk.tt("dve", VN[:], U[:], P[7][0:64, :].rr("p (a b) -> p a b", a=4), ALU.subtract)
                    for c, (d, hh, n, t0, col) in enumerate(info):
                        mk.mm(P[3][0:64, c * 128:(c + 1) * 128], Q[hh][:, t0:t0 + 64], S[:, c, :])
                        mk.mm(P[4][0:64, c * 128:(c + 1) * 128], AQT[:, c, :], VN[:, c, :])
                        mk.mm(P[5][:, c * 128:(c + 1) * 128], kd[:, c, :], VN[:, c, :])
                    mk.cp("act", O2[:], P[4][0:64, :].rr("p (a b) -> p a b", a=4))
                    Ot = Os[s % 2]
                    for c, (d, hh, n, t0, col) in enumerate(info):
                        mk.stt(Ot[:, c, :], P[3][0:64, c * 128:(c + 1) * 128], eg[:, n, col:col + 1], O2[:, c, :], ALU.mult, ALU.add)
                        mk.stt(S[:, c, :], S[:, c, :], cdec[:, n, col:col + 1], P[5][:, c * 128:(c + 1) * 128], ALU.mult, ALU.add)
                    if stop == "dn_p4":
                        continue
                    for d in range(2):
                        t0 = orders[d][s] * 64
                        mk.dma(STQ, k.osc[b][d][t0:t0 + 64, hp * 256:(hp + 1) * 256], Ot[:, 2 * d:2 * d + 2, :].rr("p a b -> p (a b)"))


def stage_hg(k, l, b):
    mk, cfg = k.mk, k.cfg
    NT, C, nch = cfg.NT, cfg.C, cfg.nch
    LE, GE = 0, 1
    P = k.PS
    with scope(mk):
        rmask = mk.sb("hg_rm", [128, 2, NT], BF16)
        mk._memset("pool", rmask[:], 1.0)
        mk._memset("pool", rmask[:, 0, :].rr("p (n c) -> p n c", c=64)[:, :, 0:1], 0.0)
        mk._memset("pool", rmask[:, 1, :].rr("p (n c) -> p n c", c=64)[:, :, 63:64], 0.0)
        q = mk.sb("hg_q", [128, NT]); fr = mk.sb("hg_fr", [128, NT]); Kf = mk.sb("hg_kf", [128, NT])
        gc = mk.sb("hg_gc", [128, NT]); tmp = fr; QG = mk.sb("hg_qg", [128, NT])
        QTL = mk.sb("hg_qtl", [128, NT], BF16); KT4 = mk.sb("hg_kt4", [128, nch, 4, 64], BF16); KD = mk.sb("hg_kd", [128, NT])
        REF = mk.sb("hg_ref", [128, nch, 4]); EL = mk.sb("hg_el", [128, nch]); oml = mk.sb("hg_oml", [128, 1])
        S = mk.sb("hg_S", [128, 128])
        vt = [mk.sb(f"hg_vt{i}", [64, 128]) for i in range(2)]; vtb = [mk.sb(f"hg_vtb{i}", [64, 128], BF16) for i in range(2)]
        SCT = [mk.sb(f"hg_sct{i}", [64, 64], BF16) for i in range(2)]; kdt = [mk.sb(f"hg_kdt{i}", [64, 128], BF16) for i in range(2)]
        Os = [mk.sb(f"hg_o{i}", [64, 128]) for i in range(2)]
        v3 = lambda t: t[:].rr("p (n c) -> p n c", c=64)
        for h in range(4):
            mk.dma("sp", q[:], uT_rows(k, b, HGQ + h))
            mk.act(q[:], q[:], AF.Silu)
            for d in range(2):
                mk.dma("sp", fr[:], uT_rows(k, b, (HGFF if d == 0 else HGFB) + h))
                lbc = k.lb[:, d, l, h:h + 1]
                mk.ts("dve", oml[:], lbc, -1.0, 1.0, op0=ALU.mult, op1=ALU.add)
                mk.act(fr[:], fr[:], AF.Sigmoid)
                mk.ts("dve", fr[:], fr[:], oml[:, 0:1], lbc, op0=ALU.mult, op1=ALU.add)
                mk.ts("dve", Kf[:], fr[:], -1.0, 1.0, op0=ALU.mult, op1=ALU.add)
                mk.act(fr[:], fr[:], AF.Ln)
                if d == 0:
                    mk.I("dve", "tensor_tensor_scan", out=gc[:], data0=rmask[:, 0, :], data1=fr[:], initial=0.0, op0=ALU.mult, op1=ALU.add)
                else:
                    mk.I("dve", "tensor_tensor_scan", out=V(gc.h[:, ::-1], [gc.trk]), data0=V(rmask.h[:, 1, ::-1], [rmask.trk]),
                         data1=V(fr.h[:, ::-1], [fr.trk]), initial=0.0, op0=ALU.mult, op1=ALU.add)
                last = 63 if d == 0 else 0
                mk.act(tmp[:], gc[:], AF.Exp)
                mk.cp("dve", EL[:], v3(tmp)[:, :, last])
                mk.tt("dve", QG[:], q[:], tmp[:], ALU.mult)
                mk._memset("pool", REF[:], 0.0)
                if d == 0:
                    mk.cp("pool", REF[:, :, 1:4], v3(gc)[:, :, 15:63:16])
                else:
                    mk.cp("pool", REF[:, :, 0:3], v3(gc)[:, :, 16:64:16])
                g4 = gc[:].rr("p (n j r) -> p n j r", j=4, r=16)
                t4 = tmp[:].rr("p (n j r) -> p n j r", j=4, r=16)
                mk.tt("dve", t4, g4, vbc(vus(REF[:], 3), [128, nch, 4, 16]), ALU.subtract)
                mk.act(tmp[:], tmp[:], AF.Exp)
                mk.tt("dve", QTL[:], q[:], tmp[:], ALU.mult)
                for J in range(4):
                    mk.tt("dve", v3(tmp), vbc(REF[:, :, J:J + 1], [128, nch, 64]), v3(gc), ALU.subtract)
                    mk.ts("dve", tmp[:], tmp[:], 60.0, None, op0=ALU.min)
                    mk.act(tmp[:], tmp[:], AF.Exp)
                    mk.tt("dve", KT4[:, :, J, :], v3(tmp), v3(Kf), ALU.mult)
                mk.tt("dve", v3(tmp), vbc(v3(gc)[:, :, last:last + 1], [128, nch, 64]), v3(gc), ALU.subtract)
                mk.act(tmp[:], tmp[:], AF.Exp)
                mk.tt("dve", KD[:], tmp[:], Kf[:], ALU.mult)
                mk._memset("dve", S[:], 0.0)
                order = cfg.order(d)
                def pre(s):
                    n = order[s]; t0 = n * 64; a = s % 2
                    mk.dma("sp", vt[a][:], V(k.utok[b].h[t0:t0 + 64, TM_HGI + h * 128:TM_HGI + (h + 1) * 128], [k.utok[b].parts[t0 // 128]]))
                    mk.cp("pool", vtb[a][:], vt[a][:])
                    for J in range(4):
                        mk.mm(P[a][0:64, 16 * J:16 * J + 16], KT4[:, n, J, :], QTL[:, t0 + 16 * J:t0 + 16 * J + 16])
                    mk.tt("dve", SCT[a][:], P[a][0:64, 0:64], k.masks[:, LE if d == 0 else GE, :], ALU.mult)
                    mk.tr(P[2 + a][0:64, 0:128], KD[:, t0:t0 + 64], k.ident[:])
                    mk.cp("act", kdt[a][:], P[2 + a][0:64, 0:128])

                def main(s):
                    n = order[s]; t0 = n * 64; a = s % 2
                    mk.mm(P[4 + a][0:64, 0:128], QG[:, t0:t0 + 64], S[:], start=True, stop=False)
                    mk.mm(P[4 + a][0:64, 0:128], SCT[a][:], vtb[a][:], start=False, stop=True)
                    mk.cp("act", Os[a][:], P[4 + a][0:64, 0:128])
                    mk.dma(STQ, k.osc[b][2 + d][t0:t0 + 64, h * 128:(h + 1) * 128], Os[a][:])
                    mk.mm(P[6 + a][:, 0:128], kdt[a][:], vtb[a][:])
                    mk.stt(S[:], S[:], EL[:, n:n + 1], P[6 + a][:, 0:128], ALU.mult, ALU.add)

                pre(0)
                for s in range(len(order)):
                    if s + 1 < len(order):
                        pre(s + 1)
                    main(s)


def stage_post(k, l, b, which, nwT, gblk):
    mk, cfg = k.mk, k.cfg
    with scope(mk):
        nw = mk.sb("po_nw", [128, 1])
        mk.dma("sp", nw[:], nwT[l])
        of = [mk.sb(f"po_of{i}", [128, 512]) for i in range(2)]; ob = [mk.sb(f"po_ob{i}", [128, 512]) for i in range(2)]
        G = [mk.sb(f"po_g{i}", [128, 4, 128]) for i in range(2)]; Y = [mk.sb(f"po_y{i}", [128, 4, 128]) for i in range(2)]
        sq = mk.sb("po_sq", [128, 512]); ss = mk.sb("po_ss", [128, 4])
        uv = V(k.uT[b].ap().rearrange("(x p) t -> p x t", p=128), [k.uT[b].trk] + list(k.uT[b].parts.values()))
        yv = V(k.yT[b][which].ap().rearrange("(ft p) t -> p ft t", p=128), [k.yT[b][which].trk])
        for ti in range(cfg.ntile):
            s = ti % 2
            rows = slice(ti * 128, (ti + 1) * 128)
            mk.dma("sp", of[s][:], k.osc[b][2 * which][rows, :])
            mk.dma("sp", ob[s][:], k.osc[b][2 * which + 1][rows, :])
            mk.dma("sp", G[s][:], uv[:, gblk:gblk + 4, rows])
            o = of[s]
            mk.tt("dve", o[:], o[:], ob[s][:], ALU.add)
            mk.tt("pool", sq[:], o[:], o[:], ALU.mult)
            mk.I("dve", "tensor_reduce", out=ss[:], in_=sq[:].rr("p (h d) -> p h d", d=128), axis=AX.X, op=ALU.add)
            mk.act(ss[:], ss[:], AF.Sqrt, scale=1.0 / 128, bias=EPS)
            mk.I("dve", "reciprocal", out=ss[:], in_=ss[:])
            mk.tt("dve", o[:].rr("p (h d) -> p h d", d=128), o[:].rr("p (h d) -> p h d", d=128), vbc(vus(ss[:], 2), [128, 4, 128]), ALU.mult)
            ps = k.PS[s]
            for h in range(4):
                mk.tr(ps[:, h * 128:(h + 1) * 128], o[:, h * 128:(h + 1) * 128], k.ident[:])
            mk.act(G[s][:], G[s][:], AF.Silu)
            mk.stt(Y[s][:].rr("p a b -> p (a b)"), ps[:], nw[:, 0:1], G[s][:].rr("p a b -> p (a b)"), ALU.mult, ALU.mult)
            mk.dma(STQ, yv[:, :, rows], Y[s][:])


def load_cast(mk, dst_fn, src_fn, npieces, stage_tiles, semL):
    for pc in range(npieces):
        st = stage_tiles[pc % 2]
        sv = src_fn(pc)
        mk.dma("sp", V(st.h[tuple(slice(0, x) for x in sv.ap.shape)], [st.trk], st), sv)
        mk.cp("pool" if pc % 2 else "dve", dst_fn(pc), V(st.h[tuple(slice(0, x) for x in sv.ap.shape)], [st.trk], st))


def stage_s3a(k, l, need_ctx):
    mk, cfg = k.mk, k.cfg
    NB = cfg.NB
    with scope(mk):
        wbr = mk.sb("a_wbr", [128, 4, 4, D], BF16)
        wo = mk.sb("a_wo", [128, 8, D], BF16)
        with scope(mk):
            stg = [mk.sb(f"a_stg{i}", [128, 4, D]) for i in range(2)]
            load_cast(mk, lambda pc: wbr[:, pc], lambda pc: k.w_br[l, :, pc], 4, stg, 1)
            load_cast(mk, lambda pc: wo[:, pc * 4:(pc + 1) * 4, :], lambda pc: k.w_out[l, :, pc * 4:(pc + 1) * 4, :], 2, stg, 1)
        yt = [mk.sb(f"a_yt{i}", [128, 4, 512]) for i in range(2)]
        ytb = mk.sb("a_ytb", [128, 4, 4, 512], BF16)
        G = [mk.sb(f"a_g{i}", [128, 4, 512]) for i in range(2)]
        merged = mk.sb("a_mg", [128, 8, 512], BF16)
        macc = mk.sb("a_macc", [128, 512]); mt = mk.sb("a_mt", [128, 512])
        hg = [mk.sb(f"a_h{i}", [128, 8, 512]) for i in range(2)]
        it = 0
        ev = 0
        for b in range(NB):
            hv = hT_view(k, b)
            uv = V(k.uT[b].ap().rearrange("(x p) t -> p x t", p=128), [k.uT[b].trk] + list(k.uT[b].parts.values()))
            for gi, (t0, n, seg) in enumerate(cfg.groups):
                if seg == 0 and not need_ctx:
                    continue
                j = NB if seg == 0 else b
                h_ = hg[it % 2]
                it += 1
                mk.dma("sp", h_[:, :, :n], hv[:, :, t0:t0 + n])
                for bi in range(4):
                    y_ = yt[bi % 2]
                    yv = V(k.yT[b][bi].ap().rearrange("(ft p) t -> p ft t", p=128), [k.yT[b][bi].trk] + list(k.yT[b][bi].parts.values()))
                    mk.dma("sp", y_[:, :, :n], yv[:, :, t0:t0 + n])
                    mk.cp("pool" if bi % 2 else "dve", ytb[:, bi, :, :n], y_[:, :, :n])
                for c in range(8):
                    g_ = G[c % 2]
                    mk.dma("sp", g_[:, :, :n], uv[:, MGB + c:MGB + c + 25:8, t0:t0 + n])
                    mk.act(g_[:, :, :n], g_[:, :, :n], AF.Sigmoid)
                    for bi in range(4):
                        ps = k.PS[ev % 4]
                        ev += 1
                        for ft in range(4):
                            mk.mm(ps[:, :n], wbr[:, bi, ft, c * 128:(c + 1) * 128], ytb[:, bi, ft, :n], start=(ft == 0), stop=(ft == 3))
                        if bi == 0:
                            mk.tt("dve", macc[:, :n], ps[:, :n], g_[:, 0, :n], ALU.mult)
                        else:
                            mk.tt("dve", mt[:, :n], ps[:, :n], g_[:, bi, :n], ALU.mult)
                            mk.tt("pool", macc[:, :n] if bi < 3 else merged[:, c, :n], macc[:, :n], mt[:, :n], ALU.add)
                for c in range(8):
                    ps = k.PS[4 + c % 4]
                    for kt in range(8):
                        mk.mm(ps[:, :n], wo[:, kt, c * 128:(c + 1) * 128], merged[:, kt, :n], start=(kt == 0), stop=(kt == 7))
                    mk.stt(h_[:, c, :n], ps[:, :n], k.modT[:, 16 + c, j:j + 1], h_[:, c, :n], ALU.mult, ALU.add)
                mk.dma(STQ, hv[:, :, t0:t0 + n], h_[:, :, :n])


def stage_s3b(k, l, need_ctx):
    mk, cfg = k.mk, k.cfg
    NB = cfg.NB
    with scope(mk):
        w1b = mk.sb("b_w1", [128, 8, 4 * D], BF16)
        w2b = mk.sb("b_w2", [128, 32, D], BF16)
        with scope(mk):
            stg = [mk.sb(f"b_stg{i}", [128, 8, 512]) for i in range(2)]
            load_cast(mk, lambda pc: w1b[:, :, pc * 512:(pc + 1) * 512], lambda pc: k.w1[l, :, :, pc * 512:(pc + 1) * 512], 8, stg, 1)
            stg2 = [V(stg[i].h[:].rearrange("p a b -> p (a b)").rearrange("p (a b) -> p a b", a=4), [stg[i].trk], stg[i]) for i in range(2)]
            for pc in range(8):
                sv = k.w2[l, :, pc * 4:(pc + 1) * 4, :]
                mk.dma("sp", stg2[pc % 2], sv)
                mk.cp("pool" if pc % 2 else "dve", w2b[:, pc * 4:(pc + 1) * 4, :], stg2[pc % 2])
        G = 256
        hg = [mk.sb(f"b_h{i}", [128, 8, G]) for i in range(2)]
        sq = mk.sb("b_sq", [128, 8, G]); tmp = mk.sb("b_tmp", [128, 8, G]); rs = mk.sb("b_rs", [128, G])
        z2 = mk.sb("b_z2", [128, 8, G], BF16)
        hid = mk.sb("b_hid", [128, 32, G], BF16)
        rl = [mk.sb(f"b_rl{i}", [128, G]) for i in range(2)]
        it = 0
        ev = 0
        for b in range(NB):
            hv = hT_view(k, b)
            subs = []
            for (t0, n, seg) in cfg.groups:
                if seg == 0 and not need_ctx:
                    continue
                for o in range(0, n, G):
                    subs.append((t0 + o, min(G, n - o), seg))
            for (t0, n, seg) in subs:
                j = NB if seg == 0 else b
                h_ = hg[it % 2]
                it += 1
                mk.dma("sp", h_[:, :, :n], hv[:, :, t0:t0 + n])
                rms_mod(k, h_, n, j, k.A2, 24, lambda kt: z2[:, kt, :n], tmp, sq, rs, k.PS[7])
                for fc in range(32):
                    ps = k.PS[ev % 4]
                    r_ = rl[ev % 2]
                    ev += 1
                    for kt in range(8):
                        mk.mm(ps[:, :n], w1b[:, kt, fc * 128:(fc + 1) * 128], z2[:, kt, :n], start=(kt == 0), stop=(kt == 7))
                    mk.act(r_[:, :n], ps[:, :n], AF.Relu)
                    mk.tt("pool" if ev % 2 else "dve", hid[:, fc, :n], r_[:, :n], r_[:, :n], ALU.mult)
                for c in range(8):
                    ps = k.PS[4 + c % 2]
                    for ft in range(32):
                        mk.mm(ps[:, :n], w2b[:, ft, c * 128:(c + 1) * 128], hid[:, ft, :n], start=(ft == 0), stop=(ft == 31))
                    mk.stt(h_[:, c, :n], ps[:, :n], k.modT[:, 40 + c, j:j + 1], h_[:, c, :n], ALU.mult, ALU.add)
                mk.dma(STQ, hv[:, :, t0:t0 + n], h_[:, :, :n])


_CACHE = {}


def host_consts(cfg):
    i = np.arange(64)
    kk, ff = i[:, None], i[None, :]
    masks = np.stack([(kk <= ff), (kk >= ff), (kk > ff), (kk < ff)], axis=1).astype(np.float32)
    rows = cfg.T // 64
    row_id = np.repeat(np.arange(rows), 64).astype(np.float32)
    col_id = np.tile(np.arange(64), rows).astype(np.float32)
    inv = (10000.0 ** (-np.arange(0, 32, 2, dtype=np.float32) / 32.0)).astype(np.float32)
    ang = np.stack([row_id[:, None] * inv, col_id[:, None] * inv], axis=1).astype(np.float32)
    rope = np.stack([np.cos(ang), np.sin(ang)], axis=1).astype(np.float32)
    return {"c_ident": np.eye(128, dtype=np.float32), "c_masks": np.ascontiguousarray(masks), "c_rope": np.ascontiguousarray(rope)}


def fm_cols():
    offs = np.cumsum([0, 512, 512, 512, 512, 8, 8, 512, 512, 512, 512, 512, 512, 512, 512, 128, 128, 4096])
    (dq, dk, dv, dg, dbeta, dalpha, hq, hff, hfb, hi, hgate, lx, lg, aq, ak, av, mg) = offs[:17]
    fm = []
    for base in (dq, dk, dv, dg, hq, hff, hfb, hgate, lx, lg):
        fm.append(np.arange(base, base + 512))
    fm.append(np.arange(mg, mg + 4096))
    fm = np.concatenate(fm)
    tm = np.concatenate([np.arange(hi, hi + 512), np.arange(aq, aq + 512), np.arange(ak, ak + 128), np.arange(av, av + 128),
                         np.arange(dbeta, dbeta + 8), np.arange(dalpha, dalpha + 8)])
    assert fm.size == NFM * 128 and tm.size == NTM
    return fm, tm


def pk(w):
    r = w.shape[0] // 128
    return np.ascontiguousarray(w.reshape(r, 128, *w.shape[1:]).swapaxes(0, 1))


def host_weights(inp, L):
    f = np.float32
    fm, tm = fm_cols()
    W = {}
    W["mod_w"] = np.stack([pk(inp["mod_w"][l]) for l in range(L)])
    W["mod_b"] = np.stack([np.ascontiguousarray(inp["mod_b"][l].reshape(48, 128).T) for l in range(L)])
    W["n1w"] = np.stack([np.ascontiguousarray(inp["norm1_w"][l].reshape(8, 128).T) for l in range(L)])
    W["n2w"] = np.stack([np.ascontiguousarray(inp["norm2_w"][l].reshape(8, 128).T) for l in range(L)])
    wfm = []
    for l in range(L):
        w = inp["w_in"][l][:, fm]
        w = w.reshape(8, 128, NFM, 128).transpose(2, 1, 0, 3)
        wfm.append(np.ascontiguousarray(w))
    W["w_fm"] = np.stack(wfm)
    W["w_tm"] = np.stack([pk(np.ascontiguousarray(inp["w_in"][l][:, tm])) for l in range(L)])
    W["dn_cw"] = np.stack([np.ascontiguousarray(inp["dn_conv_w"][l].reshape(4, 12, 128).transpose(2, 1, 0)) for l in range(L)])
    W["dn_ab"] = np.stack([np.broadcast_to(np.stack([inp["dn_a_log"][l].reshape(8), inp["dn_dt_bias"][l].reshape(8)])[None], (64, 2, 8)).copy() for l in range(L)])
    W["dn_nw"] = np.ascontiguousarray(inp["dn_norm_w"][:L].reshape(L, 128, 1))
    W["hg_nw"] = np.ascontiguousarray(inp["hg_norm_w"][:L].reshape(L, 128, 1))
    W["hg_lb"] = np.ascontiguousarray(inp["hg_lower_bounds"][:, :L].reshape(2, L, 4, 128).transpose(3, 0, 1, 2))
    W["lru_cw"] = np.stack([np.ascontiguousarray(inp["lru_conv_w"][l].reshape(4, 4, 128).transpose(2, 1, 0)) for l in range(L)])
    W["lru_cb"] = np.stack([np.ascontiguousarray(inp["lru_conv_b"][l].reshape(4, 128).T) for l in range(L)])
    W["lru_wa"] = np.stack([np.ascontiguousarray(inp["lru_w_a"][l].transpose(2, 0, 1, 3)) for l in range(L)])
    W["lru_wx"] = np.stack([np.ascontiguousarray(inp["lru_w_x"][l].transpose(2, 0, 1, 3)) for l in range(L)])
    W["lru_v"] = np.stack([np.ascontiguousarray(np.stack([inp["lru_b_a"][l], inp["lru_b_x"][l], inp["lru_lambda"][l]]).reshape(3, 2, 4, 128).transpose(3, 0, 1, 2)) for l in range(L)])
    W["att_nw"] = np.stack([np.broadcast_to(np.stack([inp["att_q_norm_w"][l], inp["att_k_norm_w"][l]])[None], (128, 2, 64)).copy() for l in range(L)])
    W["w_br"] = np.stack([np.ascontiguousarray(inp["w_branch"][l].reshape(4, 4, 128, D).transpose(2, 0, 1, 3)) for l in range(L)])
    W["w_out"] = np.stack([pk(inp["w_out"][l]) for l in range(L)])
    W["w1"] = np.stack([pk(inp["mlp_w1"][l]) for l in range(L)])
    W["w2"] = np.stack([pk(inp["mlp_w2"][l]) for l in range(L)])
    return {k_: np.ascontiguousarray(v, dtype=f) for k_, v in W.items()}


def run(inp, cfg, n_cores):
    key = (cfg.T, cfg.C, cfg.NB, cfg.L, cfg.debug)
    if key not in _CACHE:
        _CACHE[key] = build(cfg)
    nc, mk = _CACHE[key]
    inp = {k_: np.asarray(v, dtype=np.float32) for k_, v in inp.items()}
    W = host_weights(inp, cfg.L)
    W.update(host_consts(cfg))
    in_maps = []
    NB = cfg.NB
    for ci in range(n_cores):
        bs = slice(ci * NB, (ci + 1) * NB)
        m = dict(W)
        m["x_in"] = np.ascontiguousarray(np.concatenate([inp["ctx"][bs], inp["x"][bs]], axis=1))
        cv = np.concatenate([inp["c"][bs], inp["c_ctx"][None]], axis=0)
        m["cT"] = np.ascontiguousarray(cv.T.reshape(8, 128, NB + 1).transpose(1, 0, 2))
        in_maps.append(m)
    res = run_bass_kernel_spmd(nc, in_maps, core_ids=list(range(n_cores)))
    return res


def kernel(**inputs):
    cfg = Cfg(T=4096, C=256, NB=2, L=4)
    res = run(inputs, cfg, 8)
    return np.concatenate([r["out"] for r in res.results], axis=0).astype(np.float32)
```

```python
import numpy as np
import concourse.bass as bass
import concourse.mybir as mybir
from concourse.bass_utils import run_bass_kernel_spmd

F32 = mybir.dt.float32
BF16 = mybir.dt.bfloat16
AF = mybir.ActivationFunctionType
ALU = mybir.AluOpType
AX = mybir.AxisListType

SAME_ENGINE_SYNC = True


class StopBuild(Exception):
    pass


class Trk:
    __slots__ = ("w", "r_eng", "r_dma", "excl")

    def __init__(self):
        self.excl = False
        self.w = None
        self.r_eng = {}
        self.r_dma = {}


class V:
    __slots__ = ("ap", "trks", "t")

    def __init__(self, ap, trks, t=None):
        self.ap = ap
        self.trks = trks
        self.t = t

    def __getitem__(self, idx):
        return V(self.ap[idx], self.trks, self.t)

    def rr(self, pat, **kw):
        return V(self.ap.rearrange(pat, **kw), self.trks, self.t)

    def bc(self, shape):
        return V(self.ap.broadcast_to(shape) if hasattr(self.ap, "broadcast_to") else self.ap.to_broadcast(shape), self.trks)


class T:
    def __init__(self, handle, name, is_dram=True):
        self.h = handle
        self.name = name
        self.trk = Trk()
        self.parts = {}
        self.is_dram = is_dram
        self.dsem = None

    def ap(self):
        h = self.h
        return h.ap() if hasattr(h, "ap") else h[:]

    def __getitem__(self, idx):
        return V(self.h[idx], [self.trk], self)

    def p(self, key, idx=None):
        if key not in self.parts:
            self.parts[key] = Trk()
        a = self.h[idx] if idx is not None else self.ap()
        return V(a, [self.parts[key]], self)


class MK:
    def __init__(self, nc, es):
        self.nc = nc
        self.es = es
        self.E = {"pe": nc.tensor, "act": nc.scalar, "dve": nc.vector, "pool": nc.gpsimd, "sp": nc.sync}
        self.esem = {}
        self.cnt = {}
        self.seen = {}
        self.clock = {}
        for e in ("pe", "act", "dve", "pool"):
            self.esem[e] = es.enter_context(nc.semaphore("es_" + e))
            self.cnt[e] = 0
            self.clock[e] = [None]
        for e in self.E:
            self.seen[e] = {}
        self.dsem = {}
        self.dcnt = {}
        self.nwait = 0
        self.ninst = 0
        self.sem_es = es
        self.sem_ptr = 0
        self.limit = None

    def sb(self, name, shape, dt=F32):
        self.uid = getattr(self, "uid", 0) + 1
        h = self.es.enter_context(self.nc.sbuf_tensor("%s_u%d" % (name, self.uid), list(shape), dt))
        return T(h, name, False)

    def ps(self, name, shape, dt=F32):
        h = self.es.enter_context(self.nc.psum_tensor(name, list(shape), dt))
        t = T(h, name, False)
        t.trk.excl = True
        return t

    def dram(self, name, shape, dt=F32, kind="Internal"):
        h = self.nc.dram_tensor(name, list(shape), dt, kind=kind)
        return T(h, name)

    def get_dsem(self, name):
        if name not in self.dsem:
            self.dsem[name] = self.sem_es.enter_context(self.nc.semaphore("ds_" + name))
            self.dcnt[name] = 0
        return self.dsem[name]

    def _wait_event(self, eng, ev):
        if ev is None:
            return
        seen = self.seen[eng]
        if ev[0] == "e":
            _, e2, k = ev
            if e2 == eng and (eng == "pe" or not SAME_ENGINE_SYNC):
                return
            if seen.get(e2, 0) >= k:
                return
            self.E[eng].wait_ge(self.esem[e2], k)
            self.nwait += 1
            seen[e2] = k
            clk = self.clock[e2][k]
            if clk:
                for kk, vv in clk.items():
                    if seen.get(kk, 0) < vv:
                        seen[kk] = vv
        else:
            _, sname, val = ev
            key = "d:" + sname
            if seen.get(key, 0) >= val:
                return
            self.E[eng].wait_ge(self.dsem[sname], val)
            self.nwait += 1
            seen[key] = val

    def _deps(self, eng, outs, ins):
        for v in ins:
            for t in v.trks:
                self._wait_event(eng, t.w)
                if t.excl:
                    for e2, k in t.r_eng.items():
                        if e2 != eng:
                            self._wait_event(eng, ("e", e2, k))
        for v in outs:
            for t in v.trks:
                self._wait_event(eng, t.w)
                for e2, k in t.r_eng.items():
                    self._wait_event(eng, ("e", e2, k))
                for s, val in t.r_dma.items():
                    self._wait_event(eng, ("d", s, val))

    def I(self, eng, meth, **kw):
        outs, ins = [], []
        args = {}
        for k, v in kw.items():
            if isinstance(v, V):
                (outs if k in ("out", "accum_out") else ins).append(v)
                args[k] = v.ap
            else:
                args[k] = v
        if self.limit is not None and self.ninst >= self.limit:
            return None
        self._deps(eng, outs, ins)
        inst = getattr(self.E[eng], meth)(**args)
        self.cnt[eng] += 1
        k = self.cnt[eng]
        inst.then_inc(self.esem[eng], 1)
        self.ninst += 1
        snap = dict(self.seen[eng])
        self.clock[eng].append(snap)
        ev = ("e", eng, k)
        for v in outs:
            for t in v.trks:
                t.w = ev
                t.r_eng = {}
                t.r_dma = {}
        for v in ins:
            for t in v.trks:
                if t.r_eng.get(eng, 0) < k:
                    t.r_eng[eng] = k
        return inst

    def dma(self, q, out, in_, sem=None, **kw):
        if sem is None:
            st = out.t if (out.t is not None and not out.t.is_dram) else in_.t
            if st.dsem is None:
                st.dsem = "D%d" % self.sem_ptr
                self.sem_ptr += 1
            sem = st.dsem
        if self.limit is not None and self.ninst >= self.limit:
            return None
        s = self.get_dsem(sem)
        self._deps(q, [out], [in_])
        inst = self.E[q].dma_start(out=out.ap, in_=in_.ap, **kw)
        self.dcnt[sem] += 16
        val = self.dcnt[sem]
        inst.then_inc(s, 16)
        self.ninst += 1
        ev = ("d", sem, val)
        for t in out.trks:
            t.w = ev
            t.r_eng = {}
            t.r_dma = {}
        for t in in_.trks:
            if t.r_dma.get(sem, 0) < val:
                t.r_dma[sem] = val
        return inst

    def barrier(self):
        for eng in self.E:
            for e2 in self.esem:
                if self.cnt[e2] > 0:
                    self._wait_event(eng, ("e", e2, self.cnt[e2]))
            for s, val in self.dcnt.items():
                if val > 0:
                    self._wait_event(eng, ("d", s, val))

    def finish(self, eng="sp"):
        for s, val in self.dcnt.items():
            if val > 0:
                self._wait_event(eng, ("d", s, val))

    def mm(self, out, lhsT, rhs, start=True, stop=True, **kw):
        return self.I("pe", "matmul", out=out, lhsT=lhsT, rhs=rhs, start=start, stop=stop, **kw)

    def tr(self, out, in_, identity):
        return self.I("pe", "transpose", out=out, in_=in_, identity=identity)

    def act(self, out, in_, func, eng="act", **kw):
        return self.I("act", "activation", out=out, in_=in_, func=func, **kw)

    def ts(self, eng, out, in0, s1, s2=None, op0=ALU.mult, op1=None, **kw):
        if op1 is None:
            return self.I(eng, "tensor_scalar", out=out, in0=in0, scalar1=s1, scalar2=s2, op0=op0, **kw)
        return self.I(eng, "tensor_scalar", out=out, in0=in0, scalar1=s1, scalar2=s2, op0=op0, op1=op1, **kw)

    def tt(self, eng, out, in0, in1, op):
        return self.I(eng, "tensor_tensor", out=out, in0=in0, in1=in1, op=op)

    def stt(self, out, in0, scalar, in1, op0, op1, eng="dve"):
        return self.I(eng, "scalar_tensor_tensor", out=out, in0=in0, scalar=scalar, in1=in1, op0=op0, op1=op1)

    def cp(self, eng, out, in_):
        if eng == "act":
            return self.I("act", "copy", out=out, in_=in_)
        return self.I(eng, "tensor_copy", out=out, in_=in_)

    def memset(self, eng, out, val):
        return self.I(eng, "memset", ap=None, out=out, constant=val) if False else self._memset(eng, out, val)

    def _memset(self, eng, out, val):
        if self.limit is not None and self.ninst >= self.limit:
            return None
        self._deps(eng, [out], [])
        inst = self.E[eng].memset(out.ap, val)
        self.cnt[eng] += 1
        k = self.cnt[eng]
        inst.then_inc(self.esem[eng], 1)
        self.ninst += 1
        self.clock[eng].append(dict(self.seen[eng]))
        for t in out.trks:
            t.w = ("e", eng, k)
            t.r_eng = {}
            t.r_dma = {}
        return inst

from contextlib import ExitStack, contextmanager

EPS = 1e-6
STQ = "pool"
D = 1024
NFM = 72
NTM = 1296
DNQ, DNK, DNV, DNG, HGQ, HGFF, HGFB, HGG, LRX, LRG, MGB = 0, 4, 8, 12, 16, 20, 24, 28, 32, 36, 40
TM_HGI, TM_ATQ, TM_ATK, TM_ATV, TM_BETA, TM_ALPHA = 0, 512, 1024, 1152, 1280, 1288


def vus(v, axis):
    return V(v.ap.unsqueeze(axis), v.trks)


def vbc(v, shape):
    return V(v.ap.broadcast_to(list(shape)), v.trks)


class Cfg:
    def __init__(self, T=4096, C=256, NB=2, L=4, debug=False):
        self.T, self.C, self.NB, self.L = T, C, NB, L
        self.NT = T + C
        self.debug = debug
        self.groups = []
        t = 0
        while t < C:
            n = min(512, C - t)
            self.groups.append((t, n, 0))
            t += n
        while t < self.NT:
            n = min(512, self.NT - t)
            self.groups.append((t, n, 1))
            t += n
        self.ntile = self.NT // 128
        self.nch = self.NT // 64
        self.cch = C // 64

    def gi_of(self, t):
        for i, (t0, n, s) in enumerate(self.groups):
            if t0 <= t < t0 + n:
                return i
        raise ValueError

    def order(self, d):
        if d == 0:
            return list(range(self.nch))
        return list(range(self.cch - 1, -1, -1)) + list(range(self.nch - 1, self.cch - 1, -1))


class K:
    pass


@contextmanager
def scope(mk):
    old = mk.es
    ptr = mk.sem_ptr
    with ExitStack() as es:
        mk.es = es
        try:
            yield es
        finally:
            lim, mk.limit = mk.limit, None
            mk.barrier()
            mk.limit = lim
            mk.es = old
            mk.sem_ptr = ptr


def build(cfg):
    nc = bass.Bass("TRN2", target_bir_lowering=False)
    L, NB, NT, C = cfg.L, cfg.NB, cfg.NT, cfg.C
    root = ExitStack()
    mk = MK(nc, root)
    k = K()
    k.cfg, k.mk, k.nc = cfg, mk, nc
    T_ = T

    def din(name, shape):
        return T_(nc.dram_tensor(name, list(shape), F32, kind="ExternalInput"), name)

    k.x_in = din("x_in", [NB, NT, D])
    k.cT = din("cT", [128, 8, NB + 1])
    k.mod_w = din("mod_w", [L, 128, 8, 6 * D])
    k.mod_b = din("mod_b", [L, 128, 48])
    k.n1w = din("n1w", [L, 128, 8])
    k.n2w = din("n2w", [L, 128, 8])
    k.w_fm = din("w_fm", [L, NFM, 128, 8, 128])
    k.w_tm = din("w_tm", [L, 128, 8, NTM])
    k.dn_cw = din("dn_cw", [L, 128, 12, 4])
    k.dn_ab = din("dn_ab", [L, 64, 2, 8])
    k.dn_nw = din("dn_nw", [L, 128, 1])
    k.hg_nw = din("hg_nw", [L, 128, 1])
    k.hg_lb = din("hg_lb", [128, 2, L, 4])
    k.lru_cw = din("lru_cw", [L, 128, 4, 4])
    k.lru_cb = din("lru_cb", [L, 128, 4])
    k.lru_wa = din("lru_wa", [L, 128, 2, 4, 128])
    k.lru_wx = din("lru_wx", [L, 128, 2, 4, 128])
    k.lru_v = din("lru_v", [L, 128, 3, 2, 4])
    k.att_nw = din("att_nw", [L, 128, 2, 64])
    k.w_br = din("w_br", [L, 128, 4, 4, D])
    k.w_out = din("w_out", [L, 128, 8, D])
    k.w1 = din("w1", [L, 128, 8, 4 * D])
    k.w2 = din("w2", [L, 128, 32, D])
    k.c_ident = din("c_ident", [128, 128])
    k.c_masks = din("c_masks", [64, 4, 64])
    k.c_rope = din("c_rope", [cfg.T, 2, 2, 16])
    k.out = T_(nc.dram_tensor("out", [NB, cfg.T, D], F32, kind="ExternalOutput"), "out")

    kind = "ExternalOutput" if cfg.debug else "Internal"

    def dsc(name, shape):
        return T_(nc.dram_tensor(name, list(shape), F32, kind=kind), name)

    k.hT = [dsc(f"hT{b}", [D, NT]) for b in range(NB)]
    k.uT = [dsc(f"uT{b}", [NFM * 128, NT]) for b in range(NB)]
    k.utok = [dsc(f"utok{b}", [NT, NTM]) for b in range(NB)]
    k.osc = [[dsc(f"osc{b}_{i}", [NT, 512]) for i in range(4)] for b in range(NB)]
    k.yT = [[dsc(f"yT{b}_{i}", [512, NT]) for i in range(4)] for b in range(NB)]

    k.PS = [mk.ps(f"psb{i}", [128, 512]) for i in range(8)]
    k.ident = mk.sb("ident", [128, 128])
    k.identb = mk.sb("identb", [128, 128], BF16)
    k.ones = mk.sb("ones", [128, 128])
    k.masks = mk.sb("masks", [64, 4, 64])
    k.scT = mk.sb("scT", [128, 8, NB + 1])
    k.modT = mk.sb("modT", [128, 48, NB + 1])
    k.A1 = mk.sb("A1", [128, 8, NB + 1])
    k.A2 = mk.sb("A2", [128, 8, NB + 1])
    k.lb = mk.sb("lb", [128, 2, L, 4])
    mk.dma("sp", k.ident[:], k.c_ident[:, :])
    mk.dma("sp", k.masks[:], k.c_masks[:, :, :])
    mk.dma("sp", k.scT[:], k.cT[:, :, :])
    mk.dma("sp", k.lb[:], k.hg_lb[:, :, :, :])
    mk._memset("dve", k.ones[:], 1.0)
    mk.cp("dve", k.identb[:], k.ident[:])
    mk.act(k.scT[:], k.scT[:], AF.Silu)
    prep_lb(k)

    mk.limit = getattr(cfg, "limit", None)
    try:
        _stages(k, cfg, mk, L, NB)
    except StopBuild:
        pass
    mk.limit = None
    mk.barrier()
    root.close()
    return nc, mk


def _stages(k, cfg, mk, L, NB):
    stage_in(k)
    for l in range(L):
        need_ctx = l < L - 1
        stage_mods(k, l)
        for b in range(NB):
            stage_s1(k, l, b)
        if cfg.debug == "s1":
            break
        only = getattr(cfg, "only", None)
        for b in range(NB):
            if only is None or "lru" in only:
                stage_lru(k, l, b)
            if only is None or "att" in only:
                stage_att(k, l, b, need_ctx)
            if only is None or "dn" in only:
                stage_dn(k, l, b)
                if not getattr(cfg, "stop", ""):
                    stage_post(k, l, b, 0, k.dn_nw, DNG)
            if only is None or "hg" in only:
                stage_hg(k, l, b)
                stage_post(k, l, b, 1, k.hg_nw, HGG)
        if cfg.debug == "mix":
            break
        stage_s3a(k, l, need_ctx)
        stage_s3b(k, l, need_ctx)
    stage_out(k)


def prep_lb(k):
    mk, L = k.mk, k.cfg.L
    with scope(mk):
        e = mk.sb("lb_e", [128, 2, L, 4])
        s = mk.sb("lb_s", [128, 2, 4])
        mk.act(e[:], k.lb[:], AF.Exp)
        mk.cp("dve", s[:], e[:, :, 0, :])
        for l in range(1, L):
            mk.tt("dve", s[:], s[:], e[:, :, l, :], ALU.add)
        mk.I("dve", "reciprocal", out=s[:], in_=s[:])
        mk.tt("dve", e[:], e[:], vbc(vus(s[:], 2), [128, 2, L, 4]), ALU.mult)
        mk._memset("dve", k.lb[:, :, 0, :], 0.0)
        for l in range(1, L):
            mk.tt("dve", k.lb[:, :, l, :], k.lb[:, :, l - 1, :], e[:, :, l, :], ALU.add)


def hT_view(k, b):
    return V(k.hT[b].ap().rearrange("(kt p) t -> p kt t", p=128), [k.hT[b].trk])


def stage_in(k):
    mk, cfg = k.mk, k.cfg
    with scope(mk):
        xt = [mk.sb(f"in_x{i}", [128, D]) for i in range(2)]
        st = [mk.sb(f"in_s{i}", [128, 8, 128]) for i in range(2)]
        i = 0
        for b in range(cfg.NB):
            hv = hT_view(k, b)
            for ti in range(cfg.ntile):
                x, s = xt[i % 2], st[i % 2]
                mk.dma("sp", x[:], k.x_in[b, ti * 128:(ti + 1) * 128, :])
                for half in range(2):
                    ps = k.PS[half]
                    for q in range(4):
                        mk.tr(ps[:, q * 128:(q + 1) * 128], x[:, (half * 4 + q) * 128:(half * 4 + q + 1) * 128], k.ident[:])
                    mk.cp("act" if half else "dve", s[:, half * 4:half * 4 + 4, :], ps[:].rr("p (a b) -> p a b", a=4))
                mk.dma(STQ, hv[:, :, ti * 128:(ti + 1) * 128], s[:])
                i += 1


def stage_out(k):
    mk, cfg = k.mk, k.cfg
    with scope(mk):
        ht = [mk.sb(f"out_h{i}", [128, 8, 128]) for i in range(2)]
        st = [mk.sb(f"out_s{i}", [128, D]) for i in range(2)]
        i = 0
        for b in range(cfg.NB):
            hv = hT_view(k, b)
            for ti in range(cfg.T // 128):
                h, s = ht[i % 2], st[i % 2]
                t0 = cfg.C + ti * 128
                mk.dma("sp", h[:], hv[:, :, t0:t0 + 128])
                for half in range(2):
                    ps = k.PS[half]
                    for q in range(4):
                        mk.tr(ps[:, q * 128:(q + 1) * 128], h[:, half * 4 + q, :], k.ident[:])
                    mk.cp("act" if half else "dve", s[:, half * 512:(half + 1) * 512], ps[:])
                mk.dma(STQ, k.out[b, ti * 128:(ti + 1) * 128, :], s[:])
                i += 1


def stage_mods(k, l):
    mk, cfg = k.mk, k.cfg
    NJ = cfg.NB + 1
    with scope(mk):
        wp = [mk.sb(f"mod_w{i}", [128, 8, 512]) for i in range(2)]
        mb = mk.sb("mod_bt", [128, 48])
        nw = mk.sb("mod_nw", [128, 2, 8])
        mk.dma("sp", mb[:], k.mod_b[l])
        mk.dma("sp", nw[:, 0, :], k.n1w[l])
        mk.dma("sp", nw[:, 1, :], k.n2w[l])
        pm = k.PS[0]
        for pc in range(12):
            w = wp[pc % 2]
            mk.dma("sp", w[:], k.mod_w[l, :, :, pc * 512:(pc + 1) * 512])
            for j in range(4):
                o = (pc * 4 + j) * 4
                for kt in range(8):
                    mk.mm(pm[:, o:o + NJ], w[:, kt, j * 128:(j + 1) * 128], k.scT[:, kt, :], start=(kt == 0), stop=(kt == 7))
        mk.tt("dve", k.modT[:], pm[:, 0:192].rr("p (a b) -> p a b", b=4)[:, :, 0:NJ], vbc(vus(mb[:], 2), [128, 48, NJ]), ALU.add)
        for (A, w_i, sc0) in ((k.A1, 0, 8), (k.A2, 1, 32)):
            mk.ts("dve", A[:], k.modT[:, sc0:sc0 + 8, :], 1.0, None, op0=ALU.add)
            mk.tt("dve", A[:], A[:], vbc(vus(nw[:, w_i, :], 2), [128, 8, NJ]), ALU.mult)


def rms_mod(k, hg, n, j, A, sh0, zt_out, tmp, sq, rs, ps):
    mk = k.mk
    mk.act(sq[:, :, :n], hg[:, :, :n], AF.Square)
    for kt in range(8):
        mk.mm(ps[:, :n], k.ones[:], sq[:, kt, :n], start=(kt == 0), stop=(kt == 7))
    mk.act(rs[:, :n], ps[:, :n], AF.Sqrt, scale=1.0 / D, bias=EPS)
    mk.I("dve", "reciprocal", out=rs[:, :n], in_=rs[:, :n])
    mk.tt("dve", tmp[:, :, :n], hg[:, :, :n], vbc(vus(rs[:, :n], 1), [128, 8, n]), ALU.mult)
    for kt in range(8):
        if kt % 2 == 0:
            mk.ts("dve", zt_out(kt), tmp[:, kt, :n], A[:, kt, j:j + 1], k.modT[:, sh0 + kt, j:j + 1], op0=ALU.mult, op1=ALU.add)
        else:
            mk.act(zt_out(kt), tmp[:, kt, :n], AF.Identity, scale=A[:, kt, j:j + 1], bias=k.modT[:, sh0 + kt, j:j + 1])


def stage_s1(k, l, b):
    mk, cfg = k.mk, k.cfg
    NT = cfg.NT
    hv = hT_view(k, b)
    with scope(mk):
        ZT = mk.sb("s1_zt", [128, 8, NT], BF16)
        with scope(mk):
            hg = [mk.sb(f"s1_h{i}", [128, 8, 512]) for i in range(2)]
            sq = mk.sb("s1_sq", [128, 8, 512])
            tmp = mk.sb("s1_tmp", [128, 8, 512])
            rs = mk.sb("s1_rs", [128, 512])
            for gi, (t0, n, seg) in enumerate(cfg.groups):
                h = hg[gi % 2]
                mk.dma("sp", h[:, :, :n], hv[:, :, t0:t0 + n])
                j = cfg.NB if seg == 0 else b
                rms_mod(k, h, n, j, k.A1, 0, lambda kt: ZT.p(gi, (slice(None), kt, slice(t0, t0 + n))), tmp, sq, rs, k.PS[gi % 2])
        with scope(mk):
            wf = [mk.sb(f"s1_wf{i}", [128, 8, 128]) for i in range(2)]
            wb = [mk.sb(f"s1_wb{i}", [128, 8, 128], BF16) for i in range(2)]
            stg = [mk.sb(f"s1_st{i}", [128, NT]) for i in range(2)]
            ev = 0
            for blk in range(NFM):
                s = blk % 2
                mk.dma("sp", wf[s][:], k.w_fm[l, blk])
                mk.cp("pool", wb[s][:], wf[s][:])
                for gi, (t0, n, seg) in enumerate(cfg.groups):
                    ps = k.PS[ev % 4]
                    for kt in range(8):
                        mk.mm(ps[:, :n], wb[s][:, kt, :], ZT.p(gi, (slice(None), kt, slice(t0, t0 + n))), start=(kt == 0), stop=(kt == 7))
                    mk.cp("act" if ev % 2 else "dve", stg[s].p(gi, (slice(None), slice(t0, t0 + n))), ps[:, :n])
                    ev += 1
                allp = V(stg[s].ap(), [stg[s].parts[g] for g in range(len(cfg.groups))], stg[s])
                mk.dma(STQ, k.uT[b].p(blk, (slice(blk * 128, (blk + 1) * 128), slice(None))), allp)
        with scope(mk):
            wtm = mk.sb("s1_wtm", [128, 8, NTM], BF16)
            wst = [mk.sb(f"s1_wst{i}", [128, 8, 432]) for i in range(2)]
            stt_ = [mk.sb(f"s1_stt{i}", [128, NTM]) for i in range(2)]
            for pc in range(3):
                mk.dma("sp", wst[pc % 2][:], k.w_tm[l, :, :, pc * 432:(pc + 1) * 432])
                mk.cp("pool", wtm.p(pc, (slice(None), slice(None), slice(pc * 432, (pc + 1) * 432))), wst[pc % 2][:])
            wall = lambda kt, c0, cn: V(wtm.h[:, kt, c0:c0 + cn], [wtm.parts[p] for p in range(3)])
            ev = 0
            for ti in range(cfg.ntile):
                gi = cfg.gi_of(ti * 128)
                s = ti % 2
                for (c0, cn) in ((0, 512), (512, 512), (1024, NTM - 1024)):
                    ps = k.PS[ev % 4]
                    for kt in range(8):
                        mk.mm(ps[:, :cn], ZT.p(gi, (slice(None), kt, slice(ti * 128, (ti + 1) * 128))), wall(kt, c0, cn), start=(kt == 0), stop=(kt == 7))
                    mk.cp("act" if ev % 2 else "dve", stt_[s][:, c0:c0 + cn], ps[:, :cn])
                    ev += 1
                mk.dma(STQ, k.utok[b].p(ti, (slice(ti * 128, (ti + 1) * 128), slice(None))), stt_[s][:])


def uT_rows(k, b, blk):
    return k.uT[b].p(blk, (slice(blk * 128, (blk + 1) * 128), slice(None)))


def conv4(mk, out, x, w, segs):
    for (s0, sn) in segs:
        e = s0 + sn
        mk.ts("dve", out[:, s0:e], x[:, s0:e], w[:, 1:2], None, op0=ALU.mult)
        mk.stt(out[:, s0 + 1:e], x[:, s0:e - 1], w[:, 0:1], out[:, s0 + 1:e], ALU.mult, ALU.add)
        mk.stt(out[:, s0:e - 1], x[:, s0 + 1:e], w[:, 2:3], out[:, s0:e - 1], ALU.mult, ALU.add)
        mk.stt(out[:, s0:e - 2], x[:, s0 + 2:e], w[:, 3:4], out[:, s0:e - 2], ALU.mult, ALU.add)


def softplus(mk, out, x, t1):
    mk.act(t1, x, AF.Abs)
    mk.act(t1, t1, AF.Exp, scale=-1.0)
    mk.act(t1, t1, AF.Ln, bias=1.0)
    mk.stt(out, x, 0.0, t1, ALU.max, ALU.add)


def stage_lru(k, l, b):
    mk, cfg = k.mk, k.cfg
    NT, C = cfg.NT, cfg.C
    segs = [(0, C), (C, cfg.T)]
    with scope(mk):
        cw = mk.sb("lr_cw", [128, 4, 4]); cb = mk.sb("lr_cb", [128, 4])
        wa = mk.sb("lr_wa", [128, 2, 4, 128]); wx = mk.sb("lr_wx", [128, 2, 4, 128])
        vv = mk.sb("lr_v", [128, 3, 2, 4]); csp = mk.sb("lr_csp", [128, 2, 4]); t1 = mk.sb("lr_t1", [128, 2, 4])
        mk.dma("sp", cw[:], k.lru_cw[l]); mk.dma("sp", cb[:], k.lru_cb[l])
        mk.dma("sp", wa[:], k.lru_wa[l]); mk.dma("sp", wx[:], k.lru_wx[l])
        mk.dma("sp", vv[:], k.lru_v[l])
        mk.ts("dve", csp[:], vv[:, 2], -1.0, None, op0=ALU.mult)
        softplus(mk, csp[:], csp[:], t1[:])
        mk.ts("dve", csp[:], csp[:], -8.0, None, op0=ALU.mult)
        xb = mk.sb("lr_xb", [128, NT]); gb = mk.sb("lr_gb", [128, NT]); xc = mk.sb("lr_xc", [128, NT])
        R = mk.sb("lr_r", [128, NT]); IG = mk.sb("lr_ig", [128, NT]); A = mk.sb("lr_a", [128, NT])
        H = [mk.sb(f"lr_h{d}", [128, NT]) for d in range(2)]
        ev = 0
        for h in range(4):
            mk.dma("sp", xb[:], uT_rows(k, b, LRX + h))
            mk.dma("sp", gb[:], uT_rows(k, b, LRG + h))
            conv4(mk, xc, xb, cw[:, h, :], segs)
            mk.ts("dve", xc[:], xc[:], cb[:, h:h + 1], None, op0=ALU.add)
            for d in range(2):
                for (t0, n, seg) in cfg.groups:
                    pa, px = k.PS[ev % 2], k.PS[2 + ev % 2]
                    ev += 1
                    mk.mm(pa[:, :n], wa[:, d, h, :], xc[:, t0:t0 + n])
                    mk.mm(px[:, :n], wx[:, d, h, :], xc[:, t0:t0 + n])
                    mk.act(R[:, t0:t0 + n], pa[:, :n], AF.Sigmoid, bias=vv[:, 0, d, h:h + 1])
                    mk.act(IG[:, t0:t0 + n], px[:, :n], AF.Sigmoid, bias=vv[:, 1, d, h:h + 1])
                mk.act(A[:], R[:], AF.Exp, scale=csp[:, d, h:h + 1])
                mk.tt("dve", R[:], A[:], A[:], ALU.mult)
                mk.ts("dve", R[:], R[:], -1.0, 1.0, op0=ALU.mult, op1=ALU.add)
                mk.ts("dve", R[:], R[:], 0.0, None, op0=ALU.max)
                mk.act(R[:], R[:], AF.Sqrt)
                mk.tt("dve", R[:], R[:], IG[:], ALU.mult)
                mk.tt("dve", R[:], R[:], xc[:], ALU.mult)
                if d == 0:
                    mk.I("dve", "tensor_tensor_scan", out=H[0][:], data0=A[:], data1=R[:], initial=0.0, op0=ALU.mult, op1=ALU.add)
                else:
                    mk.I("dve", "tensor_tensor_scan", out=H[1][:, C - 1::-1] if False else V(H[1].h[:, 0:C][:, ::-1], [H[1].trk]),
                         data0=V(A.h[:, 0:C][:, ::-1], [A.trk]), data1=V(R.h[:, 0:C][:, ::-1], [R.trk]),
                         initial=0.0, op0=ALU.mult, op1=ALU.add)
                    mk.I("dve", "tensor_tensor_scan", out=V(H[1].h[:, C:NT][:, ::-1], [H[1].trk]),
                         data0=V(A.h[:, C:NT][:, ::-1], [A.trk]), data1=V(R.h[:, C:NT][:, ::-1], [R.trk]),
                         initial=H[1][:, 0:1], op0=ALU.mult, op1=ALU.add)
            mk.tt("dve", H[0][:], H[0][:], H[1][:], ALU.add)
            mk.act(gb[:], gb[:], AF.Gelu)
            mk.tt("dve", H[0][:], H[0][:], gb[:], ALU.mult)
            mk.dma(STQ, k.yT[b][2].p(h, (slice(h * 128, (h + 1) * 128), slice(None))), H[0][:])


def stage_att(k, l, b, need_ctx):
    mk, cfg = k.mk, k.cfg
    NT, C, ntile = cfg.NT, cfg.C, cfg.ntile
    with scope(mk):
        QT = mk.sb("at_qt", [64, 8, NT], BF16)
        KT = mk.sb("at_kt", [64, 2, NT], BF16)
        VA = mk.sb("at_va", [128, ntile, 2, 65], BF16)
        nw = mk.sb("at_nw", [128, 2, 64])
        mk.dma("sp", nw[:], k.att_nw[l])
        mk._memset("pool", VA[:], 1.0)
        with scope(mk):
            a_ = [mk.sb(f"at_a{i}", [128, 768]) for i in range(2)]
            cs_ = [mk.sb(f"at_cs{i}", [128, 2, 2, 16]) for i in range(2)]
            sq = mk.sb("at_sq", [128, 640]); ss = mk.sb("at_ss", [128, 10])
            qn = mk.sb("at_qn", [128, 640]); qr = mk.sb("at_qr", [128, 640])
            tt_ = [mk.sb(f"at_t{i}", [128, 10, 2, 16]) for i in range(4)]
            for ti in range(ntile):
                a = a_[ti % 2]
                mk.dma("sp", a[:], k.utok[b].p(ti, (slice(ti * 128, (ti + 1) * 128), slice(TM_ATQ, TM_ATQ + 768))))
                mk.tt("dve", sq[:], a[:, 0:640], a[:, 0:640], ALU.mult)
                mk.I("dve", "tensor_reduce", out=ss[:], in_=sq[:].rr("p (h d) -> p h d", d=64), axis=AX.X, op=ALU.add)
                mk.act(ss[:], ss[:], AF.Sqrt, scale=1.0 / 64, bias=EPS)
                mk.I("dve", "reciprocal", out=ss[:], in_=ss[:])
                mk.tt("dve", qn[:].rr("p (h d) -> p h d", d=64), a[:, 0:640].rr("p (h d) -> p h d", d=64), vbc(vus(ss[:], 2), [128, 10, 64]), ALU.mult)
                mk.tt("pool", qn[:, 0:512].rr("p (h d) -> p h d", d=64), qn[:, 0:512].rr("p (h d) -> p h d", d=64), vbc(vus(nw[:, 0, :], 1), [128, 8, 64]), ALU.mult)
                mk.tt("pool", qn[:, 512:640].rr("p (h d) -> p h d", d=64), qn[:, 512:640].rr("p (h d) -> p h d", d=64), vbc(vus(nw[:, 1, :], 1), [128, 2, 64]), ALU.mult)
                if ti * 128 >= C:
                    cs = cs_[ti % 2]
                    tl = ti * 128 - C
                    mk.dma("sp", cs[:], k.c_rope[tl:tl + 128])
                    xv = qn[:].rr("p (h a f r) -> p h a f r", h=10, a=2, f=2)
                    ov = qr[:].rr("p (h a f r) -> p h a f r", h=10, a=2, f=2)
                    x1, x2 = xv[:, :, :, 0, :], xv[:, :, :, 1, :]
                    cc = vbc(vus(cs[:, 0], 1), [128, 10, 2, 16]); sn = vbc(vus(cs[:, 1], 1), [128, 10, 2, 16])
                    mk.tt("dve", tt_[0][:], x1, cc, ALU.mult)
                    mk.tt("pool", tt_[1][:], x2, sn, ALU.mult)
                    mk.tt("dve", tt_[2][:], x2, cc, ALU.mult)
                    mk.tt("pool", tt_[3][:], x1, sn, ALU.mult)
                    mk.tt("dve", ov[:, :, :, 0, :], tt_[0][:], tt_[1][:], ALU.subtract)
                    mk.tt("dve", ov[:, :, :, 1, :], tt_[2][:], tt_[3][:], ALU.add)
                    src = qr
                else:
                    src = qn
                for hh in range(10):
                    ps = k.PS[hh // 4]
                    o = (hh % 4) * 128
                    mk.tr(ps[0:64, o:o + 128], src[:, hh * 64:(hh + 1) * 64], k.ident[:])
                mk.cp("act", QT[:, 0:4, ti * 128:(ti + 1) * 128], k.PS[0][0:64, :].rr("p (a b) -> p a b", a=4))
                mk.cp("dve", QT[:, 4:8, ti * 128:(ti + 1) * 128], k.PS[1][0:64, :].rr("p (a b) -> p a b", a=4))
                mk.cp("act", KT[:, 0:2, ti * 128:(ti + 1) * 128], k.PS[2][0:64, 0:256].rr("p (a b) -> p a b", a=2))
                mk.cp("pool", VA[:, ti, :, 0:64], a[:, 640:768].rr("p (g d) -> p g d", g=2))
        with scope(mk):
            OT = [mk.sb(f"at_ot{i}", [128, 512]) for i in range(2)]
            PT = [mk.sb(f"at_pt{i}", [128, 512], BF16) for i in range(2)]
            YS = [mk.sb(f"at_ys{i}", [128, 4, 128]) for i in range(2)]
            rc = mk.sb("at_rc", [128, 8])
            yv = V(k.yT[b][3].ap().rearrange("(ft p) t -> p ft t", p=128), [k.yT[b][3].trk])
            it = 0
            qtiles = list(range(ntile)) if need_ctx else list(range(C // 128, ntile))
            for qn_i, qi in enumerate(qtiles):
                keyt = list(range(C // 128)) if qi * 128 < C else list(range(ntile))
                ot = OT[qn_i % 2]
                items = [(g, kt) for g in range(2) for kt in keyt]

                def front(i, base=it):
                    g, kt = items[i]
                    S = k.PS[(base + i) % 2]; pt = PT[(base + i) % 2]
                    mk.mm(S[:, :], KT[:, g, kt * 128:(kt + 1) * 128], QT[:, 4 * g:4 * g + 4, qi * 128:(qi + 1) * 128])
                    mk.act(pt[:], S[:], AF.Exp, scale=0.125)

                def back(i, base=it):
                    g, kt = items[i]
                    pt = PT[(base + i) % 2]
                    for hq in range(4):
                        mk.mm(k.PS[2 + hq][:, 0:65], pt[:, hq * 128:(hq + 1) * 128], VA[:, kt, g, :], start=(kt == keyt[0]), stop=(kt == keyt[-1]))
                    if kt == keyt[-1]:
                        for hq in range(4):
                            hh = 4 * g + hq
                            mk.I("dve", "reciprocal", out=rc[:, hh:hh + 1], in_=k.PS[2 + hq][:, 64:65])
                            mk.ts("dve", ot[:, hh * 64:(hh + 1) * 64], k.PS[2 + hq][:, 0:64], rc[:, hh:hh + 1], None, op0=ALU.mult)

                front(0)
                for i in range(len(items)):
                    if i + 1 < len(items):
                        front(i + 1)
                    back(i)
                it += len(items)
                for ft in range(4):
                    mk.tr(k.PS[6][:, ft * 128:(ft + 1) * 128], ot[:, ft * 128:(ft + 1) * 128], k.ident[:])
                ys = YS[qn_i % 2]
                mk.cp("act", ys[:], k.PS[6][:].rr("p (a b) -> p a b", a=4))
                mk.dma(STQ, yv[:, :, qi * 128:(qi + 1) * 128], ys[:])


def stage_dn(k, l, b):
    mk, cfg = k.mk, k.cfg
    NT, C, nch = cfg.NT, cfg.C, cfg.nch
    segs = [(0, C), (C, cfg.T)]
    LE, GE, GT, LT = 0, 1, 2, 3
    with scope(mk):
        graw = mk.sb("dn_graw", [64, nch, 16]); ab = mk.sb("dn_ab", [64, 2, 8])
        beta = mk.sb("dn_beta", [64, nch, 8]); g = mk.sb("dn_g", [64, nch, 8]); t1 = mk.sb("dn_t1", [64, nch, 8])
        gam = mk.sb("dn_gam", [64, nch, 8]); erg = mk.sb("dn_erg", [64, nch, 8]); cdec = mk.sb("dn_cdec", [128, nch, 8])
        eg = mk.sb("dn_eg", [64, nch, 8]); bg = mk.sb("dn_bg", [64, nch, 8]); ea = mk.sb("dn_ea", [64, 8])
        cw = mk.sb("dn_cw", [128, 12, 4])
        mk.dma("sp", cw[:], k.dn_cw[l])
        mk.dma("sp", ab[:], k.dn_ab[l])
        gsrc = k.utok[b].h[:, TM_BETA:TM_BETA + 16].rearrange("(n c) f -> c n f", c=64)
        for n0 in range(0, nch, 16):
            n1 = min(nch, n0 + 16)
            mk.dma("sp", graw[:, n0:n1, :], V(gsrc[:, n0:n1, :], [k.utok[b].trk] + list(k.utok[b].parts.values())))
        mk.act(beta[:], graw[:, :, 0:8], AF.Sigmoid)
        mk.tt("dve", g[:], graw[:, :, 8:16], vbc(vus(ab[:, 1, :], 1), [64, nch, 8]), ALU.add)
        softplus(mk, g[:], g[:], t1[:])
        mk.act(ea[:], ab[:, 0, :], AF.Exp)
        mk.tt("dve", g[:], g[:], vbc(vus(ea[:], 1), [64, nch, 8]), ALU.mult)
        mk.ts("dve", g[:], g[:], -1.0, None, op0=ALU.mult)
        for d in range(2):
            cs = slice(d * 4, d * 4 + 4)
            n4 = nch * 4
            mk.mm(k.PS[0][0:64, 0:n4], k.masks[:, LE if d == 0 else GE, :], g[:, :, cs])
            mk.cp("dve", gam[:, :, cs], k.PS[0][0:64, 0:n4].rr("p (n f) -> p n f", f=4))
            mk.mm(k.PS[1][0:64, 0:n4], k.masks[:, GT if d == 0 else LT, :], g[:, :, cs])
            mk.act(erg[:, :, cs], k.PS[1][0:64, 0:n4].rr("p (n f) -> p n f", f=4), AF.Exp)
            mk.mm(k.PS[2][:, 0:n4], k.ones[0:64, :], g[:, :, cs])
            mk.act(cdec[:, :, cs], k.PS[2][:, 0:n4].rr("p (n f) -> p n f", f=4), AF.Exp)
        mk.act(eg[:], gam[:], AF.Exp)
        mk.tt("dve", bg[:], beta[:], eg[:], ALU.mult)
        Qb = [mk.sb(f"dn_q{i}", [128, NT], BF16) for i in range(4)]
        Kb = [mk.sb(f"dn_k{i}", [128, NT], BF16) for i in range(4)]
        Vb = [mk.sb(f"dn_v{i}", [128, NT], BF16) for i in range(4)]
        with scope(mk):
            raws = [mk.sb(f"dn_raw{i}", [128, NT]) for i in range(2)]
            tmpc = mk.sb("dn_tmpc", [128, NT])
            sq = mk.sb("dn_sq", [128, 512]); rs = mk.sb("dn_rs", [128, 512])
            ri = 0
            for h in range(4):
                for (dst, blk0, wi, scl) in ((Qb[h], DNQ, 0, 128.0 ** -0.5), (Kb[h], DNK, 1, 1.0), (Vb[h], DNV, 2, None)):
                    raw = raws[ri % 2]; ri += 1
                    mk.dma("sp", raw[:], uT_rows(k, b, blk0 + h))
                    conv4(mk, tmpc, raw, cw[:, wi * 4 + h, :], segs)
                    mk.act(tmpc[:], tmpc[:], AF.Silu)
                    if scl is None:
                        mk.cp("pool", dst[:], tmpc[:])
                        continue
                    for gi, (t0, n, seg) in enumerate(cfg.groups):
                        ps = k.PS[gi % 2]
                        mk.act(sq[:, :n], tmpc[:, t0:t0 + n], AF.Square)
                        mk.mm(ps[:, :n], k.ones[:], sq[:, :n])
                        mk.act(rs[:, :n], ps[:, :n], AF.Sqrt, bias=EPS)
                        mk.I("dve", "reciprocal", out=rs[:, :n], in_=rs[:, :n])
                        mk.stt(dst[:, t0:t0 + n], tmpc[:, t0:t0 + n], scl, rs[:, :n], ALU.mult, ALU.mult)
        id64 = k.ident[0:64, 0:64]
        orders = [cfg.order(0), cfg.order(1)]

        def stream(hp):
            u = f"s{hp}"
            Q0, Q1, Q2, Q3 = k.PS[4 * hp:4 * hp + 4]
            S = mk.sb("dn_S" + u, [128, 4, 128]); Sb = mk.sb("dn_Sb" + u, [128, 4, 128], BF16)
            mk._memset("dve", S[:], 0.0)
            mk._memset("pool", Sb[:], 0.0)
            GD = mk.sb("dn_gd" + u, [64, 4, 64]); NM = mk.sb("dn_nm" + u, [64, 4, 64]); Dm = mk.sb("dn_dm" + u, [64, 4, 64])
            Ds = mk.sb("dn_ds" + u, [64, 4, 64]); Di = mk.sb("dn_di" + u, [64, 4, 64])
            AL = mk.sb("dn_al" + u, [64, 4, 64]); AQ = mk.sb("dn_aq" + u, [64, 4, 64]); AQT = mk.sb("dn_aqt" + u, [64, 4, 64])
            BB = mk.sb("dn_bb" + u, [64, 2, 4, 64]); X = mk.sb("dn_x" + u, [64, 4, 64])
            vb = mk.sb("dn_vb" + u, [64, 4, 128]); kbd = mk.sb("dn_kbd" + u, [64, 4, 128]); kd = mk.sb("dn_kd" + u, [64, 4, 128])
            U = mk.sb("dn_u" + u, [64, 4, 128]); WT = mk.sb("dn_wt" + u, [128, 4, 64]); VN = mk.sb("dn_vn" + u, [64, 4, 128])
            O2 = mk.sb("dn_o2" + u, [64, 4, 128]); Os = [mk.sb(f"dn_o{i}" + u, [64, 4, 128]) for i in range(2)]
            v4 = lambda p, a=4: p.rr("p (a b) -> p a b", a=a)
            for s in range(nch):
                info = []
                for c in range(4):
                    d, hh = c // 2, c % 2
                    n = orders[d][s]
                    info.append((d, 2 * hp + hh, n, n * 64, d * 4 + 2 * hp + hh))
                for c, (d, h, n, t0, col) in enumerate(info):
                    mk.ts("pool", GD[:, c, :], k.masks[:, LE if d == 0 else GE, :], g[:, n, col:col + 1], None, op0=ALU.mult)
                yield
                for c, (d, h, n, t0, col) in enumerate(info):
                    kT = Kb[h][:, t0:t0 + 64]
                    mk.mm(Q0[0:64, c * 64:(c + 1) * 64], k.ones[0:64, 0:64], GD[:, c, :])
                    mk.mm(Q0[0:64, 256 + c * 64:256 + (c + 1) * 64], kT, kT)
                    mk.mm(Q1[0:64, c * 64:(c + 1) * 64], Qb[h][:, t0:t0 + 64], kT)
                    mk.mm(Q3[0:64, c * 128:(c + 1) * 128], kT, k.identb[:])
                yield
                for c, (d, h, n, t0, col) in enumerate(info):
                    mk.ts("dve", NM[:, c, :], Q0[0:64, c * 64:(c + 1) * 64], gam[:, n, col:col + 1], 0.0, op0=ALU.subtract, op1=ALU.max)
                    mk.act(kbd[:, c, :], Q3[0:64, c * 128:(c + 1) * 128], AF.Identity, scale=bg[:, n, col:col + 1])
                yield
                mk.act(Dm[:], NM[:], AF.Exp, scale=-1.0)
                for c, (d, h, n, t0, col) in enumerate(info):
                    mk.act(kd[:, c, :], Q3[0:64, c * 128:(c + 1) * 128], AF.Identity, scale=erg[:, n, col:col + 1])
                yield
                for d in range(2):
                    dsl = slice(2 * d, 2 * d + 2)
                    mk.tt("pool", Ds[:, dsl, :], Dm[:, dsl, :], vbc(vus(k.masks[:, GT if d == 0 else LT, :], 1), [64, 2, 64]), ALU.mult)
                    mk.tt("pool", Di[:, dsl, :], Dm[:, dsl, :], vbc(vus(k.masks[:, GE if d == 0 else LE, :], 1), [64, 2, 64]), ALU.mult)
                for c, (d, h, n, t0, col) in enumerate(info):
                    mk.mm(Q3[0:64, c * 128:(c + 1) * 128], Vb[h][:, t0:t0 + 64], k.identb[:])
                yield
                for c, (d, h, n, t0, col) in enumerate(info):
                    mk.stt(AL[:, c, :], Q0[0:64, 256 + c * 64:256 + (c + 1) * 64], beta[:, n, col:col + 1], Ds[:, c, :], ALU.mult, ALU.mult)
                mk.tt("dve", AQ[:], v4(Q1[0:64, 0:256]), Di[:], ALU.mult)
                for c, (d, h, n, t0, col) in enumerate(info):
                    mk.act(vb[:, c, :], Q3[0:64, c * 128:(c + 1) * 128], AF.Identity, scale=beta[:, n, col:col + 1])
                yield
                for c in range(4):
                    mk.mm(Q2[0:64, c * 64:(c + 1) * 64], AL[:, c, :], id64)
                    mk.mm(Q2[0:64, 256 + c * 64:256 + (c + 1) * 64], AQ[:, c, :], id64)
                yield
                mk.ts("dve", BB[:, 0], v4(Q2[0:64, 0:256]), -1.0, None, op0=ALU.mult)
                mk.ts("pool", BB[:, 1], AL[:], -1.0, None, op0=ALU.mult)
                yield
                mk.tt("pool", X[:], BB[:, 0], vbc(vus(id64, 1), [64, 4, 64]), ALU.add)
                mk.cp("act", AQT[:], v4(Q2[0:64, 256:512]))
                for c in range(4):
                    mk.mm(Q0[0:64, c * 64:(c + 1) * 64], BB[:, 1, c, :], BB[:, 0, c, :])
                    mk.mm(Q0[0:64, 256 + c * 64:256 + (c + 1) * 64], BB[:, 0, c, :], BB[:, 1, c, :])
                yield
                mk.cp("act", BB[:], Q0[0:64, :].rr("p (t a b) -> p t a b", t=2, a=4))
                yield
                for lev in range(1, 6):
                    if lev < 5:
                        for c in range(4):
                            mk.mm(Q0[0:64, c * 64:(c + 1) * 64], BB[:, 1, c, :], BB[:, 0, c, :])
                            mk.mm(Q0[0:64, 256 + c * 64:256 + (c + 1) * 64], BB[:, 0, c, :], BB[:, 1, c, :])
                    for c in range(4):
                        mk.mm(Q1[0:64, 256 + c * 64:256 + (c + 1) * 64], BB[:, 1, c, :], X[:, c, :])
                    yield
                    if lev < 5:
                        mk.cp("act", BB[:], Q0[0:64, :].rr("p (t a b) -> p t a b", t=2, a=4))
                    mk.tt("dve", X[:], X[:], v4(Q1[0:64, 256:512]), ALU.add)
                    yield
                for c in range(4):
                    mk.mm(Q2[0:64, c * 128:(c + 1) * 128], X[:, c, :], vb[:, c, :])
                    mk.mm(Q1[:, c * 64:(c + 1) * 64], kbd[:, c, :], X[:, c, :])
                yield
                mk.cp("act", U[:], v4(Q2[0:64, :]))
                mk.cp("dve", WT[:], v4(Q1[:, 0:256]))
                yield
                for c in range(4):
                    mk.mm(Q0[0:64, c * 128:(c + 1) * 128], WT[:, c, :], S[:, c, :])
                yield
                mk.tt("dve", VN[:], U[:], v4(Q0[0:64, :]), ALU.subtract)
                yield
                for c, (d, h, n, t0, col) in enumerate(info):
                    mk.mm(Q1[0:64, c * 128:(c + 1) * 128], Qb[h][:, t0:t0 + 64], Sb[:, c, :])
                    mk.mm(Q2[0:64, c * 128:(c + 1) * 128], AQT[:, c, :], VN[:, c, :])
                    mk.mm(Q3[:, c * 128:(c + 1) * 128], kd[:, c, :], VN[:, c, :])
                yield
                mk.cp("act", O2[:], v4(Q2[0:64, :]))
                for c, (d, h, n, t0, col) in enumerate(info):
                    mk.stt(S[:, c, :], S[:, c, :], cdec[:, n, col:col + 1], Q3[:, c * 128:(c + 1) * 128], ALU.mult, ALU.add)
                mk.cp("pool", Sb[:], S[:])
                yield
                Ot = Os[s % 2]
                for c, (d, h, n, t0, col) in enumerate(info):
                    mk.stt(Ot[:, c, :], Q1[0:64, c * 128:(c + 1) * 128], eg[:, n, col:col + 1], O2[:, c, :], ALU.mult, ALU.add)
                for d in range(2):
                    t0 = orders[d][s] * 64
                    mk.dma(STQ, k.osc[b][d][t0:t0 + 64, hp * 256:(hp + 1) * 256], Ot[:, 2 * d:2 * d + 2, :].rr("p a b -> p (a b)"))
                yield

        nstream = getattr(cfg, "dn_streams", 2)
        if nstream == 2:
            gens = [stream(0), stream(1)]
            while gens:
                for gen in list(gens):
                    try:
                        next(gen)
                    except StopIteration:
                        gens.remove(gen)
        else:
            for hp in range(2):
                for _ in stream(hp):
                    pass


def stage_hg(k, l, b):
    mk, cfg = k.mk, k.cfg
    NT, C, nch = cfg.NT, cfg.C, cfg.nch
    LE, GE = 0, 1
    P = k.PS
    with scope(mk):
        rmask = mk.sb("hg_rm", [128, 2, NT], BF16)
        mk._memset("pool", rmask[:], 1.0)
        mk._memset("pool", rmask[:, 0, :].rr("p (n c) -> p n c", c=64)[:, :, 0:1], 0.0)
        mk._memset("pool", rmask[:, 1, :].rr("p (n c) -> p n c", c=64)[:, :, 63:64], 0.0)
        q = mk.sb("hg_q", [128, NT]); fr = mk.sb("hg_fr", [128, NT]); Kf = mk.sb("hg_kf", [128, NT])
        gc = mk.sb("hg_gc", [128, NT]); tmp = fr; QG = mk.sb("hg_qg", [128, NT])
        QTL = mk.sb("hg_qtl", [128, NT], BF16); KT4 = mk.sb("hg_kt4", [128, nch, 4, 64], BF16); KD = mk.sb("hg_kd", [128, NT])
        REF = mk.sb("hg_ref", [128, nch, 4]); EL = mk.sb("hg_el", [128, nch]); oml = mk.sb("hg_oml", [128, 1])
        S = mk.sb("hg_S", [128, 128])
        vt = [mk.sb(f"hg_vt{i}", [64, 128]) for i in range(2)]; vtb = [mk.sb(f"hg_vtb{i}", [64, 128], BF16) for i in range(2)]
        SCT = [mk.sb(f"hg_sct{i}", [64, 64], BF16) for i in range(2)]; kdt = [mk.sb(f"hg_kdt{i}", [64, 128], BF16) for i in range(2)]
        Os = [mk.sb(f"hg_o{i}", [64, 128]) for i in range(2)]
        v3 = lambda t: t[:].rr("p (n c) -> p n c", c=64)
        for h in range(4):
            mk.dma("sp", q[:], uT_rows(k, b, HGQ + h))
            mk.act(q[:], q[:], AF.Silu)
            for d in range(2):
                mk.dma("sp", fr[:], uT_rows(k, b, (HGFF if d == 0 else HGFB) + h))
                lbc = k.lb[:, d, l, h:h + 1]
                mk.ts("dve", oml[:], lbc, -1.0, 1.0, op0=ALU.mult, op1=ALU.add)
                mk.act(fr[:], fr[:], AF.Sigmoid)
                mk.ts("dve", fr[:], fr[:], oml[:, 0:1], lbc, op0=ALU.mult, op1=ALU.add)
                mk.ts("dve", Kf[:], fr[:], -1.0, 1.0, op0=ALU.mult, op1=ALU.add)
                mk.act(fr[:], fr[:], AF.Ln)
                if d == 0:
                    mk.I("dve", "tensor_tensor_scan", out=gc[:], data0=rmask[:, 0, :], data1=fr[:], initial=0.0, op0=ALU.mult, op1=ALU.add)
                else:
                    mk.I("dve", "tensor_tensor_scan", out=V(gc.h[:, ::-1], [gc.trk]), data0=V(rmask.h[:, 1, ::-1], [rmask.trk]),
                         data1=V(fr.h[:, ::-1], [fr.trk]), initial=0.0, op0=ALU.mult, op1=ALU.add)
                last = 63 if d == 0 else 0
                mk.act(tmp[:], gc[:], AF.Exp)
                mk.cp("dve", EL[:], v3(tmp)[:, :, last])
                mk.tt("dve", QG[:], q[:], tmp[:], ALU.mult)
                mk._memset("pool", REF[:], 0.0)
                if d == 0:
                    mk.cp("pool", REF[:, :, 1:4], v3(gc)[:, :, 15:63:16])
                else:
                    mk.cp("pool", REF[:, :, 0:3], v3(gc)[:, :, 16:64:16])
                g4 = gc[:].rr("p (n j r) -> p n j r", j=4, r=16)
                t4 = tmp[:].rr("p (n j r) -> p n j r", j=4, r=16)
                mk.tt("dve", t4, g4, vbc(vus(REF[:], 3), [128, nch, 4, 16]), ALU.subtract)
                mk.act(tmp[:], tmp[:], AF.Exp)
                mk.tt("dve", QTL[:], q[:], tmp[:], ALU.mult)
                for J in range(4):
                    mk.tt("dve", v3(tmp), vbc(REF[:, :, J:J + 1], [128, nch, 64]), v3(gc), ALU.subtract)
                    mk.ts("dve", tmp[:], tmp[:], 60.0, None, op0=ALU.min)
                    mk.act(tmp[:], tmp[:], AF.Exp)
                    mk.tt("dve", KT4[:, :, J, :], v3(tmp), v3(Kf), ALU.mult)
                mk.tt("dve", v3(tmp), vbc(v3(gc)[:, :, last:last + 1], [128, nch, 64]), v3(gc), ALU.subtract)
                mk.act(tmp[:], tmp[:], AF.Exp)
                mk.tt("dve", KD[:], tmp[:], Kf[:], ALU.mult)
                mk._memset("dve", S[:], 0.0)
                order = cfg.order(d)
                def pre(s):
                    n = order[s]; t0 = n * 64; a = s % 2
                    mk.dma("sp", vt[a][:], V(k.utok[b].h[t0:t0 + 64, TM_HGI + h * 128:TM_HGI + (h + 1) * 128], [k.utok[b].parts[t0 // 128]]))
                    mk.cp("pool", vtb[a][:], vt[a][:])
                    for J in range(4):
                        mk.mm(P[a][0:64, 16 * J:16 * J + 16], KT4[:, n, J, :], QTL[:, t0 + 16 * J:t0 + 16 * J + 16])
                    mk.tt("dve", SCT[a][:], P[a][0:64, 0:64], k.masks[:, LE if d == 0 else GE, :], ALU.mult)
                    mk.tr(P[2 + a][0:64, 0:128], KD[:, t0:t0 + 64], k.ident[:])
                    mk.cp("act", kdt[a][:], P[2 + a][0:64, 0:128])

                def main(s):
                    n = order[s]; t0 = n * 64; a = s % 2
                    mk.mm(P[4 + a][0:64, 0:128], QG[:, t0:t0 + 64], S[:], start=True, stop=False)
                    mk.mm(P[4 + a][0:64, 0:128], SCT[a][:], vtb[a][:], start=False, stop=True)
                    mk.cp("act", Os[a][:], P[4 + a][0:64, 0:128])
                    mk.dma(STQ, k.osc[b][2 + d][t0:t0 + 64, h * 128:(h + 1) * 128], Os[a][:])
                    mk.mm(P[6 + a][:, 0:128], kdt[a][:], vtb[a][:])
                    mk.stt(S[:], S[:], EL[:, n:n + 1], P[6 + a][:, 0:128], ALU.mult, ALU.add)

                pre(0)
                for s in range(len(order)):
                    if s + 1 < len(order):
                        pre(s + 1)
                    main(s)


def stage_post(k, l, b, which, nwT, gblk):
    mk, cfg = k.mk, k.cfg
    with scope(mk):
        nw = mk.sb("po_nw", [128, 1])
        mk.dma("sp", nw[:], nwT[l])
        of = [mk.sb(f"po_of{i}", [128, 512]) for i in range(2)]; ob = [mk.sb(f"po_ob{i}", [128, 512]) for i in range(2)]
        G = [mk.sb(f"po_g{i}", [128, 4, 128]) for i in range(2)]; Y = [mk.sb(f"po_y{i}", [128, 4, 128]) for i in range(2)]
        sq = mk.sb("po_sq", [128, 512]); ss = mk.sb("po_ss", [128, 4])
        uv = V(k.uT[b].ap().rearrange("(x p) t -> p x t", p=128), [k.uT[b].trk] + list(k.uT[b].parts.values()))
        yv = V(k.yT[b][which].ap().rearrange("(ft p) t -> p ft t", p=128), [k.yT[b][which].trk])
        for ti in range(cfg.ntile):
            s = ti % 2
            rows = slice(ti * 128, (ti + 1) * 128)
            mk.dma("sp", of[s][:], k.osc[b][2 * which][rows, :])
            mk.dma("sp", ob[s][:], k.osc[b][2 * which + 1][rows, :])
            mk.dma("sp", G[s][:], uv[:, gblk:gblk + 4, rows])
            o = of[s]
            mk.tt("dve", o[:], o[:], ob[s][:], ALU.add)
            mk.tt("pool", sq[:], o[:], o[:], ALU.mult)
            mk.I("dve", "tensor_reduce", out=ss[:], in_=sq[:].rr("p (h d) -> p h d", d=128), axis=AX.X, op=ALU.add)
            mk.act(ss[:], ss[:], AF.Sqrt, scale=1.0 / 128, bias=EPS)
            mk.I("dve", "reciprocal", out=ss[:], in_=ss[:])
            mk.tt("dve", o[:].rr("p (h d) -> p h d", d=128), o[:].rr("p (h d) -> p h d", d=128), vbc(vus(ss[:], 2), [128, 4, 128]), ALU.mult)
            ps = k.PS[s]
            for h in range(4):
                mk.tr(ps[:, h * 128:(h + 1) * 128], o[:, h * 128:(h + 1) * 128], k.ident[:])
            mk.act(G[s][:], G[s][:], AF.Silu)
            mk.stt(Y[s][:].rr("p a b -> p (a b)"), ps[:], nw[:, 0:1], G[s][:].rr("p a b -> p (a b)"), ALU.mult, ALU.mult)
            mk.dma(STQ, yv[:, :, rows], Y[s][:])


def load_cast(mk, dst_fn, src_fn, npieces, stage_tiles, semL):
    for pc in range(npieces):
        st = stage_tiles[pc % 2]
        sv = src_fn(pc)
        mk.dma("sp", V(st.h[tuple(slice(0, x) for x in sv.ap.shape)], [st.trk], st), sv)
        mk.cp("pool" if pc % 2 else "dve", dst_fn(pc), V(st.h[tuple(slice(0, x) for x in sv.ap.shape)], [st.trk], st))


def stage_s3a(k, l, need_ctx):
    mk, cfg = k.mk, k.cfg
    NB = cfg.NB
    with scope(mk):
        wbr = mk.sb("a_wbr", [128, 4, 4, D], BF16)
        wo = mk.sb("a_wo", [128, 8, D], BF16)
        with scope(mk):
            stg = [mk.sb(f"a_stg{i}", [128, 4, D]) for i in range(2)]
            load_cast(mk, lambda pc: wbr[:, pc], lambda pc: k.w_br[l, :, pc], 4, stg, 1)
            load_cast(mk, lambda pc: wo[:, pc * 4:(pc + 1) * 4, :], lambda pc: k.w_out[l, :, pc * 4:(pc + 1) * 4, :], 2, stg, 1)
        yt = [mk.sb(f"a_yt{i}", [128, 4, 512]) for i in range(2)]
        ytb = mk.sb("a_ytb", [128, 4, 4, 512], BF16)
        G = [mk.sb(f"a_g{i}", [128, 4, 512]) for i in range(2)]
        merged = mk.sb("a_mg", [128, 8, 512], BF16)
        macc = mk.sb("a_macc", [128, 512]); mt = mk.sb("a_mt", [128, 512])
        hg = [mk.sb(f"a_h{i}", [128, 8, 512]) for i in range(2)]
        it = 0
        ev = 0
        for b in range(NB):
            hv = hT_view(k, b)
            uv = V(k.uT[b].ap().rearrange("(x p) t -> p x t", p=128), [k.uT[b].trk] + list(k.uT[b].parts.values()))
            for gi, (t0, n, seg) in enumerate(cfg.groups):
                if seg == 0 and not need_ctx:
                    continue
                j = NB if seg == 0 else b
                h_ = hg[it % 2]
                it += 1
                mk.dma("sp", h_[:, :, :n], hv[:, :, t0:t0 + n])
                for bi in range(4):
                    y_ = yt[bi % 2]
                    yv = V(k.yT[b][bi].ap().rearrange("(ft p) t -> p ft t", p=128), [k.yT[b][bi].trk] + list(k.yT[b][bi].parts.values()))
                    mk.dma("sp", y_[:, :, :n], yv[:, :, t0:t0 + n])
                    mk.cp("pool" if bi % 2 else "dve", ytb[:, bi, :, :n], y_[:, :, :n])
                for c in range(8):
                    g_ = G[c % 2]
                    mk.dma("sp", g_[:, :, :n], uv[:, MGB + c:MGB + c + 25:8, t0:t0 + n])
                    mk.act(g_[:, :, :n], g_[:, :, :n], AF.Sigmoid)
                    for bi in range(4):
                        ps = k.PS[ev % 4]
                        ev += 1
                        for ft in range(4):
                            mk.mm(ps[:, :n], wbr[:, bi, ft, c * 128:(c + 1) * 128], ytb[:, bi, ft, :n], start=(ft == 0), stop=(ft == 3))
                        if bi == 0:
                            mk.tt("dve", macc[:, :n], ps[:, :n], g_[:, 0, :n], ALU.mult)
                        else:
                            mk.tt("dve", mt[:, :n], ps[:, :n], g_[:, bi, :n], ALU.mult)
                            mk.tt("pool", macc[:, :n] if bi < 3 else merged[:, c, :n], macc[:, :n], mt[:, :n], ALU.add)
                for c in range(8):
                    ps = k.PS[4 + c % 4]
                    for kt in range(8):
                        mk.mm(ps[:, :n], wo[:, kt, c * 128:(c + 1) * 128], merged[:, kt, :n], start=(kt == 0), stop=(kt == 7))
                    mk.stt(h_[:, c, :n], ps[:, :n], k.modT[:, 16 + c, j:j + 1], h_[:, c, :n], ALU.mult, ALU.add)
                mk.dma(STQ, hv[:, :, t0:t0 + n], h_[:, :, :n])


def stage_s3b(k, l, need_ctx):
    mk, cfg = k.mk, k.cfg
    NB = cfg.NB
    with scope(mk):
        w1b = mk.sb("b_w1", [128, 8, 4 * D], BF16)
        w2b = mk.sb("b_w2", [128, 32, D], BF16)
        with scope(mk):
            stg = [mk.sb(f"b_stg{i}", [128, 8, 512]) for i in range(2)]
            load_cast(mk, lambda pc: w1b[:, :, pc * 512:(pc + 1) * 512], lambda pc: k.w1[l, :, :, pc * 512:(pc + 1) * 512], 8, stg, 1)
            stg2 = [V(stg[i].h[:].rearrange("p a b -> p (a b)").rearrange("p (a b) -> p a b", a=4), [stg[i].trk], stg[i]) for i in range(2)]
            for pc in range(8):
                sv = k.w2[l, :, pc * 4:(pc + 1) * 4, :]
                mk.dma("sp", stg2[pc % 2], sv)
                mk.cp("pool" if pc % 2 else "dve", w2b[:, pc * 4:(pc + 1) * 4, :], stg2[pc % 2])
        G = 256
        hg = [mk.sb(f"b_h{i}", [128, 8, G]) for i in range(2)]
        sq = mk.sb("b_sq", [128, 8, G]); tmp = mk.sb("b_tmp", [128, 8, G]); rs = mk.sb("b_rs", [128, G])
        z2 = mk.sb("b_z2", [128, 8, G], BF16)
        hid = mk.sb("b_hid", [128, 32, G], BF16)
        rl = [mk.sb(f"b_rl{i}", [128, G]) for i in range(2)]
        it = 0
        ev = 0
        for b in range(NB):
            hv = hT_view(k, b)
            subs = []
            for (t0, n, seg) in cfg.groups:
                if seg == 0 and not need_ctx:
                    continue
                for o in range(0, n, G):
                    subs.append((t0 + o, min(G, n - o), seg))
            for (t0, n, seg) in subs:
                j = NB if seg == 0 else b
                h_ = hg[it % 2]
                it += 1
                mk.dma("sp", h_[:, :, :n], hv[:, :, t0:t0 + n])
                rms_mod(k, h_, n, j, k.A2, 24, lambda kt: z2[:, kt, :n], tmp, sq, rs, k.PS[7])
                for fc in range(32):
                    ps = k.PS[ev % 4]
                    r_ = rl[ev % 2]
                    ev += 1
                    for kt in range(8):
                        mk.mm(ps[:, :n], w1b[:, kt, fc * 128:(fc + 1) * 128], z2[:, kt, :n], start=(kt == 0), stop=(kt == 7))
                    mk.act(r_[:, :n], ps[:, :n], AF.Relu)
                    mk.tt("pool" if ev % 2 else "dve", hid[:, fc, :n], r_[:, :n], r_[:, :n], ALU.mult)
                for c in range(8):
                    ps = k.PS[4 + c % 2]
                    for ft in range(32):
                        mk.mm(ps[:, :n], w2b[:, ft, c * 128:(c + 1) * 128], hid[:, ft, :n], start=(ft == 0), stop=(ft == 31))
                    mk.stt(h_[:, c, :n], ps[:, :n], k.modT[:, 40 + c, j:j + 1], h_[:, c, :n], ALU.mult, ALU.add)
                mk.dma(STQ, hv[:, :, t0:t0 + n], h_[:, :, :n])


_CACHE = {}


def host_consts(cfg):
    i = np.arange(64)
    kk, ff = i[:, None], i[None, :]
    masks = np.stack([(kk <= ff), (kk >= ff), (kk > ff), (kk < ff)], axis=1).astype(np.float32)
    rows = cfg.T // 64
    row_id = np.repeat(np.arange(rows), 64).astype(np.float32)
    col_id = np.tile(np.arange(64), rows).astype(np.float32)
    inv = (10000.0 ** (-np.arange(0, 32, 2, dtype=np.float32) / 32.0)).astype(np.float32)
    ang = np.stack([row_id[:, None] * inv, col_id[:, None] * inv], axis=1).astype(np.float32)
    rope = np.stack([np.cos(ang), np.sin(ang)], axis=1).astype(np.float32)
    return {"c_ident": np.eye(128, dtype=np.float32), "c_masks": np.ascontiguousarray(masks), "c_rope": np.ascontiguousarray(rope)}


def fm_cols():
    offs = np.cumsum([0, 512, 512, 512, 512, 8, 8, 512, 512, 512, 512, 512, 512, 512, 512, 128, 128, 4096])
    (dq, dk, dv, dg, dbeta, dalpha, hq, hff, hfb, hi, hgate, lx, lg, aq, ak, av, mg) = offs[:17]
    fm = []
    for base in (dq, dk, dv, dg, hq, hff, hfb, hgate, lx, lg):
        fm.append(np.arange(base, base + 512))
    fm.append(np.arange(mg, mg + 4096))
    fm = np.concatenate(fm)
    tm = np.concatenate([np.arange(hi, hi + 512), np.arange(aq, aq + 512), np.arange(ak, ak + 128), np.arange(av, av + 128),
                         np.arange(dbeta, dbeta + 8), np.arange(dalpha, dalpha + 8)])
    assert fm.size == NFM * 128 and tm.size == NTM
    return fm, tm


def pk(w):
    r = w.shape[0] // 128
    return np.ascontiguousarray(w.reshape(r, 128, *w.shape[1:]).swapaxes(0, 1))


def host_weights(inp, L):
    f = np.float32
    fm, tm = fm_cols()
    W = {}
    W["mod_w"] = np.stack([pk(inp["mod_w"][l]) for l in range(L)])
    W["mod_b"] = np.stack([np.ascontiguousarray(inp["mod_b"][l].reshape(48, 128).T) for l in range(L)])
    W["n1w"] = np.stack([np.ascontiguousarray(inp["norm1_w"][l].reshape(8, 128).T) for l in range(L)])
    W["n2w"] = np.stack([np.ascontiguousarray(inp["norm2_w"][l].reshape(8, 128).T) for l in range(L)])
    wfm = []
    for l in range(L):
        w = inp["w_in"][l][:, fm]
        w = w.reshape(8, 128, NFM, 128).transpose(2, 1, 0, 3)
        wfm.append(np.ascontiguousarray(w))
    W["w_fm"] = np.stack(wfm)
    W["w_tm"] = np.stack([pk(np.ascontiguousarray(inp["w_in"][l][:, tm])) for l in range(L)])
    W["dn_cw"] = np.stack([np.ascontiguousarray(inp["dn_conv_w"][l].reshape(4, 12, 128).transpose(2, 1, 0)) for l in range(L)])
    W["dn_ab"] = np.stack([np.broadcast_to(np.stack([inp["dn_a_log"][l].reshape(8), inp["dn_dt_bias"][l].reshape(8)])[None], (64, 2, 8)).copy() for l in range(L)])
    W["dn_nw"] = np.ascontiguousarray(inp["dn_norm_w"][:L].reshape(L, 128, 1))
    W["hg_nw"] = np.ascontiguousarray(inp["hg_norm_w"][:L].reshape(L, 128, 1))
    W["hg_lb"] = np.ascontiguousarray(inp["hg_lower_bounds"][:, :L].reshape(2, L, 4, 128).transpose(3, 0, 1, 2))
    W["lru_cw"] = np.stack([np.ascontiguousarray(inp["lru_conv_w"][l].reshape(4, 4, 128).transpose(2, 1, 0)) for l in range(L)])
    W["lru_cb"] = np.stack([np.ascontiguousarray(inp["lru_conv_b"][l].reshape(4, 128).T) for l in range(L)])
    W["lru_wa"] = np.stack([np.ascontiguousarray(inp["lru_w_a"][l].transpose(2, 0, 1, 3)) for l in range(L)])
    W["lru_wx"] = np.stack([np.ascontiguousarray(inp["lru_w_x"][l].transpose(2, 0, 1, 3)) for l in range(L)])
    W["lru_v"] = np.stack([np.ascontiguousarray(np.stack([inp["lru_b_a"][l], inp["lru_b_x"][l], inp["lru_lambda"][l]]).reshape(3, 2, 4, 128).transpose(3, 0, 1, 2)) for l in range(L)])
    W["att_nw"] = np.stack([np.broadcast_to(np.stack([inp["att_q_norm_w"][l], inp["att_k_norm_w"][l]])[None], (128, 2, 64)).copy() for l in range(L)])
    W["w_br"] = np.stack([np.ascontiguousarray(inp["w_branch"][l].reshape(4, 4, 128, D).transpose(2, 0, 1, 3)) for l in range(L)])
    W["w_out"] = np.stack([pk(inp["w_out"][l]) for l in range(L)])
    W["w1"] = np.stack([pk(inp["mlp_w1"][l]) for l in range(L)])
    W["w2"] = np.stack([pk(inp["mlp_w2"][l]) for l in range(L)])
    return {k_: np.ascontiguousarray(v, dtype=f) for k_, v in W.items()}


def run(inp, cfg, n_cores):
    key = (cfg.T, cfg.C, cfg.NB, cfg.L, cfg.debug)
    if key not in _CACHE:
        _CACHE[key] = build(cfg)
    nc, mk = _CACHE[key]
    inp = {k_: np.asarray(v, dtype=np.float32) for k_, v in inp.items()}
    W = host_weights(inp, cfg.L)
    W.update(host_consts(cfg))
    in_maps = []
    NB = cfg.NB
    for ci in range(n_cores):
        bs = slice(ci * NB, (ci + 1) * NB)
        m = dict(W)
        m["x_in"] = np.ascontiguousarray(np.concatenate([inp["ctx"][bs], inp["x"][bs]], axis=1))
        cv = np.concatenate([inp["c"][bs], inp["c_ctx"][None]], axis=0)
        m["cT"] = np.ascontiguousarray(cv.T.reshape(8, 128, NB + 1).transpose(1, 0, 2))
        in_maps.append(m)
    res = run_bass_kernel_spmd(nc, in_maps, core_ids=list(range(n_cores)))
    return res


def kernel(**inputs):
    cfg = Cfg(T=4096, C=256, NB=2, L=4)
    res = run(inputs, cfg, 8)
    return np.concatenate([r["out"] for r in res.results], axis=0).astype(np.float32)
```

```python
import numpy as np
import concourse.bass as bass
import concourse.mybir as mybir
from concourse.bass_utils import run_bass_kernel_spmd

F32 = mybir.dt.float32
BF16 = mybir.dt.bfloat16
AF = mybir.ActivationFunctionType
ALU = mybir.AluOpType
AX = mybir.AxisListType

SAME_ENGINE_SYNC = True


class StopBuild(Exception):
    pass


class Trk:
    __slots__ = ("w", "r_eng", "r_dma", "excl")

    def __init__(self):
        self.excl = False
        self.w = None
        self.r_eng = {}
        self.r_dma = {}


class V:
    __slots__ = ("ap", "trks", "t")

    def __init__(self, ap, trks, t=None):
        self.ap = ap
        self.trks = trks
        self.t = t

    def __getitem__(self, idx):
        return V(self.ap[idx], self.trks, self.t)

    def rr(self, pat, **kw):
        return V(self.ap.rearrange(pat, **kw), self.trks, self.t)

    def bc(self, shape):
        return V(self.ap.broadcast_to(shape) if hasattr(self.ap, "broadcast_to") else self.ap.to_broadcast(shape), self.trks)


class T:
    def __init__(self, handle, name, is_dram=True):
        self.h = handle
        self.name = name
        self.trk = Trk()
        self.parts = {}
        self.is_dram = is_dram
        self.dsem = None

    def ap(self):
        h = self.h
        return h.ap() if hasattr(h, "ap") else h[:]

    def __getitem__(self, idx):
        return V(self.h[idx], [self.trk], self)

    def p(self, key, idx=None):
        if key not in self.parts:
            self.parts[key] = Trk()
        a = self.h[idx] if idx is not None else self.ap()
        return V(a, [self.parts[key]], self)


class MK:
    def __init__(self, nc, es):
        self.nc = nc
        self.es = es
        self.E = {"pe": nc.tensor, "act": nc.scalar, "dve": nc.vector, "pool": nc.gpsimd, "sp": nc.sync}
        self.esem = {}
        self.cnt = {}
        self.seen = {}
        self.clock = {}
        for e in ("pe", "act", "dve", "pool"):
            self.esem[e] = es.enter_context(nc.semaphore("es_" + e))
            self.cnt[e] = 0
            self.clock[e] = [None]
        for e in self.E:
            self.seen[e] = {}
        self.dsem = {}
        self.dcnt = {}
        self.nwait = 0
        self.ninst = 0
        self.sem_es = es
        self.sem_ptr = 0
        self.limit = None

    def sb(self, name, shape, dt=F32):
        self.uid = getattr(self, "uid", 0) + 1
        h = self.es.enter_context(self.nc.sbuf_tensor("%s_u%d" % (name, self.uid), list(shape), dt))
        return T(h, name, False)

    def ps(self, name, shape, dt=F32):
        h = self.es.enter_context(self.nc.psum_tensor(name, list(shape), dt))
        t = T(h, name, False)
        t.trk.excl = True
        return t

    def dram(self, name, shape, dt=F32, kind="Internal"):
        h = self.nc.dram_tensor(name, list(shape), dt, kind=kind)
        return T(h, name)

    def get_dsem(self, name):
        if name not in self.dsem:
            self.dsem[name] = self.sem_es.enter_context(self.nc.semaphore("ds_" + name))
            self.dcnt[name] = 0
        return self.dsem[name]

    def _wait_event(self, eng, ev):
        if ev is None:
            return
        seen = self.seen[eng]
        if ev[0] == "e":
            _, e2, k = ev
            if e2 == eng and (eng == "pe" or not SAME_ENGINE_SYNC):
                return
            if seen.get(e2, 0) >= k:
                return
            self.E[eng].wait_ge(self.esem[e2], k)
            self.nwait += 1
            seen[e2] = k
            clk = self.clock[e2][k]
            if clk:
                for kk, vv in clk.items():
                    if seen.get(kk, 0) < vv:
                        seen[kk] = vv
        else:
            _, sname, val = ev
            key = "d:" + sname
            if seen.get(key, 0) >= val:
                return
            self.E[eng].wait_ge(self.dsem[sname], val)
            self.nwait += 1
            seen[key] = val

    def _deps(self, eng, outs, ins):
        for v in ins:
            for t in v.trks:
                self._wait_event(eng, t.w)
                if t.excl:
                    for e2, k in t.r_eng.items():
                        if e2 != eng:
                            self._wait_event(eng, ("e", e2, k))
        for v in outs:
            for t in v.trks:
                self._wait_event(eng, t.w)
                for e2, k in t.r_eng.items():
                    self._wait_event(eng, ("e", e2, k))
                for s, val in t.r_dma.items():
                    self._wait_event(eng, ("d", s, val))

    def I(self, eng, meth, **kw):
        outs, ins = [], []
        args = {}
        for k, v in kw.items():
            if isinstance(v, V):
                (outs if k in ("out", "accum_out") else ins).append(v)
                args[k] = v.ap
            else:
                args[k] = v
        if self.limit is not None and self.ninst >= self.limit:
            return None
        self._deps(eng, outs, ins)
        inst = getattr(self.E[eng], meth)(**args)
        self.cnt[eng] += 1
        k = self.cnt[eng]
        inst.then_inc(self.esem[eng], 1)
        self.ninst += 1
        snap = dict(self.seen[eng])
        self.clock[eng].append(snap)
        ev = ("e", eng, k)
        for v in outs:
            for t in v.trks:
                t.w = ev
                t.r_eng = {}
                t.r_dma = {}
        for v in ins:
            for t in v.trks:
                if t.r_eng.get(eng, 0) < k:
                    t.r_eng[eng] = k
        return inst

    def dma(self, q, out, in_, sem=None, **kw):
        if sem is None:
            st = out.t if (out.t is not None and not out.t.is_dram) else in_.t
            if st.dsem is None:
                st.dsem = "D%d" % self.sem_ptr
                self.sem_ptr += 1
            sem = st.dsem
        if self.limit is not None and self.ninst >= self.limit:
            return None
        s = self.get_dsem(sem)
        self._deps(q, [out], [in_])
        inst = self.E[q].dma_start(out=out.ap, in_=in_.ap, **kw)
        self.dcnt[sem] += 16
        val = self.dcnt[sem]
        inst.then_inc(s, 16)
        self.ninst += 1
        ev = ("d", sem, val)
        for t in out.trks:
            t.w = ev
            t.r_eng = {}
            t.r_dma = {}
        for t in in_.trks:
            if t.r_dma.get(sem, 0) < val:
                t.r_dma[sem] = val
        return inst

    def barrier(self):
        for eng in self.E:
            for e2 in self.esem:
                if self.cnt[e2] > 0:
                    self._wait_event(eng, ("e", e2, self.cnt[e2]))
            for s, val in self.dcnt.items():
                if val > 0:
                    self._wait_event(eng, ("d", s, val))

    def finish(self, eng="sp"):
        for s, val in self.dcnt.items():
            if val > 0:
                self._wait_event(eng, ("d", s, val))

    def mm(self, out, lhsT, rhs, start=True, stop=True, **kw):
        return self.I("pe", "matmul", out=out, lhsT=lhsT, rhs=rhs, start=start, stop=stop, **kw)

    def tr(self, out, in_, identity):
        return self.I("pe", "transpose", out=out, in_=in_, identity=identity)

    def act(self, out, in_, func, eng="act", **kw):
        return self.I("act", "activation", out=out, in_=in_, func=func, **kw)

    def ts(self, eng, out, in0, s1, s2=None, op0=ALU.mult, op1=None, **kw):
        if op1 is None:
            return self.I(eng, "tensor_scalar", out=out, in0=in0, scalar1=s1, scalar2=s2, op0=op0, **kw)
        return self.I(eng, "tensor_scalar", out=out, in0=in0, scalar1=s1, scalar2=s2, op0=op0, op1=op1, **kw)

    def tt(self, eng, out, in0, in1, op):
        return self.I(eng, "tensor_tensor", out=out, in0=in0, in1=in1, op=op)

    def stt(self, out, in0, scalar, in1, op0, op1, eng="dve"):
        return self.I(eng, "scalar_tensor_tensor", out=out, in0=in0, scalar=scalar, in1=in1, op0=op0, op1=op1)

    def cp(self, eng, out, in_):
        if eng == "act":
            return self.I("act", "copy", out=out, in_=in_)
        return self.I(eng, "tensor_copy", out=out, in_=in_)

    def memset(self, eng, out, val):
        return self.I(eng, "memset", ap=None, out=out, constant=val) if False else self._memset(eng, out, val)

    def _memset(self, eng, out, val):
        if self.limit is not None and self.ninst >= self.limit:
            return None
        self._deps(eng, [out], [])
        inst = self.E[eng].memset(out.ap, val)
        self.cnt[eng] += 1
        k = self.cnt[eng]
        inst.then_inc(self.esem[eng], 1)
        self.ninst += 1
        self.clock[eng].append(dict(self.seen[eng]))
        for t in out.trks:
            t.w = ("e", eng, k)
            t.r_eng = {}
            t.r_dma = {}
        return inst

from contextlib import ExitStack, contextmanager

EPS = 1e-6
STQ = "pool"
D = 1024
NFM = 72
NTM = 1296
DNQ, DNK, DNV, DNG, HGQ, HGFF, HGFB, HGG, LRX, LRG, MGB = 0, 4, 8, 12, 16, 20, 24, 28, 32, 36, 40
TM_HGI, TM_ATQ, TM_ATK, TM_ATV, TM_BETA, TM_ALPHA = 0, 512, 1024, 1152, 1280, 1288


def vus(v, axis):
    return V(v.ap.unsqueeze(axis), v.trks)


def vbc(v, shape):
    return V(v.ap.broadcast_to(list(shape)), v.trks)


class Cfg:
    def __init__(self, T=4096, C=256, NB=2, L=4, debug=False):
        self.T, self.C, self.NB, self.L = T, C, NB, L
        self.NT = T + C
        self.debug = debug
        self.groups = []
        t = 0
        while t < C:
            n = min(512, C - t)
            self.groups.append((t, n, 0))
            t += n
        while t < self.NT:
            n = min(512, self.NT - t)
            self.groups.append((t, n, 1))
            t += n
        self.ntile = self.NT // 128
        self.nch = self.NT // 64
        self.cch = C // 64

    def gi_of(self, t):
        for i, (t0, n, s) in enumerate(self.groups):
            if t0 <= t < t0 + n:
                return i
        raise ValueError

    def order(self, d):
        if d == 0:
            return list(range(self.nch))
        return list(range(self.cch - 1, -1, -1)) + list(range(self.nch - 1, self.cch - 1, -1))


class K:
    pass


@contextmanager
def scope(mk):
    old = mk.es
    ptr = mk.sem_ptr
    with ExitStack() as es:
        mk.es = es
        try:
            yield es
        finally:
            lim, mk.limit = mk.limit, None
            mk.barrier()
            mk.limit = lim
            mk.es = old
            mk.sem_ptr = ptr


def build(cfg):
    nc = bass.Bass("TRN2", target_bir_lowering=False)
    L, NB, NT, C = cfg.L, cfg.NB, cfg.NT, cfg.C
    root = ExitStack()
    mk = MK(nc, root)
    k = K()
    k.cfg, k.mk, k.nc = cfg, mk, nc
    T_ = T

    def din(name, shape):
        return T_(nc.dram_tensor(name, list(shape), F32, kind="ExternalInput"), name)

    k.x_in = din("x_in", [NB, NT, D])
    k.cT = din("cT", [128, 8, NB + 1])
    k.mod_w = din("mod_w", [L, 128, 8, 6 * D])
    k.mod_b = din("mod_b", [L, 128, 48])
    k.n1w = din("n1w", [L, 128, 8])
    k.n2w = din("n2w", [L, 128, 8])
    k.w_fm = din("w_fm", [L, NFM, 128, 8, 128])
    k.w_tm = din("w_tm", [L, 128, 8, NTM])
    k.dn_cw = din("dn_cw", [L, 128, 12, 4])
    k.dn_ab = din("dn_ab", [L, 64, 2, 8])
    k.dn_nw = din("dn_nw", [L, 128, 1])
    k.hg_nw = din("hg_nw", [L, 128, 1])
    k.hg_lb = din("hg_lb", [128, 2, L, 4])
    k.lru_cw = din("lru_cw", [L, 128, 4, 4])
    k.lru_cb = din("lru_cb", [L, 128, 4])
    k.lru_wa = din("lru_wa", [L, 128, 2, 4, 128])
    k.lru_wx = din("lru_wx", [L, 128, 2, 4, 128])
    k.lru_v = din("lru_v", [L, 128, 3, 2, 4])
    k.att_nw = din("att_nw", [L, 128, 2, 64])
    k.w_br = din("w_br", [L, 128, 4, 4, D])
    k.w_out = din("w_out", [L, 128, 8, D])
    k.w1 = din("w1", [L, 128, 8, 4 * D])
    k.w2 = din("w2", [L, 128, 32, D])
    k.c_ident = din("c_ident", [128, 128])
    k.c_masks = din("c_masks", [64, 4, 64])
    k.c_rope = din("c_rope", [cfg.T, 2, 2, 16])
    k.out = T_(nc.dram_tensor("out", [NB, cfg.T, D], F32, kind="ExternalOutput"), "out")

    kind = "ExternalOutput" if cfg.debug else "Internal"

    def dsc(name, shape):
        return T_(nc.dram_tensor(name, list(shape), F32, kind=kind), name)

    k.hT = [dsc(f"hT{b}", [D, NT]) for b in range(NB)]
    k.uT = [dsc(f"uT{b}", [NFM * 128, NT]) for b in range(NB)]
    k.utok = [dsc(f"utok{b}", [NT, NTM]) for b in range(NB)]
    k.osc = [[dsc(f"osc{b}_{i}", [NT, 512]) for i in range(4)] for b in range(NB)]
    k.yT = [[dsc(f"yT{b}_{i}", [512, NT]) for i in range(4)] for b in range(NB)]

    k.PS = [mk.ps(f"psb{i}", [128, 512]) for i in range(8)]
    k.ident = mk.sb("ident", [128, 128])
    k.identb = mk.sb("identb", [128, 128], BF16)
    k.ones = mk.sb("ones", [128, 128])
    k.masks = mk.sb("masks", [64, 4, 64])
    k.scT = mk.sb("scT", [128, 8, NB + 1])
    k.modT = mk.sb("modT", [128, 48, NB + 1])
    k.A1 = mk.sb("A1", [128, 8, NB + 1])
    k.A2 = mk.sb("A2", [128, 8, NB + 1])
    k.lb = mk.sb("lb", [128, 2, L, 4])
    mk.dma("sp", k.ident[:], k.c_ident[:, :])
    mk.dma("sp", k.masks[:], k.c_masks[:, :, :])
    mk.dma("sp", k.scT[:], k.cT[:, :, :])
    mk.dma("sp", k.lb[:], k.hg_lb[:, :, :, :])
    mk._memset("dve", k.ones[:], 1.0)
    mk.cp("dve", k.identb[:], k.ident[:])
    mk.act(k.scT[:], k.scT[:], AF.Silu)
    prep_lb(k)

    mk.limit = getattr(cfg, "limit", None)
    try:
        _stages(k, cfg, mk, L, NB)
    except StopBuild:
        pass
    mk.limit = None
    mk.barrier()
    root.close()
    return nc, mk


def _stages(k, cfg, mk, L, NB):
    stage_in(k)
    for l in range(L):
        need_ctx = l < L - 1
        stage_mods(k, l)
        for b in range(NB):
            stage_s1(k, l, b)
        if cfg.debug == "s1":
            break
        only = getattr(cfg, "only", None)
        for b in range(NB):
            if only is None or "lru" in only:
                stage_lru(k, l, b)
            if only is None or "att" in only:
                stage_att(k, l, b, need_ctx)
            if only is None or "dn" in only:
                stage_dn(k, l, b)
                if not getattr(cfg, "stop", ""):
                    stage_post(k, l, b, 0, k.dn_nw, DNG)
            if only is None or "hg" in only:
                stage_hg(k, l, b)
                stage_post(k, l, b, 1, k.hg_nw, HGG)
        if cfg.debug == "mix":
            break
        stage_s3a(k, l, need_ctx)
        stage_s3b(k, l, need_ctx)
    stage_out(k)


def prep_lb(k):
    mk, L = k.mk, k.cfg.L
    with scope(mk):
        e = mk.sb("lb_e", [128, 2, L, 4])
        s = mk.sb("lb_s", [128, 2, 4])
        mk.act(e[:], k.lb[:], AF.Exp)
        mk.cp("dve", s[:], e[:, :, 0, :])
        for l in range(1, L):
            mk.tt("dve", s[:], s[:], e[:, :, l, :], ALU.add)
        mk.I("dve", "reciprocal", out=s[:], in_=s[:])
        mk.tt("dve", e[:], e[:], vbc(vus(s[:], 2), [128, 2, L, 4]), ALU.mult)
        mk._memset("dve", k.lb[:, :, 0, :], 0.0)
        for l in range(1, L):
            mk.tt("dve", k.lb[:, :, l, :], k.lb[:, :, l - 1, :], e[:, :, l, :], ALU.add)


def hT_view(k, b):
    return V(k.hT[b].ap().rearrange("(kt p) t -> p kt t", p=128), [k.hT[b].trk])


def stage_in(k):
    mk, cfg = k.mk, k.cfg
    with scope(mk):
        xt = [mk.sb(f"in_x{i}", [128, D]) for i in range(2)]
        st = [mk.sb(f"in_s{i}", [128, 8, 128]) for i in range(2)]
        i = 0
        for b in range(cfg.NB):
            hv = hT_view(k, b)
            for ti in range(cfg.ntile):
                x, s = xt[i % 2], st[i % 2]
                mk.dma("sp", x[:], k.x_in[b, ti * 128:(ti + 1) * 128, :])
                for half in range(2):
                    ps = k.PS[half]
                    for q in range(4):
                        mk.tr(ps[:, q * 128:(q + 1) * 128], x[:, (half * 4 + q) * 128:(half * 4 + q + 1) * 128], k.ident[:])
                    mk.cp("act" if half else "dve", s[:, half * 4:half * 4 + 4, :], ps[:].rr("p (a b) -> p a b", a=4))
                mk.dma(STQ, hv[:, :, ti * 128:(ti + 1) * 128], s[:])
                i += 1


def stage_out(k):
    mk, cfg = k.mk, k.cfg
    with scope(mk):
        ht = [mk.sb(f"out_h{i}", [128, 8, 128]) for i in range(2)]
        st = [mk.sb(f"out_s{i}", [128, D]) for i in range(2)]
        i = 0
        for b in range(cfg.NB):
            hv = hT_view(k, b)
            for ti in range(cfg.T // 128):
                h, s = ht[i % 2], st[i % 2]
                t0 = cfg.C + ti * 128
                mk.dma("sp", h[:], hv[:, :, t0:t0 + 128])
                for half in range(2):
                    ps = k.PS[half]
                    for q in range(4):
                        mk.tr(ps[:, q * 128:(q + 1) * 128], h[:, half * 4 + q, :], k.ident[:])
                    mk.cp("act" if half else "dve", s[:, half * 512:(half + 1) * 512], ps[:])
                mk.dma(STQ, k.out[b, ti * 128:(ti + 1) * 128, :], s[:])
                i += 1


def stage_mods(k, l):
    mk, cfg = k.mk, k.cfg
    NJ = cfg.NB + 1
    with scope(mk):
        wp = [mk.sb(f"mod_w{i}", [128, 8, 512]) for i in range(2)]
        mb = mk.sb("mod_bt", [128, 48])
        nw = mk.sb("mod_nw", [128, 2, 8])
        mk.dma("sp", mb[:], k.mod_b[l])
        mk.dma("sp", nw[:, 0, :], k.n1w[l])
        mk.dma("sp", nw[:, 1, :], k.n2w[l])
        pm = k.PS[0]
        for pc in range(12):
            w = wp[pc % 2]
            mk.dma("sp", w[:], k.mod_w[l, :, :, pc * 512:(pc + 1) * 512])
            for j in range(4):
                o = (pc * 4 + j) * 4
                for kt in range(8):
                    mk.mm(pm[:, o:o + NJ], w[:, kt, j * 128:(j + 1) * 128], k.scT[:, kt, :], start=(kt == 0), stop=(kt == 7))
        mk.tt("dve", k.modT[:], pm[:, 0:192].rr("p (a b) -> p a b", b=4)[:, :, 0:NJ], vbc(vus(mb[:], 2), [128, 48, NJ]), ALU.add)
        for (A, w_i, sc0) in ((k.A1, 0, 8), (k.A2, 1, 32)):
            mk.ts("dve", A[:], k.modT[:, sc0:sc0 + 8, :], 1.0, None, op0=ALU.add)
            mk.tt("dve", A[:], A[:], vbc(vus(nw[:, w_i, :], 2), [128, 8, NJ]), ALU.mult)


def rms_mod(k, hg, n, j, A, sh0, zt_out, tmp, sq, rs, ps):
    mk = k.mk
    mk.act(sq[:, :, :n], hg[:, :, :n], AF.Square)
    for kt in range(8):
        mk.mm(ps[:, :n], k.ones[:], sq[:, kt, :n], start=(kt == 0), stop=(kt == 7))
    mk.act(rs[:, :n], ps[:, :n], AF.Sqrt, scale=1.0 / D, bias=EPS)
    mk.I("dve", "reciprocal", out=rs[:, :n], in_=rs[:, :n])
    mk.tt("dve", tmp[:, :, :n], hg[:, :, :n], vbc(vus(rs[:, :n], 1), [128, 8, n]), ALU.mult)
    for kt in range(8):
        if kt % 2 == 0:
            mk.ts("dve", zt_out(kt), tmp[:, kt, :n], A[:, kt, j:j + 1], k.modT[:, sh0 + kt, j:j + 1], op0=ALU.mult, op1=ALU.add)
        else:
            mk.act(zt_out(kt), tmp[:, kt, :n], AF.Identity, scale=A[:, kt, j:j + 1], bias=k.modT[:, sh0 + kt, j:j + 1])


def stage_s1(k, l, b):
    mk, cfg = k.mk, k.cfg
    NT = cfg.NT
    hv = hT_view(k, b)
    with scope(mk):
        ZT = mk.sb("s1_zt", [128, 8, NT], BF16)
        with scope(mk):
            hg = [mk.sb(f"s1_h{i}", [128, 8, 512]) for i in range(2)]
            sq = mk.sb("s1_sq", [128, 8, 512])
            tmp = mk.sb("s1_tmp", [128, 8, 512])
            rs = mk.sb("s1_rs", [128, 512])
            for gi, (t0, n, seg) in enumerate(cfg.groups):
                h = hg[gi % 2]
                mk.dma("sp", h[:, :, :n], hv[:, :, t0:t0 + n])
                j = cfg.NB if seg == 0 else b
                rms_mod(k, h, n, j, k.A1, 0, lambda kt: ZT.p(gi, (slice(None), kt, slice(t0, t0 + n))), tmp, sq, rs, k.PS[gi % 2])
        with scope(mk):
            wf = [mk.sb(f"s1_wf{i}", [128, 8, 128]) for i in range(2)]
            wb = [mk.sb(f"s1_wb{i}", [128, 8, 128], BF16) for i in range(2)]
            stg = [mk.sb(f"s1_st{i}", [128, NT]) for i in range(2)]
            ev = 0
            for blk in range(NFM):
                s = blk % 2
                mk.dma("sp", wf[s][:], k.w_fm[l, blk])
                mk.cp("pool", wb[s][:], wf[s][:])
                for gi, (t0, n, seg) in enumerate(cfg.groups):
                    ps = k.PS[ev % 4]
                    for kt in range(8):
                        mk.mm(ps[:, :n], wb[s][:, kt, :], ZT.p(gi, (slice(None), kt, slice(t0, t0 + n))), start=(kt == 0), stop=(kt == 7))
                    mk.cp("act" if ev % 2 else "dve", stg[s].p(gi, (slice(None), slice(t0, t0 + n))), ps[:, :n])
                    ev += 1
                allp = V(stg[s].ap(), [stg[s].parts[g] for g in range(len(cfg.groups))], stg[s])
                mk.dma(STQ, k.uT[b].p(blk, (slice(blk * 128, (blk + 1) * 128), slice(None))), allp)
        with scope(mk):
            wtm = mk.sb("s1_wtm", [128, 8, NTM], BF16)
            wst = [mk.sb(f"s1_wst{i}", [128, 8, 432]) for i in range(2)]
            stt_ = [mk.sb(f"s1_stt{i}", [128, NTM]) for i in range(2)]
            for pc in range(3):
                mk.dma("sp", wst[pc % 2][:], k.w_tm[l, :, :, pc * 432:(pc + 1) * 432])
                mk.cp("pool", wtm.p(pc, (slice(None), slice(None), slice(pc * 432, (pc + 1) * 432))), wst[pc % 2][:])
            wall = lambda kt, c0, cn: V(wtm.h[:, kt, c0:c0 + cn], [wtm.parts[p] for p in range(3)])
            ev = 0
            for ti in range(cfg.ntile):
                gi = cfg.gi_of(ti * 128)
                s = ti % 2
                for (c0, cn) in ((0, 512), (512, 512), (1024, NTM - 1024)):
                    ps = k.PS[ev % 4]
                    for kt in range(8):
                        mk.mm(ps[:, :cn], ZT.p(gi, (slice(None), kt, slice(ti * 128, (ti + 1) * 128))), wall(kt, c0, cn), start=(kt == 0), stop=(kt == 7))
                    mk.cp("act" if ev % 2 else "dve", stt_[s][:, c0:c0 + cn], ps[:, :cn])
                    ev += 1
                mk.dma(STQ, k.utok[b].p(ti, (slice(ti * 128, (ti + 1) * 128), slice(None))), stt_[s][:])


def uT_rows(k, b, blk):
    return k.uT[b].p(blk, (slice(blk * 128, (blk + 1) * 128), slice(None)))


def conv4(mk, out, x, w, segs):
    for (s0, sn) in segs:
        e = s0 + sn
        mk.ts("dve", out[:, s0:e], x[:, s0:e], w[:, 1:2], None, op0=ALU.mult)
        mk.stt(out[:, s0 + 1:e], x[:, s0:e - 1], w[:, 0:1], out[:, s0 + 1:e], ALU.mult, ALU.add)
        mk.stt(out[:, s0:e - 1], x[:, s0 + 1:e], w[:, 2:3], out[:, s0:e - 1], ALU.mult, ALU.add)
        mk.stt(out[:, s0:e - 2], x[:, s0 + 2:e], w[:, 3:4], out[:, s0:e - 2], ALU.mult, ALU.add)


def softplus(mk, out, x, t1):
    mk.act(t1, x, AF.Abs)
    mk.act(t1, t1, AF.Exp, scale=-1.0)
    mk.act(t1, t1, AF.Ln, bias=1.0)
    mk.stt(out, x, 0.0, t1, ALU.max, ALU.add)


def stage_lru(k, l, b):
    mk, cfg = k.mk, k.cfg
    NT, C = cfg.NT, cfg.C
    segs = [(0, C), (C, cfg.T)]
    with scope(mk):
        cw = mk.sb("lr_cw", [128, 4, 4]); cb = mk.sb("lr_cb", [128, 4])
        wa = mk.sb("lr_wa", [128, 2, 4, 128]); wx = mk.sb("lr_wx", [128, 2, 4, 128])
        vv = mk.sb("lr_v", [128, 3, 2, 4]); csp = mk.sb("lr_csp", [128, 2, 4]); t1 = mk.sb("lr_t1", [128, 2, 4])
        mk.dma("sp", cw[:], k.lru_cw[l]); mk.dma("sp", cb[:], k.lru_cb[l])
        mk.dma("sp", wa[:], k.lru_wa[l]); mk.dma("sp", wx[:], k.lru_wx[l])
        mk.dma("sp", vv[:], k.lru_v[l])
        mk.ts("dve", csp[:], vv[:, 2], -1.0, None, op0=ALU.mult)
        softplus(mk, csp[:], csp[:], t1[:])
        mk.ts("dve", csp[:], csp[:], -8.0, None, op0=ALU.mult)
        xb = mk.sb("lr_xb", [128, NT]); gb = mk.sb("lr_gb", [128, NT]); xc = mk.sb("lr_xc", [128, NT])
        R = mk.sb("lr_r", [128, NT]); IG = mk.sb("lr_ig", [128, NT]); A = mk.sb("lr_a", [128, NT])
        H = [mk.sb(f"lr_h{d}", [128, NT]) for d in range(2)]
        ev = 0
        for h in range(4):
            mk.dma("sp", xb[:], uT_rows(k, b, LRX + h))
            mk.dma("sp", gb[:], uT_rows(k, b, LRG + h))
            conv4(mk, xc, xb, cw[:, h, :], segs)
            mk.ts("dve", xc[:], xc[:], cb[:, h:h + 1], None, op0=ALU.add)
            for d in range(2):
                for (t0, n, seg) in cfg.groups:
                    pa, px = k.PS[ev % 2], k.PS[2 + ev % 2]
                    ev += 1
                    mk.mm(pa[:, :n], wa[:, d, h, :], xc[:, t0:t0 + n])
                    mk.mm(px[:, :n], wx[:, d, h, :], xc[:, t0:t0 + n])
                    mk.act(R[:, t0:t0 + n], pa[:, :n], AF.Sigmoid, bias=vv[:, 0, d, h:h + 1])
                    mk.act(IG[:, t0:t0 + n], px[:, :n], AF.Sigmoid, bias=vv[:, 1, d, h:h + 1])
                mk.act(A[:], R[:], AF.Exp, scale=csp[:, d, h:h + 1])
                mk.tt("dve", R[:], A[:], A[:], ALU.mult)
                mk.ts("dve", R[:], R[:], -1.0, 1.0, op0=ALU.mult, op1=ALU.add)
                mk.ts("dve", R[:], R[:], 0.0, None, op0=ALU.max)
                mk.act(R[:], R[:], AF.Sqrt)
                mk.tt("dve", R[:], R[:], IG[:], ALU.mult)
                mk.tt("dve", R[:], R[:], xc[:], ALU.mult)
                if d == 0:
                    mk.I("dve", "tensor_tensor_scan", out=H[0][:], data0=A[:], data1=R[:], initial=0.0, op0=ALU.mult, op1=ALU.add)
                else:
                    mk.I("dve", "tensor_tensor_scan", out=H[1][:, C - 1::-1] if False else V(H[1].h[:, 0:C][:, ::-1], [H[1].trk]),
                         data0=V(A.h[:, 0:C][:, ::-1], [A.trk]), data1=V(R.h[:, 0:C][:, ::-1], [R.trk]),
                         initial=0.0, op0=ALU.mult, op1=ALU.add)
                    mk.I("dve", "tensor_tensor_scan", out=V(H[1].h[:, C:NT][:, ::-1], [H[1].trk]),
                         data0=V(A.h[:, C:NT][:, ::-1], [A.trk]), data1=V(R.h[:, C:NT][:, ::-1], [R.trk]),
                         initial=H[1][:, 0:1], op0=ALU.mult, op1=ALU.add)
            mk.tt("dve", H[0][:], H[0][:], H[1][:], ALU.add)
            mk.act(gb[:], gb[:], AF.Gelu)
            mk.tt("dve", H[0][:], H[0][:], gb[:], ALU.mult)
            mk.dma(STQ, k.yT[b][2].p(h, (slice(h * 128, (h + 1) * 128), slice(None))), H[0][:])


def stage_att(k, l, b, need_ctx):
    mk, cfg = k.mk, k.cfg
    NT, C, ntile = cfg.NT, cfg.C, cfg.ntile
    with scope(mk):
        QT = mk.sb("at_qt", [64, 8, NT], BF16)
        KT = mk.sb("at_kt", [64, 2, NT], BF16)
        VA = mk.sb("at_va", [128, ntile, 2, 65], BF16)
        nw = mk.sb("at_nw", [128, 2, 64])
        mk.dma("sp", nw[:], k.att_nw[l])
        mk._memset("pool", VA[:], 1.0)
        with scope(mk):
            a_ = [mk.sb(f"at_a{i}", [128, 768]) for i in range(2)]
            cs_ = [mk.sb(f"at_cs{i}", [128, 2, 2, 16]) for i in range(2)]
            sq = mk.sb("at_sq", [128, 640]); ss = mk.sb("at_ss", [128, 10])
            qn = mk.sb("at_qn", [128, 640]); qr = mk.sb("at_qr", [128, 640])
            tt_ = [mk.sb(f"at_t{i}", [128, 10, 2, 16]) for i in range(4)]
            for ti in range(ntile):
                a = a_[ti % 2]
                mk.dma("sp", a[:], k.utok[b].p(ti, (slice(ti * 128, (ti + 1) * 128), slice(TM_ATQ, TM_ATQ + 768))))
                mk.tt("dve", sq[:], a[:, 0:640], a[:, 0:640], ALU.mult)
                mk.I("dve", "tensor_reduce", out=ss[:], in_=sq[:].rr("p (h d) -> p h d", d=64), axis=AX.X, op=ALU.add)
                mk.act(ss[:], ss[:], AF.Sqrt, scale=1.0 / 64, bias=EPS)
                mk.I("dve", "reciprocal", out=ss[:], in_=ss[:])
                mk.tt("dve", qn[:].rr("p (h d) -> p h d", d=64), a[:, 0:640].rr("p (h d) -> p h d", d=64), vbc(vus(ss[:], 2), [128, 10, 64]), ALU.mult)
                mk.tt("pool", qn[:, 0:512].rr("p (h d) -> p h d", d=64), qn[:, 0:512].rr("p (h d) -> p h d", d=64), vbc(vus(nw[:, 0, :], 1), [128, 8, 64]), ALU.mult)
                mk.tt("pool", qn[:, 512:640].rr("p (h d) -> p h d", d=64), qn[:, 512:640].rr("p (h d) -> p h d", d=64), vbc(vus(nw[:, 1, :], 1), [128, 2, 64]), ALU.mult)
                if ti * 128 >= C:
                    cs = cs_[ti % 2]
                    tl = ti * 128 - C
                    mk.dma("sp", cs[:], k.c_rope[tl:tl + 128])
                    xv = qn[:].rr("p (h a f r) -> p h a f r", h=10, a=2, f=2)
                    ov = qr[:].rr("p (h a f r) -> p h a f r", h=10, a=2, f=2)
                    x1, x2 = xv[:, :, :, 0, :], xv[:, :, :, 1, :]
                    cc = vbc(vus(cs[:, 0], 1), [128, 10, 2, 16]); sn = vbc(vus(cs[:, 1], 1), [128, 10, 2, 16])
                    mk.tt("dve", tt_[0][:], x1, cc, ALU.mult)
                    mk.tt("pool", tt_[1][:], x2, sn, ALU.mult)
                    mk.tt("dve", tt_[2][:], x2, cc, ALU.mult)
                    mk.tt("pool", tt_[3][:], x1, sn, ALU.mult)
                    mk.tt("dve", ov[:, :, :, 0, :], tt_[0][:], tt_[1][:], ALU.subtract)
                    mk.tt("dve", ov[:, :, :, 1, :], tt_[2][:], tt_[3][:], ALU.add)
                    src = qr
                else:
                    src = qn
                for hh in range(10):
                    ps = k.PS[hh // 4]
                    o = (hh % 4) * 128
                    mk.tr(ps[0:64, o:o + 128], src[:, hh * 64:(hh + 1) * 64], k.ident[:])
                mk.cp("act", QT[:, 0:4, ti * 128:(ti + 1) * 128], k.PS[0][0:64, :].rr("p (a b) -> p a b", a=4))
                mk.cp("dve", QT[:, 4:8, ti * 128:(ti + 1) * 128], k.PS[1][0:64, :].rr("p (a b) -> p a b", a=4))
                mk.cp("act", KT[:, 0:2, ti * 128:(ti + 1) * 128], k.PS[2][0:64, 0:256].rr("p (a b) -> p a b", a=2))
                mk.cp("pool", VA[:, ti, :, 0:64], a[:, 640:768].rr("p (g d) -> p g d", g=2))
        with scope(mk):
            OT = [mk.sb(f"at_ot{i}", [128, 512]) for i in range(2)]
            PT = [mk.sb(f"at_pt{i}", [128, 512], BF16) for i in range(3)]
            YS = [mk.sb(f"at_ys{i}", [128, 4, 128]) for i in range(2)]
            rc = mk.sb("at_rc", [128, 8])
            yv = V(k.yT[b][3].ap().rearrange("(ft p) t -> p ft t", p=128), [k.yT[b][3].trk])
            it = 0
            qtiles = list(range(ntile)) if need_ctx else list(range(C // 128, ntile))
            for qn_i, qi in enumerate(qtiles):
                keyt = list(range(C // 128)) if qi * 128 < C else list(range(ntile))
                ot = OT[qn_i % 2]
                items = [(g, kt) for g in range(2) for kt in keyt]

                def front(i, base=it):
                    g, kt = items[i]
                    S = k.PS[(0, 1, 7)[(base + i) % 3]]; pt = PT[(base + i) % 3]
                    mk.mm(S[:, :], KT[:, g, kt * 128:(kt + 1) * 128], QT[:, 4 * g:4 * g + 4, qi * 128:(qi + 1) * 128])
                    mk.act(pt[:], S[:], AF.Exp, scale=0.125)

                def back(i, base=it):
                    g, kt = items[i]
                    pt = PT[(base + i) % 3]
                    for hq in range(4):
                        mk.mm(k.PS[2 + hq][:, 0:65], pt[:, hq * 128:(hq + 1) * 128], VA[:, kt, g, :], start=(kt == keyt[0]), stop=(kt == keyt[-1]))
                    if kt == keyt[-1]:
                        for hq in range(4):
                            hh = 4 * g + hq
                            mk.I("dve", "reciprocal", out=rc[:, hh:hh + 1], in_=k.PS[2 + hq][:, 64:65])
                            mk.ts("dve", ot[:, hh * 64:(hh + 1) * 64], k.PS[2 + hq][:, 0:64], rc[:, hh:hh + 1], None, op0=ALU.mult)

                front(0)
                if len(items) > 1:
                    front(1)
                for i in range(len(items)):
                    if i + 2 < len(items):
                        front(i + 2)
                    back(i)
                it += len(items)
                for ft in range(4):
                    mk.tr(k.PS[6][:, ft * 128:(ft + 1) * 128], ot[:, ft * 128:(ft + 1) * 128], k.ident[:])
                ys = YS[qn_i % 2]
                mk.cp("act", ys[:], k.PS[6][:].rr("p (a b) -> p a b", a=4))
                mk.dma(STQ, yv[:, :, qi * 128:(qi + 1) * 128], ys[:])


def stage_dn(k, l, b):
    mk, cfg = k.mk, k.cfg
    NT, C, nch = cfg.NT, cfg.C, cfg.nch
    segs = [(0, C), (C, cfg.T)]
    LE, GE, GT, LT = 0, 1, 2, 3
    with scope(mk):
        graw = mk.sb("dn_graw", [64, nch, 16]); ab = mk.sb("dn_ab", [64, 2, 8])
        beta = mk.sb("dn_beta", [64, nch, 8]); g = mk.sb("dn_g", [64, nch, 8]); t1 = mk.sb("dn_t1", [64, nch, 8])
        gam = mk.sb("dn_gam", [64, nch, 8]); erg = mk.sb("dn_erg", [64, nch, 8]); cdec = mk.sb("dn_cdec", [128, nch, 8])
        eg = mk.sb("dn_eg", [64, nch, 8]); bg = mk.sb("dn_bg", [64, nch, 8]); ea = mk.sb("dn_ea", [64, 8])
        cw = mk.sb("dn_cw", [128, 12, 4])
        mk.dma("sp", cw[:], k.dn_cw[l])
        mk.dma("sp", ab[:], k.dn_ab[l])
        gsrc = k.utok[b].h[:, TM_BETA:TM_BETA + 16].rearrange("(n c) f -> c n f", c=64)
        for n0 in range(0, nch, 16):
            n1 = min(nch, n0 + 16)
            mk.dma("sp", graw[:, n0:n1, :], V(gsrc[:, n0:n1, :], [k.utok[b].trk] + list(k.utok[b].parts.values())))
        mk.act(beta[:], graw[:, :, 0:8], AF.Sigmoid)
        mk.tt("dve", g[:], graw[:, :, 8:16], vbc(vus(ab[:, 1, :], 1), [64, nch, 8]), ALU.add)
        softplus(mk, g[:], g[:], t1[:])
        mk.act(ea[:], ab[:, 0, :], AF.Exp)
        mk.tt("dve", g[:], g[:], vbc(vus(ea[:], 1), [64, nch, 8]), ALU.mult)
        mk.ts("dve", g[:], g[:], -1.0, None, op0=ALU.mult)
        for d in range(2):
            cs = slice(d * 4, d * 4 + 4)
            n4 = nch * 4
            mk.mm(k.PS[0][0:64, 0:n4], k.masks[:, LE if d == 0 else GE, :], g[:, :, cs])
            mk.cp("dve", gam[:, :, cs], k.PS[0][0:64, 0:n4].rr("p (n f) -> p n f", f=4))
            mk.mm(k.PS[1][0:64, 0:n4], k.masks[:, GT if d == 0 else LT, :], g[:, :, cs])
            mk.act(erg[:, :, cs], k.PS[1][0:64, 0:n4].rr("p (n f) -> p n f", f=4), AF.Exp)
            mk.mm(k.PS[2][:, 0:n4], k.ones[0:64, :], g[:, :, cs])
            mk.act(cdec[:, :, cs], k.PS[2][:, 0:n4].rr("p (n f) -> p n f", f=4), AF.Exp)
        mk.act(eg[:], gam[:], AF.Exp)
        mk.tt("dve", bg[:], beta[:], eg[:], ALU.mult)
        Qb = [mk.sb(f"dn_q{i}", [128, NT], BF16) for i in range(4)]
        Kb = [mk.sb(f"dn_k{i}", [128, NT], BF16) for i in range(4)]
        Vb = [mk.sb(f"dn_v{i}", [128, NT], BF16) for i in range(4)]
        with scope(mk):
            raws = [mk.sb(f"dn_raw{i}", [128, NT]) for i in range(2)]
            tmpc = mk.sb("dn_tmpc", [128, NT])
            sq = mk.sb("dn_sq", [128, 512]); rs = mk.sb("dn_rs", [128, 512])
            ri = 0
            for h in range(4):
                for (dst, blk0, wi, scl) in ((Qb[h], DNQ, 0, 128.0 ** -0.5), (Kb[h], DNK, 1, 1.0), (Vb[h], DNV, 2, None)):
                    raw = raws[ri % 2]; ri += 1
                    mk.dma("sp", raw[:], uT_rows(k, b, blk0 + h))
                    conv4(mk, tmpc, raw, cw[:, wi * 4 + h, :], segs)
                    mk.act(tmpc[:], tmpc[:], AF.Silu)
                    if scl is None:
                        mk.cp("pool", dst[:], tmpc[:])
                        continue
                    for gi, (t0, n, seg) in enumerate(cfg.groups):
                        ps = k.PS[gi % 2]
                        mk.act(sq[:, :n], tmpc[:, t0:t0 + n], AF.Square)
                        mk.mm(ps[:, :n], k.ones[:], sq[:, :n])
                        mk.act(rs[:, :n], ps[:, :n], AF.Sqrt, bias=EPS)
                        mk.I("dve", "reciprocal", out=rs[:, :n], in_=rs[:, :n])
                        mk.stt(dst[:, t0:t0 + n], tmpc[:, t0:t0 + n], scl, rs[:, :n], ALU.mult, ALU.mult)
        id64 = k.ident[0:64, 0:64]
        orders = [cfg.order(0), cfg.order(1)]

        def stream(hp):
            u = f"s{hp}"
            Q0, Q1, Q2, Q3 = k.PS[4 * hp:4 * hp + 4]
            S = mk.sb("dn_S" + u, [128, 4, 128]); Sb = mk.sb("dn_Sb" + u, [128, 4, 128], BF16)
            mk._memset("dve", S[:], 0.0)
            mk._memset("pool", Sb[:], 0.0)
            GD = mk.sb("dn_gd" + u, [64, 4, 64]); NM = mk.sb("dn_nm" + u, [64, 4, 64]); Dm = mk.sb("dn_dm" + u, [64, 4, 64])
            Ds = mk.sb("dn_ds" + u, [64, 4, 64]); Di = mk.sb("dn_di" + u, [64, 4, 64])
            AL = mk.sb("dn_al" + u, [64, 4, 64]); AQ = mk.sb("dn_aq" + u, [64, 4, 64]); AQT = mk.sb("dn_aqt" + u, [64, 4, 64])
            BB = mk.sb("dn_bb" + u, [64, 2, 4, 64]); X = mk.sb("dn_x" + u, [64, 4, 64])
            vb = mk.sb("dn_vb" + u, [64, 4, 128]); kbd = mk.sb("dn_kbd" + u, [64, 4, 128]); kd = mk.sb("dn_kd" + u, [64, 4, 128])
            U = mk.sb("dn_u" + u, [64, 4, 128]); WT = mk.sb("dn_wt" + u, [128, 4, 64]); VN = mk.sb("dn_vn" + u, [64, 4, 128])
            O2 = mk.sb("dn_o2" + u, [64, 4, 128]); Os = [mk.sb(f"dn_o{i}" + u, [64, 4, 128]) for i in range(2)]
            v4 = lambda p, a=4: p.rr("p (a b) -> p a b", a=a)
            for s in range(nch):
                info = []
                for c in range(4):
                    d, hh = c // 2, c % 2
                    n = orders[d][s]
                    info.append((d, 2 * hp + hh, n, n * 64, d * 4 + 2 * hp + hh))
                for c, (d, h, n, t0, col) in enumerate(info):
                    mk.ts("pool", GD[:, c, :], k.masks[:, LE if d == 0 else GE, :], g[:, n, col:col + 1], None, op0=ALU.mult)
                yield
                for c, (d, h, n, t0, col) in enumerate(info):
                    kT = Kb[h][:, t0:t0 + 64]
                    mk.mm(Q0[0:64, c * 64:(c + 1) * 64], k.ones[0:64, 0:64], GD[:, c, :])
                    mk.mm(Q0[0:64, 256 + c * 64:256 + (c + 1) * 64], kT, kT)
                    mk.mm(Q1[0:64, c * 64:(c + 1) * 64], Qb[h][:, t0:t0 + 64], kT)
                    mk.mm(Q3[0:64, c * 128:(c + 1) * 128], kT, k.identb[:])
                yield
                for c, (d, h, n, t0, col) in enumerate(info):
                    mk.ts("dve", NM[:, c, :], Q0[0:64, c * 64:(c + 1) * 64], gam[:, n, col:col + 1], 0.0, op0=ALU.subtract, op1=ALU.max)
                    mk.act(kbd[:, c, :], Q3[0:64, c * 128:(c + 1) * 128], AF.Identity, scale=bg[:, n, col:col + 1])
                yield
                mk.act(Dm[:], NM[:], AF.Exp, scale=-1.0)
                for c, (d, h, n, t0, col) in enumerate(info):
                    mk.act(kd[:, c, :], Q3[0:64, c * 128:(c + 1) * 128], AF.Identity, scale=erg[:, n, col:col + 1])
                yield
                for d in range(2):
                    dsl = slice(2 * d, 2 * d + 2)
                    mk.tt("pool", Ds[:, dsl, :], Dm[:, dsl, :], vbc(vus(k.masks[:, GT if d == 0 else LT, :], 1), [64, 2, 64]), ALU.mult)
                    mk.tt("pool", Di[:, dsl, :], Dm[:, dsl, :], vbc(vus(k.masks[:, GE if d == 0 else LE, :], 1), [64, 2, 64]), ALU.mult)
                for c, (d, h, n, t0, col) in enumerate(info):
                    mk.mm(Q3[0:64, c * 128:(c + 1) * 128], Vb[h][:, t0:t0 + 64], k.identb[:])
                yield
                for c, (d, h, n, t0, col) in enumerate(info):
                    mk.stt(AL[:, c, :], Q0[0:64, 256 + c * 64:256 + (c + 1) * 64], beta[:, n, col:col + 1], Ds[:, c, :], ALU.mult, ALU.mult)
                mk.tt("dve", AQ[:], v4(Q1[0:64, 0:256]), Di[:], ALU.mult)
                for c, (d, h, n, t0, col) in enumerate(info):
                    mk.act(vb[:, c, :], Q3[0:64, c * 128:(c + 1) * 128], AF.Identity, scale=beta[:, n, col:col + 1])
                yield
                for c in range(4):
                    mk.mm(Q2[0:64, c * 64:(c + 1) * 64], AL[:, c, :], id64)
                    mk.mm(Q2[0:64, 256 + c * 64:256 + (c + 1) * 64], AQ[:, c, :], id64)
                yield
                mk.ts("dve", BB[:, 0], v4(Q2[0:64, 0:256]), -1.0, None, op0=ALU.mult)
                mk.ts("pool", BB[:, 1], AL[:], -1.0, None, op0=ALU.mult)
                yield
                mk.tt("pool", X[:], BB[:, 0], vbc(vus(id64, 1), [64, 4, 64]), ALU.add)
                mk.cp("act", AQT[:], v4(Q2[0:64, 256:512]))
                for c in range(4):
                    mk.mm(Q0[0:64, c * 64:(c + 1) * 64], BB[:, 1, c, :], BB[:, 0, c, :])
                    mk.mm(Q0[0:64, 256 + c * 64:256 + (c + 1) * 64], BB[:, 0, c, :], BB[:, 1, c, :])
                yield
                mk.cp("act", BB[:], Q0[0:64, :].rr("p (t a b) -> p t a b", t=2, a=4))
                yield
                for lev in range(1, 6):
                    if lev < 5:
                        for c in range(4):
                            mk.mm(Q0[0:64, c * 64:(c + 1) * 64], BB[:, 1, c, :], BB[:, 0, c, :])
                            mk.mm(Q0[0:64, 256 + c * 64:256 + (c + 1) * 64], BB[:, 0, c, :], BB[:, 1, c, :])
                    for c in range(4):
                        mk.mm(Q1[0:64, 256 + c * 64:256 + (c + 1) * 64], BB[:, 1, c, :], X[:, c, :])
                    yield
                    if lev < 5:
                        mk.cp("act", BB[:], Q0[0:64, :].rr("p (t a b) -> p t a b", t=2, a=4))
                    mk.tt("dve", X[:], X[:], v4(Q1[0:64, 256:512]), ALU.add)
                    yield
                for c in range(4):
                    mk.mm(Q2[0:64, c * 128:(c + 1) * 128], X[:, c, :], vb[:, c, :])
                    mk.mm(Q1[:, c * 64:(c + 1) * 64], kbd[:, c, :], X[:, c, :])
                yield
                mk.cp("act", U[:], v4(Q2[0:64, :]))
                mk.cp("dve", WT[:], v4(Q1[:, 0:256]))
                yield
                for c in range(4):
                    mk.mm(Q0[0:64, c * 128:(c + 1) * 128], WT[:, c, :], S[:, c, :])
                yield
                mk.tt("dve", VN[:], U[:], v4(Q0[0:64, :]), ALU.subtract)
                yield
                for c, (d, h, n, t0, col) in enumerate(info):
                    mk.mm(Q1[0:64, c * 128:(c + 1) * 128], Qb[h][:, t0:t0 + 64], Sb[:, c, :])
                    mk.mm(Q2[0:64, c * 128:(c + 1) * 128], AQT[:, c, :], VN[:, c, :])
                    mk.mm(Q3[:, c * 128:(c + 1) * 128], kd[:, c, :], VN[:, c, :])
                yield
                mk.cp("act", O2[:], v4(Q2[0:64, :]))
                for c, (d, h, n, t0, col) in enumerate(info):
                    mk.stt(S[:, c, :], S[:, c, :], cdec[:, n, col:col + 1], Q3[:, c * 128:(c + 1) * 128], ALU.mult, ALU.add)
                mk.cp("pool", Sb[:], S[:])
                yield
                Ot = Os[s % 2]
                for c, (d, h, n, t0, col) in enumerate(info):
                    mk.stt(Ot[:, c, :], Q1[0:64, c * 128:(c + 1) * 128], eg[:, n, col:col + 1], O2[:, c, :], ALU.mult, ALU.add)
                for d in range(2):
                    t0 = orders[d][s] * 64
                    mk.dma(STQ, k.osc[b][d][t0:t0 + 64, hp * 256:(hp + 1) * 256], Ot[:, 2 * d:2 * d + 2, :].rr("p a b -> p (a b)"))
                yield

        nstream = getattr(cfg, "dn_streams", 2)
        if nstream == 2:
            gens = [stream(0), stream(1)]
            while gens:
                for gen in list(gens):
                    try:
                        next(gen)
                    except StopIteration:
                        gens.remove(gen)
        else:
            for hp in range(2):
                for _ in stream(hp):
                    pass


def stage_hg(k, l, b):
    mk, cfg = k.mk, k.cfg
    NT, C, nch = cfg.NT, cfg.C, cfg.nch
    LE, GE = 0, 1
    with scope(mk):
        rmask = mk.sb("hg_rm", [128, 2, NT], BF16)
        mk._memset("pool", rmask[:], 1.0)
        mk._memset("pool", rmask[:, 0, :].rr("p (n c) -> p n c", c=64)[:, :, 0:1], 0.0)
        mk._memset("pool", rmask[:, 1, :].rr("p (n c) -> p n c", c=64)[:, :, 63:64], 0.0)
        q = mk.sb("hg_q", [128, NT])
        v3 = lambda t: t[:].rr("p (n c) -> p n c", c=64)
        ST = []
        for d in range(2):
            u = f"d{d}"
            st = K()
            st.fr = mk.sb("hg_fr" + u, [128, NT]); st.Kf = mk.sb("hg_kf" + u, [128, NT], BF16); st.gc = mk.sb("hg_gc" + u, [128, NT])
            st.QG = mk.sb("hg_qg" + u, [128, NT], BF16); st.QTL = mk.sb("hg_qtl" + u, [128, NT], BF16); st.KD = mk.sb("hg_kd" + u, [128, NT], BF16)
            st.REF = mk.sb("hg_ref" + u, [128, nch, 4]); st.EL = mk.sb("hg_el" + u, [128, nch]); st.oml = mk.sb("hg_oml" + u, [128, 1])
            st.S = mk.sb("hg_S" + u, [128, 128]); st.Sb = mk.sb("hg_Sb" + u, [128, 128], BF16)
            st.vt = [mk.sb(f"hg_vt{i}" + u, [64, 128]) for i in range(2)]; st.vtb = [mk.sb(f"hg_vtb{i}" + u, [64, 128], BF16) for i in range(2)]
            st.SCT = [mk.sb(f"hg_sct{i}" + u, [64, 64], BF16) for i in range(2)]; st.kdt = [mk.sb(f"hg_kdt{i}" + u, [64, 128], BF16) for i in range(2)]
            st.Y = [mk.sb(f"hg_y{i}" + u, [128, 4, 64]) for i in range(2)]; st.K4 = [mk.sb(f"hg_k4{i}" + u, [128, 4, 64], BF16) for i in range(2)]
            st.Os = [mk.sb(f"hg_o{i}" + u, [64, 128]) for i in range(2)]
            ST.append(st)

        def stream(h, d):
            st = ST[d]
            P = k.PS[4 * d:4 * d + 4]
            fr, Kf, gc, QG, QTL, KD, REF, EL, oml, S, Sb = st.fr, st.Kf, st.gc, st.QG, st.QTL, st.KD, st.REF, st.EL, st.oml, st.S, st.Sb
            tmp = fr
            mk.dma("sp", fr[:], uT_rows(k, b, (HGFF if d == 0 else HGFB) + h))
            lbc = k.lb[:, d, l, h:h + 1]
            mk.ts("dve", oml[:], lbc, -1.0, 1.0, op0=ALU.mult, op1=ALU.add)
            mk.act(fr[:], fr[:], AF.Sigmoid)
            yield
            mk.ts("dve", fr[:], fr[:], oml[:, 0:1], lbc, op0=ALU.mult, op1=ALU.add)
            mk.ts("dve", Kf[:], fr[:], -1.0, 1.0, op0=ALU.mult, op1=ALU.add)
            mk.act(fr[:], fr[:], AF.Ln)
            yield
            if d == 0:
                mk.I("dve", "tensor_tensor_scan", out=gc[:], data0=rmask[:, 0, :], data1=fr[:], initial=0.0, op0=ALU.mult, op1=ALU.add)
            else:
                mk.I("dve", "tensor_tensor_scan", out=V(gc.h[:, ::-1], [gc.trk]), data0=V(rmask.h[:, 1, ::-1], [rmask.trk]),
                     data1=V(fr.h[:, ::-1], [fr.trk]), initial=0.0, op0=ALU.mult, op1=ALU.add)
            last = 63 if d == 0 else 0
            mk.act(tmp[:], gc[:], AF.Exp)
            yield
            mk.cp("dve", EL[:], v3(tmp)[:, :, last])
            mk.tt("dve", QG[:], q[:], tmp[:], ALU.mult)
            mk._memset("pool", REF[:], 0.0)
            if d == 0:
                mk.cp("pool", REF[:, :, 1:4], v3(gc)[:, :, 15:63:16])
            else:
                mk.cp("pool", REF[:, :, 0:3], v3(gc)[:, :, 16:64:16])
            yield
            g4 = gc[:].rr("p (n j r) -> p n j r", j=4, r=16)
            t4 = tmp[:].rr("p (n j r) -> p n j r", j=4, r=16)
            mk.tt("dve", t4, g4, vbc(vus(REF[:], 3), [128, nch, 4, 16]), ALU.subtract)
            mk.act(tmp[:], tmp[:], AF.Exp)
            yield
            mk.tt("dve", QTL[:], q[:], tmp[:], ALU.mult)
            mk.tt("dve", v3(tmp), vbc(v3(gc)[:, :, last:last + 1], [128, nch, 64]), v3(gc), ALU.subtract)
            mk.act(tmp[:], tmp[:], AF.Exp)
            yield
            mk.tt("dve", KD[:], tmp[:], Kf[:], ALU.mult)
            mk._memset("dve", S[:], 0.0)
            mk._memset("pool", Sb[:], 0.0)
            yield
            order = cfg.order(d)

            def pre(s):
                n = order[s]; t0 = n * 64; a = s % 2
                Y, K4 = st.Y[a], st.K4[a]
                mk.dma("sp", st.vt[a][:], V(k.utok[b].h[t0:t0 + 64, TM_HGI + h * 128:TM_HGI + (h + 1) * 128], [k.utok[b].parts[t0 // 128]]))
                mk.tt("dve", Y[:], vbc(vus(REF[:, n, :], 2), [128, 4, 64]), vbc(vus(gc[:, t0:t0 + 64], 1), [128, 4, 64]), ALU.subtract)
                mk.ts("dve", Y[:], Y[:], 60.0, None, op0=ALU.min)
                mk.cp("pool", st.vtb[a][:], st.vt[a][:])
                yield
                mk.act(Y[:], Y[:], AF.Exp)
                mk.mm(P[1][0:64, 0:128], KD[:, t0:t0 + 64], k.identb[:])
                yield
                mk.tt("dve", K4[:], Y[:], vbc(vus(Kf[:, t0:t0 + 64], 1), [128, 4, 64]), ALU.mult)
                mk.cp("act", st.kdt[a][:], P[1][0:64, 0:128])
                yield
                for J in range(4):
                    mk.mm(P[0][0:64, 16 * J:16 * J + 16], K4[:, J, :], QTL[:, t0 + 16 * J:t0 + 16 * J + 16])
                yield
                mk.tt("dve", st.SCT[a][:], P[0][0:64, 0:64], k.masks[:, LE if d == 0 else GE, :], ALU.mult)
                yield

            def main(s):
                n = order[s]; t0 = n * 64; a = s % 2
                mk.mm(P[2][0:64, 0:128], QG[:, t0:t0 + 64], Sb[:], start=True, stop=False)
                mk.mm(P[2][0:64, 0:128], st.SCT[a][:], st.vtb[a][:], start=False, stop=True)
                mk.mm(P[3][:, 0:128], st.kdt[a][:], st.vtb[a][:])
                yield
                mk.stt(S[:], S[:], EL[:, n:n + 1], P[3][:, 0:128], ALU.mult, ALU.add)
                mk.cp("act", st.Os[a][:], P[2][0:64, 0:128])
                yield
                mk.cp("pool", Sb[:], S[:])
                mk.dma(STQ, k.osc[b][2 + d][t0:t0 + 64, h * 128:(h + 1) * 128], st.Os[a][:])
                yield

            yield from pre(0)
            for s in range(len(order)):
                if s + 1 < len(order):
                    yield from pre(s + 1)
                yield from main(s)

        for h in range(4):
            mk.dma("sp", q[:], uT_rows(k, b, HGQ + h))
            mk.act(q[:], q[:], AF.Silu)
            gens = [stream(h, 0), stream(h, 1)]
            while gens:
                for gen in list(gens):
                    try:
                        next(gen)
                    except StopIteration:
                        gens.remove(gen)


def stage_post(k, l, b, which, nwT, gblk):
    mk, cfg = k.mk, k.cfg
    with scope(mk):
        nw = mk.sb("po_nw", [128, 1])
        mk.dma("sp", nw[:], nwT[l])
        of = [mk.sb(f"po_of{i}", [128, 512]) for i in range(2)]; ob = [mk.sb(f"po_ob{i}", [128, 512]) for i in range(2)]
        G = [mk.sb(f"po_g{i}", [128, 4, 128]) for i in range(2)]; Y = [mk.sb(f"po_y{i}", [128, 4, 128]) for i in range(2)]
        sq = mk.sb("po_sq", [128, 512]); ss = mk.sb("po_ss", [128, 4])
        uv = V(k.uT[b].ap().rearrange("(x p) t -> p x t", p=128), [k.uT[b].trk] + list(k.uT[b].parts.values()))
        yv = V(k.yT[b][which].ap().rearrange("(ft p) t -> p ft t", p=128), [k.yT[b][which].trk])
        for ti in range(cfg.ntile):
            s = ti % 2
            rows = slice(ti * 128, (ti + 1) * 128)
            mk.dma("sp", of[s][:], k.osc[b][2 * which][rows, :])
            mk.dma("sp", ob[s][:], k.osc[b][2 * which + 1][rows, :])
            mk.dma("sp", G[s][:], uv[:, gblk:gblk + 4, rows])
            o = of[s]
            mk.tt("dve", o[:], o[:], ob[s][:], ALU.add)
            mk.tt("pool", sq[:], o[:], o[:], ALU.mult)
            mk.I("dve", "tensor_reduce", out=ss[:], in_=sq[:].rr("p (h d) -> p h d", d=128), axis=AX.X, op=ALU.add)
            mk.act(ss[:], ss[:], AF.Sqrt, scale=1.0 / 128, bias=EPS)
            mk.I("dve", "reciprocal", out=ss[:], in_=ss[:])
            mk.tt("dve", o[:].rr("p (h d) -> p h d", d=128), o[:].rr("p (h d) -> p h d", d=128), vbc(vus(ss[:], 2), [128, 4, 128]), ALU.mult)
            ps = k.PS[s]
            for h in range(4):
                mk.tr(ps[:, h * 128:(h + 1) * 128], o[:, h * 128:(h + 1) * 128], k.ident[:])
            mk.act(G[s][:], G[s][:], AF.Silu)
            mk.stt(Y[s][:].rr("p a b -> p (a b)"), ps[:], nw[:, 0:1], G[s][:].rr("p a b -> p (a b)"), ALU.mult, ALU.mult)
            mk.dma(STQ, yv[:, :, rows], Y[s][:])


def load_cast(mk, dst_fn, src_fn, npieces, stage_tiles, semL):
    for pc in range(npieces):
        st = stage_tiles[pc % 2]
        sv = src_fn(pc)
        mk.dma("sp", V(st.h[tuple(slice(0, x) for x in sv.ap.shape)], [st.trk], st), sv)
        mk.cp("pool" if pc % 2 else "dve", dst_fn(pc), V(st.h[tuple(slice(0, x) for x in sv.ap.shape)], [st.trk], st))


def stage_s3a(k, l, need_ctx):
    mk, cfg = k.mk, k.cfg
    NB = cfg.NB
    with scope(mk):
        wbr = mk.sb("a_wbr", [128, 4, 4, D], BF16)
        wo = mk.sb("a_wo", [128, 8, D], BF16)
        with scope(mk):
            stg = [mk.sb(f"a_stg{i}", [128, 4, D]) for i in range(2)]
            load_cast(mk, lambda pc: wbr[:, pc], lambda pc: k.w_br[l, :, pc], 4, stg, 1)
            load_cast(mk, lambda pc: wo[:, pc * 4:(pc + 1) * 4, :], lambda pc: k.w_out[l, :, pc * 4:(pc + 1) * 4, :], 2, stg, 1)
        yt = [mk.sb(f"a_yt{i}", [128, 4, 512]) for i in range(2)]
        ytb = mk.sb("a_ytb", [128, 4, 4, 512], BF16)
        G = [mk.sb(f"a_g{i}", [128, 4, 512]) for i in range(2)]
        merged = mk.sb("a_mg", [128, 8, 512], BF16)
        macc = mk.sb("a_macc", [128, 512]); mt = mk.sb("a_mt", [128, 512])
        hg = [mk.sb(f"a_h{i}", [128, 8, 512]) for i in range(2)]
        it = 0
        ev = 0
        for b in range(NB):
            hv = hT_view(k, b)
            uv = V(k.uT[b].ap().rearrange("(x p) t -> p x t", p=128), [k.uT[b].trk] + list(k.uT[b].parts.values()))
            for gi, (t0, n, seg) in enumerate(cfg.groups):
                if seg == 0 and not need_ctx:
                    continue
                j = NB if seg == 0 else b
                h_ = hg[it % 2]
                it += 1
                mk.dma("sp", h_[:, :, :n], hv[:, :, t0:t0 + n])
                for bi in range(4):
                    y_ = yt[bi % 2]
                    yv = V(k.yT[b][bi].ap().rearrange("(ft p) t -> p ft t", p=128), [k.yT[b][bi].trk] + list(k.yT[b][bi].parts.values()))
                    mk.dma("sp", y_[:, :, :n], yv[:, :, t0:t0 + n])
                    mk.cp("pool" if bi % 2 else "dve", ytb[:, bi, :, :n], y_[:, :, :n])
                for c in range(8):
                    g_ = G[c % 2]
                    mk.dma("sp", g_[:, :, :n], uv[:, MGB + c:MGB + c + 25:8, t0:t0 + n])
                    mk.act(g_[:, :, :n], g_[:, :, :n], AF.Sigmoid)
                    for bi in range(4):
                        ps = k.PS[ev % 4]
                        ev += 1
                        for ft in range(4):
                            mk.mm(ps[:, :n], wbr[:, bi, ft, c * 128:(c + 1) * 128], ytb[:, bi, ft, :n], start=(ft == 0), stop=(ft == 3))
                        if bi == 0:
                            mk.tt("dve", macc[:, :n], ps[:, :n], g_[:, 0, :n], ALU.mult)
                        else:
                            mk.tt("dve", mt[:, :n], ps[:, :n], g_[:, bi, :n], ALU.mult)
                            mk.tt("pool", macc[:, :n] if bi < 3 else merged[:, c, :n], macc[:, :n], mt[:, :n], ALU.add)
                for c in range(8):
                    ps = k.PS[4 + c % 4]
                    for kt in range(8):
                        mk.mm(ps[:, :n], wo[:, kt, c * 128:(c + 1) * 128], merged[:, kt, :n], start=(kt == 0), stop=(kt == 7))
                    mk.stt(h_[:, c, :n], ps[:, :n], k.modT[:, 16 + c, j:j + 1], h_[:, c, :n], ALU.mult, ALU.add)
                mk.dma(STQ, hv[:, :, t0:t0 + n], h_[:, :, :n])


def stage_s3b(k, l, need_ctx):
    mk, cfg = k.mk, k.cfg
    NB = cfg.NB
    with scope(mk):
        w1b = mk.sb("b_w1", [128, 8, 4 * D], BF16)
        w2b = mk.sb("b_w2", [128, 32, D], BF16)
        with scope(mk):
            stg = [mk.sb(f"b_stg{i}", [128, 8, 512]) for i in range(2)]
            load_cast(mk, lambda pc: w1b[:, :, pc * 512:(pc + 1) * 512], lambda pc: k.w1[l, :, :, pc * 512:(pc + 1) * 512], 8, stg, 1)
            stg2 = [V(stg[i].h[:].rearrange("p a b -> p (a b)").rearrange("p (a b) -> p a b", a=4), [stg[i].trk], stg[i]) for i in range(2)]
            for pc in range(8):
                sv = k.w2[l, :, pc * 4:(pc + 1) * 4, :]
                mk.dma("sp", stg2[pc % 2], sv)
                mk.cp("pool" if pc % 2 else "dve", w2b[:, pc * 4:(pc + 1) * 4, :], stg2[pc % 2])
        G = 256
        hg = [mk.sb(f"b_h{i}", [128, 8, G]) for i in range(2)]
        sq = mk.sb("b_sq", [128, 8, G]); tmp = mk.sb("b_tmp", [128, 8, G]); rs = mk.sb("b_rs", [128, G])
        z2 = mk.sb("b_z2", [128, 8, G], BF16)
        hid = mk.sb("b_hid", [128, 32, G], BF16)
        rl = [mk.sb(f"b_rl{i}", [128, G]) for i in range(2)]
        it = 0
        ev = 0
        for b in range(NB):
            hv = hT_view(k, b)
            subs = []
            for (t0, n, seg) in cfg.groups:
                if seg == 0 and not need_ctx:
                    continue
                for o in range(0, n, G):
                    subs.append((t0 + o, min(G, n - o), seg))
            for (t0, n, seg) in subs:
                j = NB if seg == 0 else b
                h_ = hg[it % 2]
                it += 1
                mk.dma("sp", h_[:, :, :n], hv[:, :, t0:t0 + n])
                rms_mod(k, h_, n, j, k.A2, 24, lambda kt: z2[:, kt, :n], tmp, sq, rs, k.PS[7])
                for fc in range(32):
                    ps = k.PS[ev % 4]
                    r_ = rl[ev % 2]
                    ev += 1
                    for kt in range(8):
                        mk.mm(ps[:, :n], w1b[:, kt, fc * 128:(fc + 1) * 128], z2[:, kt, :n], start=(kt == 0), stop=(kt == 7))
                    mk.act(r_[:, :n], ps[:, :n], AF.Relu)
                    mk.tt("pool" if ev % 2 else "dve", hid[:, fc, :n], r_[:, :n], r_[:, :n], ALU.mult)
                for c in range(8):
                    ps = k.PS[4 + c % 2]
                    for ft in range(32):
                        mk.mm(ps[:, :n], w2b[:, ft, c * 128:(c + 1) * 128], hid[:, ft, :n], start=(ft == 0), stop=(ft == 31))
                    mk.stt(h_[:, c, :n], ps[:, :n], k.modT[:, 40 + c, j:j + 1], h_[:, c, :n], ALU.mult, ALU.add)
                mk.dma(STQ, hv[:, :, t0:t0 + n], h_[:, :, :n])


_CACHE = {}


def host_consts(cfg):
    i = np.arange(64)
    kk, ff = i[:, None], i[None, :]
    masks = np.stack([(kk <= ff), (kk >= ff), (kk > ff), (kk < ff)], axis=1).astype(np.float32)
    rows = cfg.T // 64
    row_id = np.repeat(np.arange(rows), 64).astype(np.float32)
    col_id = np.tile(np.arange(64), rows).astype(np.float32)
    inv = (10000.0 ** (-np.arange(0, 32, 2, dtype=np.float32) / 32.0)).astype(np.float32)
    ang = np.stack([row_id[:, None] * inv, col_id[:, None] * inv], axis=1).astype(np.float32)
    rope = np.stack([np.cos(ang), np.sin(ang)], axis=1).astype(np.float32)
    return {"c_ident": np.eye(128, dtype=np.float32), "c_masks": np.ascontiguousarray(masks), "c_rope": np.ascontiguousarray(rope)}


def fm_cols():
    offs = np.cumsum([0, 512, 512, 512, 512, 8, 8, 512, 512, 512, 512, 512, 512, 512, 512, 128, 128, 4096])
    (dq, dk, dv, dg, dbeta, dalpha, hq, hff, hfb, hi, hgate, lx, lg, aq, ak, av, mg) = offs[:17]
    fm = []
    for base in (dq, dk, dv, dg, hq, hff, hfb, hgate, lx, lg):
        fm.append(np.arange(base, base + 512))
    fm.append(np.arange(mg, mg + 4096))
    fm = np.concatenate(fm)
    tm = np.concatenate([np.arange(hi, hi + 512), np.arange(aq, aq + 512), np.arange(ak, ak + 128), np.arange(av, av + 128),
                         np.arange(dbeta, dbeta + 8), np.arange(dalpha, dalpha + 8)])
    assert fm.size == NFM * 128 and tm.size == NTM
    return fm, tm


def pk(w):
    r = w.shape[0] // 128
    return np.ascontiguousarray(w.reshape(r, 128, *w.shape[1:]).swapaxes(0, 1))


def host_weights(inp, L):
    f = np.float32
    fm, tm = fm_cols()
    W = {}
    W["mod_w"] = np.stack([pk(inp["mod_w"][l]) for l in range(L)])
    W["mod_b"] = np.stack([np.ascontiguousarray(inp["mod_b"][l].reshape(48, 128).T) for l in range(L)])
    W["n1w"] = np.stack([np.ascontiguousarray(inp["norm1_w"][l].reshape(8, 128).T) for l in range(L)])
    W["n2w"] = np.stack([np.ascontiguousarray(inp["norm2_w"][l].reshape(8, 128).T) for l in range(L)])
    wfm = []
    for l in range(L):
        w = inp["w_in"][l][:, fm]
        w = w.reshape(8, 128, NFM, 128).transpose(2, 1, 0, 3)
        wfm.append(np.ascontiguousarray(w))
    W["w_fm"] = np.stack(wfm)
    W["w_tm"] = np.stack([pk(np.ascontiguousarray(inp["w_in"][l][:, tm])) for l in range(L)])
    W["dn_cw"] = np.stack([np.ascontiguousarray(inp["dn_conv_w"][l].reshape(4, 12, 128).transpose(2, 1, 0)) for l in range(L)])
    W["dn_ab"] = np.stack([np.broadcast_to(np.stack([inp["dn_a_log"][l].reshape(8), inp["dn_dt_bias"][l].reshape(8)])[None], (64, 2, 8)).copy() for l in range(L)])
    W["dn_nw"] = np.ascontiguousarray(inp["dn_norm_w"][:L].reshape(L, 128, 1))
    W["hg_nw"] = np.ascontiguousarray(inp["hg_norm_w"][:L].reshape(L, 128, 1))
    W["hg_lb"] = np.ascontiguousarray(inp["hg_lower_bounds"][:, :L].reshape(2, L, 4, 128).transpose(3, 0, 1, 2))
    W["lru_cw"] = np.stack([np.ascontiguousarray(inp["lru_conv_w"][l].reshape(4, 4, 128).transpose(2, 1, 0)) for l in range(L)])
    W["lru_cb"] = np.stack([np.ascontiguousarray(inp["lru_conv_b"][l].reshape(4, 128).T) for l in range(L)])
    W["lru_wa"] = np.stack([np.ascontiguousarray(inp["lru_w_a"][l].transpose(2, 0, 1, 3)) for l in range(L)])
    W["lru_wx"] = np.stack([np.ascontiguousarray(inp["lru_w_x"][l].transpose(2, 0, 1, 3)) for l in range(L)])
    W["lru_v"] = np.stack([np.ascontiguousarray(np.stack([inp["lru_b_a"][l], inp["lru_b_x"][l], inp["lru_lambda"][l]]).reshape(3, 2, 4, 128).transpose(3, 0, 1, 2)) for l in range(L)])
    W["att_nw"] = np.stack([np.broadcast_to(np.stack([inp["att_q_norm_w"][l], inp["att_k_norm_w"][l]])[None], (128, 2, 64)).copy() for l in range(L)])
    W["w_br"] = np.stack([np.ascontiguousarray(inp["w_branch"][l].reshape(4, 4, 128, D).transpose(2, 0, 1, 3)) for l in range(L)])
    W["w_out"] = np.stack([pk(inp["w_out"][l]) for l in range(L)])
    W["w1"] = np.stack([pk(inp["mlp_w1"][l]) for l in range(L)])
    W["w2"] = np.stack([pk(inp["mlp_w2"][l]) for l in range(L)])
    return {k_: np.ascontiguousarray(v, dtype=f) for k_, v in W.items()}


def run(inp, cfg, n_cores):
    key = (cfg.T, cfg.C, cfg.NB, cfg.L, cfg.debug)
    if key not in _CACHE:
        _CACHE[key] = build(cfg)
    nc, mk = _CACHE[key]
    inp = {k_: np.asarray(v, dtype=np.float32) for k_, v in inp.items()}
    W = host_weights(inp, cfg.L)
    W.update(host_consts(cfg))
    in_maps = []
    NB = cfg.NB
    for ci in range(n_cores):
        bs = slice(ci * NB, (ci + 1) * NB)
        m = dict(W)
        m["x_in"] = np.ascontiguousarray(np.concatenate([inp["ctx"][bs], inp["x"][bs]], axis=1))
        cv = np.concatenate([inp["c"][bs], inp["c_ctx"][None]], axis=0)
        m["cT"] = np.ascontiguousarray(cv.T.reshape(8, 128, NB + 1).transpose(1, 0, 2))
        in_maps.append(m)
    res = run_bass_kernel_spmd(nc, in_maps, core_ids=list(range(n_cores)))
    return res


def kernel(**inputs):
    cfg = Cfg(T=4096, C=256, NB=2, L=4)
    res = run(inputs, cfg, 8)
    return np.concatenate([r["out"] for r in res.results], axis=0).astype(np.float32)
```

```python
import numpy as np
import concourse.bass as bass
import concourse.mybir as mybir
from concourse.bass_utils import run_bass_kernel_spmd

F32 = mybir.dt.float32
BF16 = mybir.dt.bfloat16
AF = mybir.ActivationFunctionType
ALU = mybir.AluOpType
AX = mybir.AxisListType

SAME_ENGINE_SYNC = True


class StopBuild(Exception):
    pass


class Trk:
    __slots__ = ("w", "r_eng", "r_dma", "excl")

    def __init__(self):
        self.excl = False
        self.w = None
        self.r_eng = {}
        self.r_dma = {}


class V:
    __slots__ = ("ap", "trks", "t")

    def __init__(self, ap, trks, t=None):
        self.ap = ap
        self.trks = trks
        self.t = t

    def __getitem__(self, idx):
        return V(self.ap[idx], self.trks, self.t)

    def rr(self, pat, **kw):
        return V(self.ap.rearrange(pat, **kw), self.trks, self.t)

    def bc(self, shape):
        return V(self.ap.broadcast_to(shape) if hasattr(self.ap, "broadcast_to") else self.ap.to_broadcast(shape), self.trks)


class T:
    def __init__(self, handle, name, is_dram=True):
        self.h = handle
        self.name = name
        self.trk = Trk()
        self.parts = {}
        self.is_dram = is_dram
        self.dsem = None

    def ap(self):
        h = self.h
        return h.ap() if hasattr(h, "ap") else h[:]

    def __getitem__(self, idx):
        return V(self.h[idx], [self.trk], self)

    def p(self, key, idx=None):
        if key not in self.parts:
            self.parts[key] = Trk()
        a = self.h[idx] if idx is not None else self.ap()
        return V(a, [self.parts[key]], self)


class MK:
    def __init__(self, nc, es):
        self.nc = nc
        self.es = es
        self.E = {"pe": nc.tensor, "act": nc.scalar, "dve": nc.vector, "pool": nc.gpsimd, "sp": nc.sync}
        self.esem = {}
        self.cnt = {}
        self.seen = {}
        self.clock = {}
        for e in ("pe", "act", "dve", "pool"):
            self.esem[e] = es.enter_context(nc.semaphore("es_" + e))
            self.cnt[e] = 0
            self.clock[e] = [None]
        for e in self.E:
            self.seen[e] = {}
        self.dsem = {}
        self.dcnt = {}
        self.nwait = 0
        self.ninst = 0
        self.sem_es = es
        self.sem_ptr = 0
        self.limit = None

    def sb(self, name, shape, dt=F32):
        self.uid = getattr(self, "uid", 0) + 1
        h = self.es.enter_context(self.nc.sbuf_tensor("%s_u%d" % (name, self.uid), list(shape), dt))
        return T(h, name, False)

    def ps(self, name, shape, dt=F32):
        h = self.es.enter_context(self.nc.psum_tensor(name, list(shape), dt))
        t = T(h, name, False)
        t.trk.excl = True
        return t

    def dram(self, name, shape, dt=F32, kind="Internal"):
        h = self.nc.dram_tensor(name, list(shape), dt, kind=kind)
        return T(h, name)

    def get_dsem(self, name):
        if name not in self.dsem:
            self.dsem[name] = self.sem_es.enter_context(self.nc.semaphore("ds_" + name))
            self.dcnt[name] = 0
        return self.dsem[name]

    def _wait_event(self, eng, ev):
        if ev is None:
            return
        seen = self.seen[eng]
        if ev[0] == "e":
            _, e2, k = ev
            if e2 == eng and (eng == "pe" or not SAME_ENGINE_SYNC):
                return
            if seen.get(e2, 0) >= k:
                return
            self.E[eng].wait_ge(self.esem[e2], k)
            self.nwait += 1
            seen[e2] = k
            clk = self.clock[e2][k]
            if clk:
                for kk, vv in clk.items():
                    if seen.get(kk, 0) < vv:
                        seen[kk] = vv
        else:
            _, sname, val = ev
            key = "d:" + sname
            if seen.get(key, 0) >= val:
                return
            self.E[eng].wait_ge(self.dsem[sname], val)
            self.nwait += 1
            seen[key] = val

    def _deps(self, eng, outs, ins):
        for v in ins:
            for t in v.trks:
                self._wait_event(eng, t.w)
                if t.excl:
                    for e2, k in t.r_eng.items():
                        if e2 != eng:
                            self._wait_event(eng, ("e", e2, k))
        for v in outs:
            for t in v.trks:
                self._wait_event(eng, t.w)
                for e2, k in t.r_eng.items():
                    self._wait_event(eng, ("e", e2, k))
                for s, val in t.r_dma.items():
                    self._wait_event(eng, ("d", s, val))

    def I(self, eng, meth, **kw):
        outs, ins = [], []
        args = {}
        for k, v in kw.items():
            if isinstance(v, V):
                (outs if k in ("out", "accum_out") else ins).append(v)
                args[k] = v.ap
            else:
                args[k] = v
        if self.limit is not None and self.ninst >= self.limit:
            return None
        self._deps(eng, outs, ins)
        inst = getattr(self.E[eng], meth)(**args)
        self.cnt[eng] += 1
        k = self.cnt[eng]
        inst.then_inc(self.esem[eng], 1)
        self.ninst += 1
        snap = dict(self.seen[eng])
        self.clock[eng].append(snap)
        ev = ("e", eng, k)
        for v in outs:
            for t in v.trks:
                t.w = ev
                t.r_eng = {}
                t.r_dma = {}
        for v in ins:
            for t in v.trks:
                if t.r_eng.get(eng, 0) < k:
                    t.r_eng[eng] = k
        return inst

    def dma(self, q, out, in_, sem=None, **kw):
        if sem is None:
            st = out.t if (out.t is not None and not out.t.is_dram) else in_.t
            if st.dsem is None:
                st.dsem = "D%d" % self.sem_ptr
                self.sem_ptr += 1
            sem = st.dsem
        if self.limit is not None and self.ninst >= self.limit:
            return None
        s = self.get_dsem(sem)
        self._deps(q, [out], [in_])
        inst = self.E[q].dma_start(out=out.ap, in_=in_.ap, **kw)
        self.dcnt[sem] += 16
        val = self.dcnt[sem]
        inst.then_inc(s, 16)
        self.ninst += 1
        ev = ("d", sem, val)
        for t in out.trks:
            t.w = ev
            t.r_eng = {}
            t.r_dma = {}
        for t in in_.trks:
            if t.r_dma.get(sem, 0) < val:
                t.r_dma[sem] = val
        return inst

    def barrier(self):
        for eng in self.E:
            for e2 in self.esem:
                if self.cnt[e2] > 0:
                    self._wait_event(eng, ("e", e2, self.cnt[e2]))
            for s, val in self.dcnt.items():
                if val > 0:
                    self._wait_event(eng, ("d", s, val))

    def finish(self, eng="sp"):
        for s, val in self.dcnt.items():
            if val > 0:
                self._wait_event(eng, ("d", s, val))

    def mm(self, out, lhsT, rhs, start=True, stop=True, **kw):
        return self.I("pe", "matmul", out=out, lhsT=lhsT, rhs=rhs, start=start, stop=stop, **kw)

    def tr(self, out, in_, identity):
        return self.I("pe", "transpose", out=out, in_=in_, identity=identity)

    def act(self, out, in_, func, eng="act", **kw):
        return self.I("act", "activation", out=out, in_=in_, func=func, **kw)

    def ts(self, eng, out, in0, s1, s2=None, op0=ALU.mult, op1=None, **kw):
        if op1 is None:
            return self.I(eng, "tensor_scalar", out=out, in0=in0, scalar1=s1, scalar2=s2, op0=op0, **kw)
        return self.I(eng, "tensor_scalar", out=out, in0=in0, scalar1=s1, scalar2=s2, op0=op0, op1=op1, **kw)

    def tt(self, eng, out, in0, in1, op):
        return self.I(eng, "tensor_tensor", out=out, in0=in0, in1=in1, op=op)

    def stt(self, out, in0, scalar, in1, op0, op1, eng="dve"):
        return self.I(eng, "scalar_tensor_tensor", out=out, in0=in0, scalar=scalar, in1=in1, op0=op0, op1=op1)

    def cp(self, eng, out, in_):
        if eng == "act":
            return self.I("act", "copy", out=out, in_=in_)
        return self.I(eng, "tensor_copy", out=out, in_=in_)

    def memset(self, eng, out, val):
        return self.I(eng, "memset", ap=None, out=out, constant=val) if False else self._memset(eng, out, val)

    def _memset(self, eng, out, val):
        if self.limit is not None and self.ninst >= self.limit:
            return None
        self._deps(eng, [out], [])
        inst = self.E[eng].memset(out.ap, val)
        self.cnt[eng] += 1
        k = self.cnt[eng]
        inst.then_inc(self.esem[eng], 1)
        self.ninst += 1
        self.clock[eng].append(dict(self.seen[eng]))
        for t in out.trks:
            t.w = ("e", eng, k)
            t.r_eng = {}
            t.r_dma = {}
        return inst

from contextlib import ExitStack, contextmanager

EPS = 1e-6
STQ = "pool"
D = 1024
NFM = 72
NTM = 1296
DNQ, DNK, DNV, DNG, HGQ, HGFF, HGFB, HGG, LRX, LRG, MGB = 0, 4, 8, 12, 16, 20, 24, 28, 32, 36, 40
TM_HGI, TM_ATQ, TM_ATK, TM_ATV, TM_BETA, TM_ALPHA = 0, 512, 1024, 1152, 1280, 1288


def vus(v, axis):
    return V(v.ap.unsqueeze(axis), v.trks)


def vbc(v, shape):
    return V(v.ap.broadcast_to(list(shape)), v.trks)


class Cfg:
    def __init__(self, T=4096, C=256, NB=2, L=4, debug=False):
        self.T, self.C, self.NB, self.L = T, C, NB, L
        self.NT = T + C
        self.debug = debug
        self.groups = []
        t = 0
        while t < C:
            n = min(512, C - t)
            self.groups.append((t, n, 0))
            t += n
        while t < self.NT:
            n = min(512, self.NT - t)
            self.groups.append((t, n, 1))
            t += n
        self.ntile = self.NT // 128
        self.nch = self.NT // 64
        self.cch = C // 64

    def gi_of(self, t):
        for i, (t0, n, s) in enumerate(self.groups):
            if t0 <= t < t0 + n:
                return i
        raise ValueError

    def order(self, d):
        if d == 0:
            return list(range(self.nch))
        return list(range(self.cch - 1, -1, -1)) + list(range(self.nch - 1, self.cch - 1, -1))


class K:
    pass


@contextmanager
def scope(mk):
    old = mk.es
    ptr = mk.sem_ptr
    with ExitStack() as es:
        mk.es = es
        try:
            yield es
        finally:
            lim, mk.limit = mk.limit, None
            mk.barrier()
            mk.limit = lim
            mk.es = old
            mk.sem_ptr = ptr


def build(cfg):
    nc = bass.Bass("TRN2", target_bir_lowering=False)
    L, NB, NT, C = cfg.L, cfg.NB, cfg.NT, cfg.C
    root = ExitStack()
    mk = MK(nc, root)
    k = K()
    k.cfg, k.mk, k.nc = cfg, mk, nc
    T_ = T

    def din(name, shape):
        return T_(nc.dram_tensor(name, list(shape), F32, kind="ExternalInput"), name)

    k.x_in = din("x_in", [NB, NT, D])
    k.cT = din("cT", [128, 8, NB + 1])
    k.mod_w = din("mod_w", [L, 128, 8, 6 * D])
    k.mod_b = din("mod_b", [L, 128, 48])
    k.n1w = din("n1w", [L, 128, 8])
    k.n2w = din("n2w", [L, 128, 8])
    k.w_fm = din("w_fm", [L, NFM, 128, 8, 128])
    k.w_tm = din("w_tm", [L, 128, 8, NTM])
    k.dn_cw = din("dn_cw", [L, 128, 12, 4])
    k.dn_ab = din("dn_ab", [L, 64, 2, 8])
    k.dn_nw = din("dn_nw", [L, 128, 1])
    k.hg_nw = din("hg_nw", [L, 128, 1])
    k.hg_lb = din("hg_lb", [128, 2, L, 4])
    k.lru_cw = din("lru_cw", [L, 128, 4, 4])
    k.lru_cb = din("lru_cb", [L, 128, 4])
    k.lru_wa = din("lru_wa", [L, 128, 2, 4, 128])
    k.lru_wx = din("lru_wx", [L, 128, 2, 4, 128])
    k.lru_v = din("lru_v", [L, 128, 3, 2, 4])
    k.att_nw = din("att_nw", [L, 128, 2, 64])
    k.w_br = din("w_br", [L, 128, 4, 4, D])
    k.w_out = din("w_out", [L, 128, 8, D])
    k.w1 = din("w1", [L, 128, 8, 4 * D])
    k.w2 = din("w2", [L, 128, 32, D])
    k.c_ident = din("c_ident", [128, 128])
    k.c_masks = din("c_masks", [64, 4, 64])
    k.c_rope = din("c_rope", [cfg.T, 2, 2, 16])
    k.out = T_(nc.dram_tensor("out", [NB, cfg.T, D], F32, kind="ExternalOutput"), "out")

    kind = "ExternalOutput" if cfg.debug else "Internal"

    def dsc(name, shape):
        return T_(nc.dram_tensor(name, list(shape), F32, kind=kind), name)

    k.hT = [dsc(f"hT{b}", [D, NT]) for b in range(NB)]
    k.uT = [dsc(f"uT{b}", [NFM * 128, NT]) for b in range(NB)]
    k.utok = [dsc(f"utok{b}", [NT, NTM]) for b in range(NB)]
    k.osc = [[dsc(f"osc{b}_{i}", [NT, 512]) for i in range(4)] for b in range(NB)]
    k.yT = [[dsc(f"yT{b}_{i}", [512, NT]) for i in range(4)] for b in range(NB)]

    k.PS = [mk.ps(f"psb{i}", [128, 512]) for i in range(8)]
    k.ident = mk.sb("ident", [128, 128])
    k.identb = mk.sb("identb", [128, 128], BF16)
    k.ones = mk.sb("ones", [128, 128])
    k.masks = mk.sb("masks", [64, 4, 64])
    k.scT = mk.sb("scT", [128, 8, NB + 1])
    k.modT = mk.sb("modT", [128, 48, NB + 1])
    k.A1 = mk.sb("A1", [128, 8, NB + 1])
    k.A2 = mk.sb("A2", [128, 8, NB + 1])
    k.lb = mk.sb("lb", [128, 2, L, 4])
    mk.dma("sp", k.ident[:], k.c_ident[:, :])
    mk.dma("sp", k.masks[:], k.c_masks[:, :, :])
    mk.dma("sp", k.scT[:], k.cT[:, :, :])
    mk.dma("sp", k.lb[:], k.hg_lb[:, :, :, :])
    mk._memset("dve", k.ones[:], 1.0)
    mk.cp("dve", k.identb[:], k.ident[:])
    mk.act(k.scT[:], k.scT[:], AF.Silu)
    prep_lb(k)

    mk.limit = getattr(cfg, "limit", None)
    try:
        _stages(k, cfg, mk, L, NB)
    except StopBuild:
        pass
    mk.limit = None
    mk.barrier()
    root.close()
    return nc, mk


def _stages(k, cfg, mk, L, NB):
    stage_in(k)
    for l in range(L):
        need_ctx = l < L - 1
        stage_mods(k, l)
        for b in range(NB):
            stage_s1(k, l, b)
        if cfg.debug == "s1":
            break
        only = getattr(cfg, "only", None)
        for b in range(NB):
            if only is None or "lru" in only:
                stage_lru(k, l, b)
            if only is None or "att" in only:
                stage_att(k, l, b, need_ctx)
            if only is None or "dn" in only:
                stage_dn(k, l, b)
                if not getattr(cfg, "stop", ""):
                    stage_post(k, l, b, 0, k.dn_nw, DNG)
            if only is None or "hg" in only:
                stage_hg(k, l, b)
                stage_post(k, l, b, 1, k.hg_nw, HGG)
        if cfg.debug == "mix":
            break
        stage_s3a(k, l, need_ctx)
        stage_s3b(k, l, need_ctx)
    stage_out(k)


def prep_lb(k):
    mk, L = k.mk, k.cfg.L
    with scope(mk):
        e = mk.sb("lb_e", [128, 2, L, 4])
        s = mk.sb("lb_s", [128, 2, 4])
        mk.act(e[:], k.lb[:], AF.Exp)
        mk.cp("dve", s[:], e[:, :, 0, :])
        for l in range(1, L):
            mk.tt("dve", s[:], s[:], e[:, :, l, :], ALU.add)
        mk.I("dve", "reciprocal", out=s[:], in_=s[:])
        mk.tt("dve", e[:], e[:], vbc(vus(s[:], 2), [128, 2, L, 4]), ALU.mult)
        mk._memset("dve", k.lb[:, :, 0, :], 0.0)
        for l in range(1, L):
            mk.tt("dve", k.lb[:, :, l, :], k.lb[:, :, l - 1, :], e[:, :, l, :], ALU.add)


def hT_view(k, b):
    return V(k.hT[b].ap().rearrange("(kt p) t -> p kt t", p=128), [k.hT[b].trk])


def stage_in(k):
    mk, cfg = k.mk, k.cfg
    with scope(mk):
        xt = [mk.sb(f"in_x{i}", [128, D]) for i in range(2)]
        st = [mk.sb(f"in_s{i}", [128, 8, 128]) for i in range(2)]
        i = 0
        for b in range(cfg.NB):
            hv = hT_view(k, b)
            for ti in range(cfg.ntile):
                x, s = xt[i % 2], st[i % 2]
                mk.dma("sp", x[:], k.x_in[b, ti * 128:(ti + 1) * 128, :])
                for half in range(2):
                    ps = k.PS[half]
                    for q in range(4):
                        mk.tr(ps[:, q * 128:(q + 1) * 128], x[:, (half * 4 + q) * 128:(half * 4 + q + 1) * 128], k.ident[:])
                    mk.cp("act" if half else "dve", s[:, half * 4:half * 4 + 4, :], ps[:].rr("p (a b) -> p a b", a=4))
                mk.dma(STQ, hv[:, :, ti * 128:(ti + 1) * 128], s[:])
                i += 1


def stage_out(k):
    mk, cfg = k.mk, k.cfg
    with scope(mk):
        ht = [mk.sb(f"out_h{i}", [128, 8, 128]) for i in range(2)]
        st = [mk.sb(f"out_s{i}", [128, D]) for i in range(2)]
        i = 0
        for b in range(cfg.NB):
            hv = hT_view(k, b)
            for ti in range(cfg.T // 128):
                h, s = ht[i % 2], st[i % 2]
                t0 = cfg.C + ti * 128
                mk.dma("sp", h[:], hv[:, :, t0:t0 + 128])
                for half in range(2):
                    ps = k.PS[half]
                    for q in range(4):
                        mk.tr(ps[:, q * 128:(q + 1) * 128], h[:, half * 4 + q, :], k.ident[:])
                    mk.cp("act" if half else "dve", s[:, half * 512:(half + 1) * 512], ps[:])
                mk.dma(STQ, k.out[b, ti * 128:(ti + 1) * 128, :], s[:])
                i += 1


def stage_mods(k, l):
    mk, cfg = k.mk, k.cfg
    NJ = cfg.NB + 1
    with scope(mk):
        wp = [mk.sb(f"mod_w{i}", [128, 8, 512]) for i in range(2)]
        mb = mk.sb("mod_bt", [128, 48])
        nw = mk.sb("mod_nw", [128, 2, 8])
        mk.dma("sp", mb[:], k.mod_b[l])
        mk.dma("sp", nw[:, 0, :], k.n1w[l])
        mk.dma("sp", nw[:, 1, :], k.n2w[l])
        pm = k.PS[0]
        for pc in range(12):
            w = wp[pc % 2]
            mk.dma("sp", w[:], k.mod_w[l, :, :, pc * 512:(pc + 1) * 512])
            for j in range(4):
                o = (pc * 4 + j) * 4
                for kt in range(8):
                    mk.mm(pm[:, o:o + NJ], w[:, kt, j * 128:(j + 1) * 128], k.scT[:, kt, :], start=(kt == 0), stop=(kt == 7))
        mk.tt("dve", k.modT[:], pm[:, 0:192].rr("p (a b) -> p a b", b=4)[:, :, 0:NJ], vbc(vus(mb[:], 2), [128, 48, NJ]), ALU.add)
        for (A, w_i, sc0) in ((k.A1, 0, 8), (k.A2, 1, 32)):
            mk.ts("dve", A[:], k.modT[:, sc0:sc0 + 8, :], 1.0, None, op0=ALU.add)
            mk.tt("dve", A[:], A[:], vbc(vus(nw[:, w_i, :], 2), [128, 8, NJ]), ALU.mult)


def rms_mod(k, hg, n, j, A, sh0, zt_out, tmp, sq, rs, ps):
    mk = k.mk
    mk.act(sq[:, :, :n], hg[:, :, :n], AF.Square)
    for kt in range(8):
        mk.mm(ps[:, :n], k.ones[:], sq[:, kt, :n], start=(kt == 0), stop=(kt == 7))
    mk.act(rs[:, :n], ps[:, :n], AF.Sqrt, scale=1.0 / D, bias=EPS)
    mk.I("dve", "reciprocal", out=rs[:, :n], in_=rs[:, :n])
    mk.tt("dve", tmp[:, :, :n], hg[:, :, :n], vbc(vus(rs[:, :n], 1), [128, 8, n]), ALU.mult)
    for kt in range(8):
        if kt % 2 == 0:
            mk.ts("dve", zt_out(kt), tmp[:, kt, :n], A[:, kt, j:j + 1], k.modT[:, sh0 + kt, j:j + 1], op0=ALU.mult, op1=ALU.add)
        else:
            mk.act(zt_out(kt), tmp[:, kt, :n], AF.Identity, scale=A[:, kt, j:j + 1], bias=k.modT[:, sh0 + kt, j:j + 1])


def stage_s1(k, l, b):
    mk, cfg = k.mk, k.cfg
    NT = cfg.NT
    hv = hT_view(k, b)
    with scope(mk):
        ZT = mk.sb("s1_zt", [128, 8, NT], BF16)
        with scope(mk):
            hg = [mk.sb(f"s1_h{i}", [128, 8, 512]) for i in range(2)]
            sq = mk.sb("s1_sq", [128, 8, 512])
            tmp = mk.sb("s1_tmp", [128, 8, 512])
            rs = mk.sb("s1_rs", [128, 512])
            for gi, (t0, n, seg) in enumerate(cfg.groups):
                h = hg[gi % 2]
                mk.dma("sp", h[:, :, :n], hv[:, :, t0:t0 + n])
                j = cfg.NB if seg == 0 else b
                rms_mod(k, h, n, j, k.A1, 0, lambda kt: ZT.p(gi, (slice(None), kt, slice(t0, t0 + n))), tmp, sq, rs, k.PS[gi % 2])
        with scope(mk):
            wf = [mk.sb(f"s1_wf{i}", [128, 8, 128]) for i in range(2)]
            wb = [mk.sb(f"s1_wb{i}", [128, 8, 128], BF16) for i in range(2)]
            stg = [mk.sb(f"s1_st{i}", [128, NT]) for i in range(2)]
            ev = 0
            for blk in range(NFM):
                s = blk % 2
                mk.dma("sp", wf[s][:], k.w_fm[l, blk])
                mk.cp("pool", wb[s][:], wf[s][:])
                for gi, (t0, n, seg) in enumerate(cfg.groups):
                    ps = k.PS[ev % 4]
                    for kt in range(8):
                        mk.mm(ps[:, :n], wb[s][:, kt, :], ZT.p(gi, (slice(None), kt, slice(t0, t0 + n))), start=(kt == 0), stop=(kt == 7))
                    mk.cp("act" if ev % 2 else "dve", stg[s].p(gi, (slice(None), slice(t0, t0 + n))), ps[:, :n])
                    ev += 1
                allp = V(stg[s].ap(), [stg[s].parts[g] for g in range(len(cfg.groups))], stg[s])
                mk.dma(STQ, k.uT[b].p(blk, (slice(blk * 128, (blk + 1) * 128), slice(None))), allp)
        with scope(mk):
            wtm = mk.sb("s1_wtm", [128, 8, NTM], BF16)
            wst = [mk.sb(f"s1_wst{i}", [128, 8, 432]) for i in range(2)]
            stt_ = [mk.sb(f"s1_stt{i}", [128, NTM]) for i in range(2)]
            for pc in range(3):
                mk.dma("sp", wst[pc % 2][:], k.w_tm[l, :, :, pc * 432:(pc + 1) * 432])
                mk.cp("pool", wtm.p(pc, (slice(None), slice(None), slice(pc * 432, (pc + 1) * 432))), wst[pc % 2][:])
            wall = lambda kt, c0, cn: V(wtm.h[:, kt, c0:c0 + cn], [wtm.parts[p] for p in range(3)])
            ev = 0
            for ti in range(cfg.ntile):
                gi = cfg.gi_of(ti * 128)
                s = ti % 2
                for (c0, cn) in ((0, 512), (512, 512), (1024, NTM - 1024)):
                    ps = k.PS[ev % 4]
                    for kt in range(8):
                        mk.mm(ps[:, :cn], ZT.p(gi, (slice(None), kt, slice(ti * 128, (ti + 1) * 128))), wall(kt, c0, cn), start=(kt == 0), stop=(kt == 7))
                    mk.cp("act" if ev % 2 else "dve", stt_[s][:, c0:c0 + cn], ps[:, :cn])
                    ev += 1
                mk.dma(STQ, k.utok[b].p(ti, (slice(ti * 128, (ti + 1) * 128), slice(None))), stt_[s][:])


def uT_rows(k, b, blk):
    return k.uT[b].p(blk, (slice(blk * 128, (blk + 1) * 128), slice(None)))


def conv4(mk, out, x, w, segs):
    for (s0, sn) in segs:
        e = s0 + sn
        mk.ts("dve", out[:, s0:e], x[:, s0:e], w[:, 1:2], None, op0=ALU.mult)
        mk.stt(out[:, s0 + 1:e], x[:, s0:e - 1], w[:, 0:1], out[:, s0 + 1:e], ALU.mult, ALU.add)
        mk.stt(out[:, s0:e - 1], x[:, s0 + 1:e], w[:, 2:3], out[:, s0:e - 1], ALU.mult, ALU.add)
        mk.stt(out[:, s0:e - 2], x[:, s0 + 2:e], w[:, 3:4], out[:, s0:e - 2], ALU.mult, ALU.add)


def softplus(mk, out, x, t1):
    mk.act(t1, x, AF.Abs)
    mk.act(t1, t1, AF.Exp, scale=-1.0)
    mk.act(t1, t1, AF.Ln, bias=1.0)
    mk.stt(out, x, 0.0, t1, ALU.max, ALU.add)


def stage_lru(k, l, b):
    mk, cfg = k.mk, k.cfg
    NT, C = cfg.NT, cfg.C
    segs = [(0, C), (C, cfg.T)]
    with scope(mk):
        cw = mk.sb("lr_cw", [128, 4, 4]); cb = mk.sb("lr_cb", [128, 4])
        wa = mk.sb("lr_wa", [128, 2, 4, 128]); wx = mk.sb("lr_wx", [128, 2, 4, 128])
        vv = mk.sb("lr_v", [128, 3, 2, 4]); csp = mk.sb("lr_csp", [128, 2, 4]); t1 = mk.sb("lr_t1", [128, 2, 4])
        mk.dma("sp", cw[:], k.lru_cw[l]); mk.dma("sp", cb[:], k.lru_cb[l])
        mk.dma("sp", wa[:], k.lru_wa[l]); mk.dma("sp", wx[:], k.lru_wx[l])
        mk.dma("sp", vv[:], k.lru_v[l])
        mk.ts("dve", csp[:], vv[:, 2], -1.0, None, op0=ALU.mult)
        softplus(mk, csp[:], csp[:], t1[:])
        mk.ts("dve", csp[:], csp[:], -8.0, None, op0=ALU.mult)
        xb = mk.sb("lr_xb", [128, NT]); gb = mk.sb("lr_gb", [128, NT]); xc = mk.sb("lr_xc", [128, NT])
        R = mk.sb("lr_r", [128, NT]); IG = mk.sb("lr_ig", [128, NT]); A = mk.sb("lr_a", [128, NT])
        H = [mk.sb(f"lr_h{d}", [128, NT]) for d in range(2)]
        ev = 0
        for h in range(4):
            mk.dma("sp", xb[:], uT_rows(k, b, LRX + h))
            mk.dma("sp", gb[:], uT_rows(k, b, LRG + h))
            conv4(mk, xc, xb, cw[:, h, :], segs)
            mk.ts("dve", xc[:], xc[:], cb[:, h:h + 1], None, op0=ALU.add)
            for d in range(2):
                for (t0, n, seg) in cfg.groups:
                    pa, px = k.PS[ev % 2], k.PS[2 + ev % 2]
                    ev += 1
                    mk.mm(pa[:, :n], wa[:, d, h, :], xc[:, t0:t0 + n])
                    mk.mm(px[:, :n], wx[:, d, h, :], xc[:, t0:t0 + n])
                    mk.act(R[:, t0:t0 + n], pa[:, :n], AF.Sigmoid, bias=vv[:, 0, d, h:h + 1])
                    mk.act(IG[:, t0:t0 + n], px[:, :n], AF.Sigmoid, bias=vv[:, 1, d, h:h + 1])
                mk.act(A[:], R[:], AF.Exp, scale=csp[:, d, h:h + 1])
                mk.tt("dve", R[:], A[:], A[:], ALU.mult)
                mk.ts("dve", R[:], R[:], -1.0, 1.0, op0=ALU.mult, op1=ALU.add)
                mk.ts("dve", R[:], R[:], 0.0, None, op0=ALU.max)
                mk.act(R[:], R[:], AF.Sqrt)
                mk.tt("dve", R[:], R[:], IG[:], ALU.mult)
                mk.tt("dve", R[:], R[:], xc[:], ALU.mult)
                if d == 0:
                    mk.I("dve", "tensor_tensor_scan", out=H[0][:], data0=A[:], data1=R[:], initial=0.0, op0=ALU.mult, op1=ALU.add)
                else:
                    mk.I("dve", "tensor_tensor_scan", out=H[1][:, C - 1::-1] if False else V(H[1].h[:, 0:C][:, ::-1], [H[1].trk]),
                         data0=V(A.h[:, 0:C][:, ::-1], [A.trk]), data1=V(R.h[:, 0:C][:, ::-1], [R.trk]),
                         initial=0.0, op0=ALU.mult, op1=ALU.add)
                    mk.I("dve", "tensor_tensor_scan", out=V(H[1].h[:, C:NT][:, ::-1], [H[1].trk]),
                         data0=V(A.h[:, C:NT][:, ::-1], [A.trk]), data1=V(R.h[:, C:NT][:, ::-1], [R.trk]),
                         initial=H[1][:, 0:1], op0=ALU.mult, op1=ALU.add)
            mk.tt("dve", H[0][:], H[0][:], H[1][:], ALU.add)
            mk.act(gb[:], gb[:], AF.Gelu)
            mk.tt("dve", H[0][:], H[0][:], gb[:], ALU.mult)
            mk.dma(STQ, k.yT[b][2].p(h, (slice(h * 128, (h + 1) * 128), slice(None))), H[0][:])


def stage_att(k, l, b, need_ctx):
    mk, cfg = k.mk, k.cfg
    NT, C, ntile = cfg.NT, cfg.C, cfg.ntile
    with scope(mk):
        QT = mk.sb("at_qt", [64, 8, NT], BF16)
        KT = mk.sb("at_kt", [64, 2, NT], BF16)
        VA = mk.sb("at_va", [128, ntile, 2, 65], BF16)
        nw = mk.sb("at_nw", [128, 2, 64])
        mk.dma("sp", nw[:], k.att_nw[l])
        mk._memset("pool", VA[:], 1.0)
        with scope(mk):
            a_ = [mk.sb(f"at_a{i}", [128, 768]) for i in range(2)]
            cs_ = [mk.sb(f"at_cs{i}", [128, 2, 2, 16]) for i in range(2)]
            sq = mk.sb("at_sq", [128, 640]); ss = mk.sb("at_ss", [128, 10])
            qn = mk.sb("at_qn", [128, 640]); qr = mk.sb("at_qr", [128, 640])
            tt_ = [mk.sb(f"at_t{i}", [128, 10, 2, 16]) for i in range(4)]
            for ti in range(ntile):
                a = a_[ti % 2]
                mk.dma("sp", a[:], k.utok[b].p(ti, (slice(ti * 128, (ti + 1) * 128), slice(TM_ATQ, TM_ATQ + 768))))
                mk.tt("dve", sq[:], a[:, 0:640], a[:, 0:640], ALU.mult)
                mk.I("dve", "tensor_reduce", out=ss[:], in_=sq[:].rr("p (h d) -> p h d", d=64), axis=AX.X, op=ALU.add)
                mk.act(ss[:], ss[:], AF.Sqrt, scale=1.0 / 64, bias=EPS)
                mk.I("dve", "reciprocal", out=ss[:], in_=ss[:])
                mk.tt("dve", qn[:].rr("p (h d) -> p h d", d=64), a[:, 0:640].rr("p (h d) -> p h d", d=64), vbc(vus(ss[:], 2), [128, 10, 64]), ALU.mult)
                mk.tt("pool", qn[:, 0:512].rr("p (h d) -> p h d", d=64), qn[:, 0:512].rr("p (h d) -> p h d", d=64), vbc(vus(nw[:, 0, :], 1), [128, 8, 64]), ALU.mult)
                mk.tt("pool", qn[:, 512:640].rr("p (h d) -> p h d", d=64), qn[:, 512:640].rr("p (h d) -> p h d", d=64), vbc(vus(nw[:, 1, :], 1), [128, 2, 64]), ALU.mult)
                if ti * 128 >= C:
                    cs = cs_[ti % 2]
                    tl = ti * 128 - C
                    mk.dma("sp", cs[:], k.c_rope[tl:tl + 128])
                    xv = qn[:].rr("p (h a f r) -> p h a f r", h=10, a=2, f=2)
                    ov = qr[:].rr("p (h a f r) -> p h a f r", h=10, a=2, f=2)
                    x1, x2 = xv[:, :, :, 0, :], xv[:, :, :, 1, :]
                    cc = vbc(vus(cs[:, 0], 1), [128, 10, 2, 16]); sn = vbc(vus(cs[:, 1], 1), [128, 10, 2, 16])
                    mk.tt("dve", tt_[0][:], x1, cc, ALU.mult)
                    mk.tt("pool", tt_[1][:], x2, sn, ALU.mult)
                    mk.tt("dve", tt_[2][:], x2, cc, ALU.mult)
                    mk.tt("pool", tt_[3][:], x1, sn, ALU.mult)
                    mk.tt("dve", ov[:, :, :, 0, :], tt_[0][:], tt_[1][:], ALU.subtract)
                    mk.tt("dve", ov[:, :, :, 1, :], tt_[2][:], tt_[3][:], ALU.add)
                    src = qr
                else:
                    src = qn
                for hh in range(10):
                    ps = k.PS[hh // 4]
                    o = (hh % 4) * 128
                    mk.tr(ps[0:64, o:o + 128], src[:, hh * 64:(hh + 1) * 64], k.ident[:])
                mk.cp("act", QT[:, 0:4, ti * 128:(ti + 1) * 128], k.PS[0][0:64, :].rr("p (a b) -> p a b", a=4))
                mk.cp("dve", QT[:, 4:8, ti * 128:(ti + 1) * 128], k.PS[1][0:64, :].rr("p (a b) -> p a b", a=4))
                mk.cp("act", KT[:, 0:2, ti * 128:(ti + 1) * 128], k.PS[2][0:64, 0:256].rr("p (a b) -> p a b", a=2))
                mk.cp("pool", VA[:, ti, :, 0:64], a[:, 640:768].rr("p (g d) -> p g d", g=2))
        with scope(mk):
            OT = [mk.sb(f"at_ot{i}", [128, 512]) for i in range(2)]
            PT = [mk.sb(f"at_pt{i}", [128, 512], BF16) for i in range(3)]
            YS = [mk.sb(f"at_ys{i}", [128, 4, 128]) for i in range(2)]
            rc = mk.sb("at_rc", [128, 8])
            yv = V(k.yT[b][3].ap().rearrange("(ft p) t -> p ft t", p=128), [k.yT[b][3].trk])
            it = 0
            qtiles = list(range(ntile)) if need_ctx else list(range(C // 128, ntile))
            for qn_i, qi in enumerate(qtiles):
                keyt = list(range(C // 128)) if qi * 128 < C else list(range(ntile))
                ot = OT[qn_i % 2]
                items = [(g, kt) for g in range(2) for kt in keyt]

                def front(i, base=it):
                    g, kt = items[i]
                    S = k.PS[(0, 1, 7)[(base + i) % 3]]; pt = PT[(base + i) % 3]
                    mk.mm(S[:, :], KT[:, g, kt * 128:(kt + 1) * 128], QT[:, 4 * g:4 * g + 4, qi * 128:(qi + 1) * 128])
                    mk.act(pt[:], S[:], AF.Exp, scale=0.125)

                def back(i, base=it):
                    g, kt = items[i]
                    pt = PT[(base + i) % 3]
                    for hq in range(4):
                        mk.mm(k.PS[2 + hq][:, 0:65], pt[:, hq * 128:(hq + 1) * 128], VA[:, kt, g, :], start=(kt == keyt[0]), stop=(kt == keyt[-1]))
                    if kt == keyt[-1]:
                        for hq in range(4):
                            hh = 4 * g + hq
                            mk.I("dve", "reciprocal", out=rc[:, hh:hh + 1], in_=k.PS[2 + hq][:, 64:65])
                            mk.ts("dve", ot[:, hh * 64:(hh + 1) * 64], k.PS[2 + hq][:, 0:64], rc[:, hh:hh + 1], None, op0=ALU.mult)

                front(0)
                if len(items) > 1:
                    front(1)
                for i in range(len(items)):
                    if i + 2 < len(items):
                        front(i + 2)
                    back(i)
                it += len(items)
                for ft in range(4):
                    mk.tr(k.PS[6][:, ft * 128:(ft + 1) * 128], ot[:, ft * 128:(ft + 1) * 128], k.ident[:])
                ys = YS[qn_i % 2]
                mk.cp("act", ys[:], k.PS[6][:].rr("p (a b) -> p a b", a=4))
                mk.dma(STQ, yv[:, :, qi * 128:(qi + 1) * 128], ys[:])


def stage_dn(k, l, b):
    mk, cfg = k.mk, k.cfg
    NT, C, nch = cfg.NT, cfg.C, cfg.nch
    segs = [(0, C), (C, cfg.T)]
    LE, GE, GT, LT = 0, 1, 2, 3
    with scope(mk):
        graw = mk.sb("dn_graw", [64, nch, 16]); ab = mk.sb("dn_ab", [64, 2, 8])
        beta = mk.sb("dn_beta", [64, nch, 8]); g = mk.sb("dn_g", [64, nch, 8]); t1 = mk.sb("dn_t1", [64, nch, 8])
        gam = mk.sb("dn_gam", [64, nch, 8]); erg = mk.sb("dn_erg", [64, nch, 8]); cdec = mk.sb("dn_cdec", [128, nch, 8])
        eg = mk.sb("dn_eg", [64, nch, 8]); bg = mk.sb("dn_bg", [64, nch, 8]); ea = mk.sb("dn_ea", [64, 8])
        cw = mk.sb("dn_cw", [128, 12, 4])
        mk.dma("sp", cw[:], k.dn_cw[l])
        mk.dma("sp", ab[:], k.dn_ab[l])
        gsrc = k.utok[b].h[:, TM_BETA:TM_BETA + 16].rearrange("(n c) f -> c n f", c=64)
        for n0 in range(0, nch, 16):
            n1 = min(nch, n0 + 16)
            mk.dma("sp", graw[:, n0:n1, :], V(gsrc[:, n0:n1, :], [k.utok[b].trk] + list(k.utok[b].parts.values())))
        mk.act(beta[:], graw[:, :, 0:8], AF.Sigmoid)
        mk.tt("dve", g[:], graw[:, :, 8:16], vbc(vus(ab[:, 1, :], 1), [64, nch, 8]), ALU.add)
        softplus(mk, g[:], g[:], t1[:])
        mk.act(ea[:], ab[:, 0, :], AF.Exp)
        mk.tt("dve", g[:], g[:], vbc(vus(ea[:], 1), [64, nch, 8]), ALU.mult)
        mk.ts("dve", g[:], g[:], -1.0, None, op0=ALU.mult)
        for d in range(2):
            cs = slice(d * 4, d * 4 + 4)
            n4 = nch * 4
            mk.mm(k.PS[0][0:64, 0:n4], k.masks[:, LE if d == 0 else GE, :], g[:, :, cs])
            mk.cp("dve", gam[:, :, cs], k.PS[0][0:64, 0:n4].rr("p (n f) -> p n f", f=4))
            mk.mm(k.PS[1][0:64, 0:n4], k.masks[:, GT if d == 0 else LT, :], g[:, :, cs])
            mk.act(erg[:, :, cs], k.PS[1][0:64, 0:n4].rr("p (n f) -> p n f", f=4), AF.Exp)
            mk.mm(k.PS[2][:, 0:n4], k.ones[0:64, :], g[:, :, cs])
            mk.act(cdec[:, :, cs], k.PS[2][:, 0:n4].rr("p (n f) -> p n f", f=4), AF.Exp)
        mk.act(eg[:], gam[:], AF.Exp)
        mk.tt("dve", bg[:], beta[:], eg[:], ALU.mult)
        Qb = [mk.sb(f"dn_q{i}", [128, NT], BF16) for i in range(4)]
        Kb = [mk.sb(f"dn_k{i}", [128, NT], BF16) for i in range(4)]
        Vb = [mk.sb(f"dn_v{i}", [128, NT], BF16) for i in range(4)]
        with scope(mk):
            raws = [mk.sb(f"dn_raw{i}", [128, NT]) for i in range(2)]
            tmpc = mk.sb("dn_tmpc", [128, NT])
            sq = mk.sb("dn_sq", [128, 512]); rs = mk.sb("dn_rs", [128, 512])
            ri = 0
            for h in range(4):
                for (dst, blk0, wi, scl) in ((Qb[h], DNQ, 0, 128.0 ** -0.5), (Kb[h], DNK, 1, 1.0), (Vb[h], DNV, 2, None)):
                    raw = raws[ri % 2]; ri += 1
                    mk.dma("sp", raw[:], uT_rows(k, b, blk0 + h))
                    conv4(mk, tmpc, raw, cw[:, wi * 4 + h, :], segs)
                    mk.act(tmpc[:], tmpc[:], AF.Silu)
                    if scl is None:
                        mk.cp("pool", dst[:], tmpc[:])
                        continue
                    for gi, (t0, n, seg) in enumerate(cfg.groups):
                        ps = k.PS[gi % 2]
                        mk.act(sq[:, :n], tmpc[:, t0:t0 + n], AF.Square)
                        mk.mm(ps[:, :n], k.ones[:], sq[:, :n])
                        mk.act(rs[:, :n], ps[:, :n], AF.Sqrt, bias=EPS)
                        mk.I("dve", "reciprocal", out=rs[:, :n], in_=rs[:, :n])
                        mk.stt(dst[:, t0:t0 + n], tmpc[:, t0:t0 + n], scl, rs[:, :n], ALU.mult, ALU.mult)
        id64 = k.ident[0:64, 0:64]
        orders = [cfg.order(0), cfg.order(1)]

        def stream(hp):
            u = f"s{hp}"
            Q0, Q1, Q2, Q3 = k.PS[4 * hp:4 * hp + 4]
            S = mk.sb("dn_S" + u, [128, 4, 128]); Sb = mk.sb("dn_Sb" + u, [128, 4, 128], BF16)
            mk._memset("dve", S[:], 0.0)
            mk._memset("pool", Sb[:], 0.0)
            GD = mk.sb("dn_gd" + u, [64, 4, 64]); NM = mk.sb("dn_nm" + u, [64, 4, 64]); Dm = mk.sb("dn_dm" + u, [64, 4, 64])
            Ds = mk.sb("dn_ds" + u, [64, 4, 64]); Di = mk.sb("dn_di" + u, [64, 4, 64])
            AL = mk.sb("dn_al" + u, [64, 4, 64]); AQ = mk.sb("dn_aq" + u, [64, 4, 64]); AQT = mk.sb("dn_aqt" + u, [64, 4, 64])
            BB = mk.sb("dn_bb" + u, [64, 2, 4, 64]); X = mk.sb("dn_x" + u, [64, 4, 64])
            vb = mk.sb("dn_vb" + u, [64, 4, 128]); kbd = mk.sb("dn_kbd" + u, [64, 4, 128]); kd = mk.sb("dn_kd" + u, [64, 4, 128])
            U = mk.sb("dn_u" + u, [64, 4, 128]); WT = mk.sb("dn_wt" + u, [128, 4, 64]); VN = mk.sb("dn_vn" + u, [64, 4, 128])
            O2 = mk.sb("dn_o2" + u, [64, 4, 128]); Os = [mk.sb(f"dn_o{i}" + u, [64, 4, 128]) for i in range(2)]
            v4 = lambda p, a=4: p.rr("p (a b) -> p a b", a=a)
            for s in range(nch):
                info = []
                for c in range(4):
                    d, hh = c // 2, c % 2
                    n = orders[d][s]
                    info.append((d, 2 * hp + hh, n, n * 64, d * 4 + 2 * hp + hh))
                for c, (d, h, n, t0, col) in enumerate(info):
                    mk.ts("pool", GD[:, c, :], k.masks[:, LE if d == 0 else GE, :], g[:, n, col:col + 1], None, op0=ALU.mult)
                yield
                for c, (d, h, n, t0, col) in enumerate(info):
                    kT = Kb[h][:, t0:t0 + 64]
                    mk.mm(Q0[0:64, c * 64:(c + 1) * 64], k.ones[0:64, 0:64], GD[:, c, :])
                    mk.mm(Q0[0:64, 256 + c * 64:256 + (c + 1) * 64], kT, kT)
                    mk.mm(Q1[0:64, c * 64:(c + 1) * 64], Qb[h][:, t0:t0 + 64], kT)
                    mk.mm(Q3[0:64, c * 128:(c + 1) * 128], kT, k.identb[:])
                yield
                for c, (d, h, n, t0, col) in enumerate(info):
                    mk.ts("dve", NM[:, c, :], Q0[0:64, c * 64:(c + 1) * 64], gam[:, n, col:col + 1], 0.0, op0=ALU.subtract, op1=ALU.max)
                    mk.act(kbd[:, c, :], Q3[0:64, c * 128:(c + 1) * 128], AF.Identity, scale=bg[:, n, col:col + 1])
                yield
                mk.act(Dm[:], NM[:], AF.Exp, scale=-1.0)
                for c, (d, h, n, t0, col) in enumerate(info):
                    mk.act(kd[:, c, :], Q3[0:64, c * 128:(c + 1) * 128], AF.Identity, scale=erg[:, n, col:col + 1])
                yield
                for d in range(2):
                    dsl = slice(2 * d, 2 * d + 2)
                    mk.tt("pool", Ds[:, dsl, :], Dm[:, dsl, :], vbc(vus(k.masks[:, GT if d == 0 else LT, :], 1), [64, 2, 64]), ALU.mult)
                    mk.tt("pool", Di[:, dsl, :], Dm[:, dsl, :], vbc(vus(k.masks[:, GE if d == 0 else LE, :], 1), [64, 2, 64]), ALU.mult)
                for c, (d, h, n, t0, col) in enumerate(info):
                    mk.mm(Q3[0:64, c * 128:(c + 1) * 128], Vb[h][:, t0:t0 + 64], k.identb[:])
                yield
                for c, (d, h, n, t0, col) in enumerate(info):
                    mk.stt(AL[:, c, :], Q0[0:64, 256 + c * 64:256 + (c + 1) * 64], beta[:, n, col:col + 1], Ds[:, c, :], ALU.mult, ALU.mult)
                mk.tt("dve", AQ[:], v4(Q1[0:64, 0:256]), Di[:], ALU.mult)
                for c, (d, h, n, t0, col) in enumerate(info):
                    mk.act(vb[:, c, :], Q3[0:64, c * 128:(c + 1) * 128], AF.Identity, scale=beta[:, n, col:col + 1])
                yield
                for c in range(4):
                    mk.mm(Q2[0:64, c * 64:(c + 1) * 64], AL[:, c, :], id64)
                    mk.mm(Q2[0:64, 256 + c * 64:256 + (c + 1) * 64], AQ[:, c, :], id64)
                yield
                mk.ts("dve", BB[:, 0], v4(Q2[0:64, 0:256]), -1.0, None, op0=ALU.mult)
                mk.ts("pool", BB[:, 1], AL[:], -1.0, None, op0=ALU.mult)
                yield
                mk.tt("pool", X[:], BB[:, 0], vbc(vus(id64, 1), [64, 4, 64]), ALU.add)
                mk.cp("act", AQT[:], v4(Q2[0:64, 256:512]))
                for c in range(4):
                    mk.mm(Q0[0:64, c * 64:(c + 1) * 64], BB[:, 1, c, :], BB[:, 0, c, :])
                    mk.mm(Q0[0:64, 256 + c * 64:256 + (c + 1) * 64], BB[:, 0, c, :], BB[:, 1, c, :])
                yield
                mk.cp("act", BB[:], Q0[0:64, :].rr("p (t a b) -> p t a b", t=2, a=4))
                yield
                for lev in range(1, 6):
                    if lev < 5:
                        for c in range(4):
                            mk.mm(Q0[0:64, c * 64:(c + 1) * 64], BB[:, 1, c, :], BB[:, 0, c, :])
                            mk.mm(Q0[0:64, 256 + c * 64:256 + (c + 1) * 64], BB[:, 0, c, :], BB[:, 1, c, :])
                    for c in range(4):
                        mk.mm(Q1[0:64, 256 + c * 64:256 + (c + 1) * 64], BB[:, 1, c, :], X[:, c, :])
                    yield
                    if lev < 5:
                        mk.cp("act", BB[:], Q0[0:64, :].rr("p (t a b) -> p t a b", t=2, a=4))
                    mk.tt("dve", X[:], X[:], v4(Q1[0:64, 256:512]), ALU.add)
                    yield
                for c in range(4):
                    mk.mm(Q2[0:64, c * 128:(c + 1) * 128], X[:, c, :], vb[:, c, :])
                    mk.mm(Q1[:, c * 64:(c + 1) * 64], kbd[:, c, :], X[:, c, :])
                yield
                mk.cp("act", U[:], v4(Q2[0:64, :]))
                mk.cp("dve", WT[:], v4(Q1[:, 0:256]))
                yield
                for c in range(4):
                    mk.mm(Q0[0:64, c * 128:(c + 1) * 128], WT[:, c, :], S[:, c, :])
                yield
                mk.tt("dve", VN[:], U[:], v4(Q0[0:64, :]), ALU.subtract)
                yield
                for c, (d, h, n, t0, col) in enumerate(info):
                    mk.mm(Q1[0:64, c * 128:(c + 1) * 128], Qb[h][:, t0:t0 + 64], Sb[:, c, :])
                    mk.mm(Q2[0:64, c * 128:(c + 1) * 128], AQT[:, c, :], VN[:, c, :])
                    mk.mm(Q3[:, c * 128:(c + 1) * 128], kd[:, c, :], VN[:, c, :])
                yield
                mk.cp("act", O2[:], v4(Q2[0:64, :]))
                for c, (d, h, n, t0, col) in enumerate(info):
                    mk.stt(S[:, c, :], S[:, c, :], cdec[:, n, col:col + 1], Q3[:, c * 128:(c + 1) * 128], ALU.mult, ALU.add)
                mk.cp("pool", Sb[:], S[:])
                yield
                Ot = Os[s % 2]
                for c, (d, h, n, t0, col) in enumerate(info):
                    mk.stt(Ot[:, c, :], Q1[0:64, c * 128:(c + 1) * 128], eg[:, n, col:col + 1], O2[:, c, :], ALU.mult, ALU.add)
                for d in range(2):
                    t0 = orders[d][s] * 64
                    mk.dma(STQ, k.osc[b][d][t0:t0 + 64, hp * 256:(hp + 1) * 256], Ot[:, 2 * d:2 * d + 2, :].rr("p a b -> p (a b)"))
                yield

        nstream = getattr(cfg, "dn_streams", 2)
        if nstream == 2:
            gens = [stream(0), stream(1)]
            for _ in range(getattr(cfg, "dn_offset", 13)):
                next(gens[0])
            while gens:
                for gen in list(gens):
                    try:
                        next(gen)
                    except StopIteration:
                        gens.remove(gen)
        else:
            for hp in range(2):
                for _ in stream(hp):
                    pass


def stage_hg(k, l, b):
    mk, cfg = k.mk, k.cfg
    NT, C, nch = cfg.NT, cfg.C, cfg.nch
    LE, GE = 0, 1
    with scope(mk):
        rmask = mk.sb("hg_rm", [128, 2, NT], BF16)
        mk._memset("pool", rmask[:], 1.0)
        mk._memset("pool", rmask[:, 0, :].rr("p (n c) -> p n c", c=64)[:, :, 0:1], 0.0)
        mk._memset("pool", rmask[:, 1, :].rr("p (n c) -> p n c", c=64)[:, :, 63:64], 0.0)
        q = mk.sb("hg_q", [128, NT])
        v3 = lambda t: t[:].rr("p (n c) -> p n c", c=64)
        ST = []
        for d in range(2):
            u = f"d{d}"
            st = K()
            st.fr = mk.sb("hg_fr" + u, [128, NT]); st.Kf = mk.sb("hg_kf" + u, [128, NT], BF16); st.gc = mk.sb("hg_gc" + u, [128, NT])
            st.QG = mk.sb("hg_qg" + u, [128, NT], BF16); st.QTL = mk.sb("hg_qtl" + u, [128, NT], BF16); st.KD = mk.sb("hg_kd" + u, [128, NT], BF16)
            st.REF = mk.sb("hg_ref" + u, [128, nch, 4]); st.EL = mk.sb("hg_el" + u, [128, nch]); st.oml = mk.sb("hg_oml" + u, [128, 1])
            st.S = mk.sb("hg_S" + u, [128, 128]); st.Sb = mk.sb("hg_Sb" + u, [128, 128], BF16)
            st.vt = [mk.sb(f"hg_vt{i}" + u, [64, 128]) for i in range(2)]; st.vtb = [mk.sb(f"hg_vtb{i}" + u, [64, 128], BF16) for i in range(2)]
            st.SCT = [mk.sb(f"hg_sct{i}" + u, [64, 64], BF16) for i in range(2)]; st.kdt = [mk.sb(f"hg_kdt{i}" + u, [64, 128], BF16) for i in range(2)]
            st.Y = [mk.sb(f"hg_y{i}" + u, [128, 4, 64]) for i in range(2)]; st.K4 = [mk.sb(f"hg_k4{i}" + u, [128, 4, 64], BF16) for i in range(2)]
            st.Os = [mk.sb(f"hg_o{i}" + u, [64, 128]) for i in range(2)]
            ST.append(st)

        def stream(h, d):
            st = ST[d]
            P = k.PS[4 * d:4 * d + 4]
            fr, Kf, gc, QG, QTL, KD, REF, EL, oml, S, Sb = st.fr, st.Kf, st.gc, st.QG, st.QTL, st.KD, st.REF, st.EL, st.oml, st.S, st.Sb
            tmp = fr
            mk.dma("sp", fr[:], uT_rows(k, b, (HGFF if d == 0 else HGFB) + h))
            lbc = k.lb[:, d, l, h:h + 1]
            mk.ts("dve", oml[:], lbc, -1.0, 1.0, op0=ALU.mult, op1=ALU.add)
            mk.act(fr[:], fr[:], AF.Sigmoid)
            yield
            mk.ts("dve", fr[:], fr[:], oml[:, 0:1], lbc, op0=ALU.mult, op1=ALU.add)
            mk.ts("dve", Kf[:], fr[:], -1.0, 1.0, op0=ALU.mult, op1=ALU.add)
            mk.act(fr[:], fr[:], AF.Ln)
            yield
            if d == 0:
                mk.I("dve", "tensor_tensor_scan", out=gc[:], data0=rmask[:, 0, :], data1=fr[:], initial=0.0, op0=ALU.mult, op1=ALU.add)
            else:
                mk.I("dve", "tensor_tensor_scan", out=V(gc.h[:, ::-1], [gc.trk]), data0=V(rmask.h[:, 1, ::-1], [rmask.trk]),
                     data1=V(fr.h[:, ::-1], [fr.trk]), initial=0.0, op0=ALU.mult, op1=ALU.add)
            last = 63 if d == 0 else 0
            mk.act(tmp[:], gc[:], AF.Exp)
            yield
            mk.cp("dve", EL[:], v3(tmp)[:, :, last])
            mk.tt("dve", QG[:], q[:], tmp[:], ALU.mult)
            mk._memset("pool", REF[:], 0.0)
            if d == 0:
                mk.cp("pool", REF[:, :, 1:4], v3(gc)[:, :, 15:63:16])
            else:
                mk.cp("pool", REF[:, :, 0:3], v3(gc)[:, :, 16:64:16])
            yield
            g4 = gc[:].rr("p (n j r) -> p n j r", j=4, r=16)
            t4 = tmp[:].rr("p (n j r) -> p n j r", j=4, r=16)
            mk.tt("dve", t4, g4, vbc(vus(REF[:], 3), [128, nch, 4, 16]), ALU.subtract)
            mk.act(tmp[:], tmp[:], AF.Exp)
            yield
            mk.tt("dve", QTL[:], q[:], tmp[:], ALU.mult)
            mk.tt("dve", v3(tmp), vbc(v3(gc)[:, :, last:last + 1], [128, nch, 64]), v3(gc), ALU.subtract)
            mk.act(tmp[:], tmp[:], AF.Exp)
            yield
            mk.tt("dve", KD[:], tmp[:], Kf[:], ALU.mult)
            mk._memset("dve", S[:], 0.0)
            mk._memset("pool", Sb[:], 0.0)
            yield
            order = cfg.order(d)

            def pre(s):
                n = order[s]; t0 = n * 64; a = s % 2
                Y, K4 = st.Y[a], st.K4[a]
                mk.dma("sp", st.vt[a][:], V(k.utok[b].h[t0:t0 + 64, TM_HGI + h * 128:TM_HGI + (h + 1) * 128], [k.utok[b].parts[t0 // 128]]))
                mk.tt("dve", Y[:], vbc(vus(REF[:, n, :], 2), [128, 4, 64]), vbc(vus(gc[:, t0:t0 + 64], 1), [128, 4, 64]), ALU.subtract)
                mk.ts("dve", Y[:], Y[:], 60.0, None, op0=ALU.min)
                mk.cp("pool", st.vtb[a][:], st.vt[a][:])
                yield
                mk.act(Y[:], Y[:], AF.Exp)
                mk.mm(P[1][0:64, 0:128], KD[:, t0:t0 + 64], k.identb[:])
                yield
                mk.tt("dve", K4[:], Y[:], vbc(vus(Kf[:, t0:t0 + 64], 1), [128, 4, 64]), ALU.mult)
                mk.cp("act", st.kdt[a][:], P[1][0:64, 0:128])
                yield
                for J in range(4):
                    mk.mm(P[0][0:64, 16 * J:16 * J + 16], K4[:, J, :], QTL[:, t0 + 16 * J:t0 + 16 * J + 16])
                yield
                mk.tt("dve", st.SCT[a][:], P[0][0:64, 0:64], k.masks[:, LE if d == 0 else GE, :], ALU.mult)
                yield

            def main(s):
                n = order[s]; t0 = n * 64; a = s % 2
                mk.mm(P[2][0:64, 0:128], QG[:, t0:t0 + 64], Sb[:], start=True, stop=False)
                mk.mm(P[2][0:64, 0:128], st.SCT[a][:], st.vtb[a][:], start=False, stop=True)
                mk.mm(P[3][:, 0:128], st.kdt[a][:], st.vtb[a][:])
                yield
                mk.stt(S[:], S[:], EL[:, n:n + 1], P[3][:, 0:128], ALU.mult, ALU.add)
                mk.cp("act", st.Os[a][:], P[2][0:64, 0:128])
                yield
                mk.cp("pool", Sb[:], S[:])
                mk.dma(STQ, k.osc[b][2 + d][t0:t0 + 64, h * 128:(h + 1) * 128], st.Os[a][:])
                yield

            yield from pre(0)
            for s in range(len(order)):
                if s + 1 < len(order):
                    yield from pre(s + 1)
                yield from main(s)

        for h in range(4):
            mk.dma("sp", q[:], uT_rows(k, b, HGQ + h))
            mk.act(q[:], q[:], AF.Silu)
            gens = [stream(h, 0), stream(h, 1)]
            for _ in range(getattr(cfg, "hg_offset", 0)):
                next(gens[0])
            while gens:
                for gen in list(gens):
                    try:
                        next(gen)
                    except StopIteration:
                        gens.remove(gen)


def stage_post(k, l, b, which, nwT, gblk):
    mk, cfg = k.mk, k.cfg
    with scope(mk):
        nw = mk.sb("po_nw", [128, 1])
        mk.dma("sp", nw[:], nwT[l])
        of = [mk.sb(f"po_of{i}", [128, 512]) for i in range(2)]; ob = [mk.sb(f"po_ob{i}", [128, 512]) for i in range(2)]
        G = [mk.sb(f"po_g{i}", [128, 4, 128]) for i in range(2)]; Y = [mk.sb(f"po_y{i}", [128, 4, 128]) for i in range(2)]
        sq = mk.sb("po_sq", [128, 512]); ss = mk.sb("po_ss", [128, 4])
        uv = V(k.uT[b].ap().rearrange("(x p) t -> p x t", p=128), [k.uT[b].trk] + list(k.uT[b].parts.values()))
        yv = V(k.yT[b][which].ap().rearrange("(ft p) t -> p ft t", p=128), [k.yT[b][which].trk])
        for ti in range(cfg.ntile):
            s = ti % 2
            rows = slice(ti * 128, (ti + 1) * 128)
            mk.dma("sp", of[s][:], k.osc[b][2 * which][rows, :])
            mk.dma("sp", ob[s][:], k.osc[b][2 * which + 1][rows, :])
            mk.dma("sp", G[s][:], uv[:, gblk:gblk + 4, rows])
            o = of[s]
            mk.tt("dve", o[:], o[:], ob[s][:], ALU.add)
            mk.tt("pool", sq[:], o[:], o[:], ALU.mult)
            mk.I("dve", "tensor_reduce", out=ss[:], in_=sq[:].rr("p (h d) -> p h d", d=128), axis=AX.X, op=ALU.add)
            mk.act(ss[:], ss[:], AF.Sqrt, scale=1.0 / 128, bias=EPS)
            mk.I("dve", "reciprocal", out=ss[:], in_=ss[:])
            mk.tt("dve", o[:].rr("p (h d) -> p h d", d=128), o[:].rr("p (h d) -> p h d", d=128), vbc(vus(ss[:], 2), [128, 4, 128]), ALU.mult)
            ps = k.PS[s]
            for h in range(4):
                mk.tr(ps[:, h * 128:(h + 1) * 128], o[:, h * 128:(h + 1) * 128], k.ident[:])
            mk.act(G[s][:], G[s][:], AF.Silu)
            mk.stt(Y[s][:].rr("p a b -> p (a b)"), ps[:], nw[:, 0:1], G[s][:].rr("p a b -> p (a b)"), ALU.mult, ALU.mult)
            mk.dma(STQ, yv[:, :, rows], Y[s][:])


def load_cast(mk, dst_fn, src_fn, npieces, stage_tiles, semL):
    for pc in range(npieces):
        st = stage_tiles[pc % 2]
        sv = src_fn(pc)
        mk.dma("sp", V(st.h[tuple(slice(0, x) for x in sv.ap.shape)], [st.trk], st), sv)
        mk.cp("pool" if pc % 2 else "dve", dst_fn(pc), V(st.h[tuple(slice(0, x) for x in sv.ap.shape)], [st.trk], st))


def stage_s3a(k, l, need_ctx):
    mk, cfg = k.mk, k.cfg
    NB = cfg.NB
    with scope(mk):
        wbr = mk.sb("a_wbr", [128, 4, 4, D], BF16)
        wo = mk.sb("a_wo", [128, 8, D], BF16)
        with scope(mk):
            stg = [mk.sb(f"a_stg{i}", [128, 4, D]) for i in range(2)]
            load_cast(mk, lambda pc: wbr[:, pc], lambda pc: k.w_br[l, :, pc], 4, stg, 1)
            load_cast(mk, lambda pc: wo[:, pc * 4:(pc + 1) * 4, :], lambda pc: k.w_out[l, :, pc * 4:(pc + 1) * 4, :], 2, stg, 1)
        yt = [mk.sb(f"a_yt{i}", [128, 4, 512]) for i in range(2)]
        ytb = mk.sb("a_ytb", [128, 4, 4, 512], BF16)
        G = [mk.sb(f"a_g{i}", [128, 4, 512]) for i in range(2)]
        merged = mk.sb("a_mg", [128, 8, 512], BF16)
        macc = mk.sb("a_macc", [128, 512]); mt = mk.sb("a_mt", [128, 512])
        hg = [mk.sb(f"a_h{i}", [128, 8, 512]) for i in range(2)]
        it = 0
        ev = 0
        for b in range(NB):
            hv = hT_view(k, b)
            uv = V(k.uT[b].ap().rearrange("(x p) t -> p x t", p=128), [k.uT[b].trk] + list(k.uT[b].parts.values()))
            for gi, (t0, n, seg) in enumerate(cfg.groups):
                if seg == 0 and not need_ctx:
                    continue
                j = NB if seg == 0 else b
                h_ = hg[it % 2]
                it += 1
                mk.dma("sp", h_[:, :, :n], hv[:, :, t0:t0 + n])
                for bi in range(4):
                    y_ = yt[bi % 2]
                    yv = V(k.yT[b][bi].ap().rearrange("(ft p) t -> p ft t", p=128), [k.yT[b][bi].trk] + list(k.yT[b][bi].parts.values()))
                    mk.dma("sp", y_[:, :, :n], yv[:, :, t0:t0 + n])
                    mk.cp("pool" if bi % 2 else "dve", ytb[:, bi, :, :n], y_[:, :, :n])
                for c in range(8):
                    g_ = G[c % 2]
                    mk.dma("sp", g_[:, :, :n], uv[:, MGB + c:MGB + c + 25:8, t0:t0 + n])
                    mk.act(g_[:, :, :n], g_[:, :, :n], AF.Sigmoid)
                    for bi in range(4):
                        ps = k.PS[ev % 4]
                        ev += 1
                        for ft in range(4):
                            mk.mm(ps[:, :n], wbr[:, bi, ft, c * 128:(c + 1) * 128], ytb[:, bi, ft, :n], start=(ft == 0), stop=(ft == 3))
                        if bi == 0:
                            mk.tt("dve", macc[:, :n], ps[:, :n], g_[:, 0, :n], ALU.mult)
                        else:
                            mk.tt("dve", mt[:, :n], ps[:, :n], g_[:, bi, :n], ALU.mult)
                            mk.tt("pool", macc[:, :n] if bi < 3 else merged[:, c, :n], macc[:, :n], mt[:, :n], ALU.add)
                for c in range(8):
                    ps = k.PS[4 + c % 4]
                    for kt in range(8):
                        mk.mm(ps[:, :n], wo[:, kt, c * 128:(c + 1) * 128], merged[:, kt, :n], start=(kt == 0), stop=(kt == 7))
                    mk.stt(h_[:, c, :n], ps[:, :n], k.modT[:, 16 + c, j:j + 1], h_[:, c, :n], ALU.mult, ALU.add)
                mk.dma(STQ, hv[:, :, t0:t0 + n], h_[:, :, :n])


def stage_s3b(k, l, need_ctx):
    mk, cfg = k.mk, k.cfg
    NB = cfg.NB
    with scope(mk):
        w1b = mk.sb("b_w1", [128, 8, 4 * D], BF16)
        w2b = mk.sb("b_w2", [128, 32, D], BF16)
        with scope(mk):
            stg = [mk.sb(f"b_stg{i}", [128, 8, 512]) for i in range(2)]
            load_cast(mk, lambda pc: w1b[:, :, pc * 512:(pc + 1) * 512], lambda pc: k.w1[l, :, :, pc * 512:(pc + 1) * 512], 8, stg, 1)
            stg2 = [V(stg[i].h[:].rearrange("p a b -> p (a b)").rearrange("p (a b) -> p a b", a=4), [stg[i].trk], stg[i]) for i in range(2)]
            for pc in range(8):
                sv = k.w2[l, :, pc * 4:(pc + 1) * 4, :]
                mk.dma("sp", stg2[pc % 2], sv)
                mk.cp("pool" if pc % 2 else "dve", w2b[:, pc * 4:(pc + 1) * 4, :], stg2[pc % 2])
        G = 256
        hg = [mk.sb(f"b_h{i}", [128, 8, G]) for i in range(2)]
        sq = mk.sb("b_sq", [128, 8, G]); tmp = mk.sb("b_tmp", [128, 8, G]); rs = mk.sb("b_rs", [128, G])
        z2 = mk.sb("b_z2", [128, 8, G], BF16)
        hid = mk.sb("b_hid", [128, 32, G], BF16)
        rl = [mk.sb(f"b_rl{i}", [128, G]) for i in range(2)]
        it = 0
        ev = 0
        for b in range(NB):
            hv = hT_view(k, b)
            subs = []
            for (t0, n, seg) in cfg.groups:
                if seg == 0 and not need_ctx:
                    continue
                for o in range(0, n, G):
                    subs.append((t0 + o, min(G, n - o), seg))
            for (t0, n, seg) in subs:
                j = NB if seg == 0 else b
                h_ = hg[it % 2]
                it += 1
                mk.dma("sp", h_[:, :, :n], hv[:, :, t0:t0 + n])
                rms_mod(k, h_, n, j, k.A2, 24, lambda kt: z2[:, kt, :n], tmp, sq, rs, k.PS[7])
                for fc in range(32):
                    ps = k.PS[ev % 4]
                    r_ = rl[ev % 2]
                    ev += 1
                    for kt in range(8):
                        mk.mm(ps[:, :n], w1b[:, kt, fc * 128:(fc + 1) * 128], z2[:, kt, :n], start=(kt == 0), stop=(kt == 7))
                    mk.act(r_[:, :n], ps[:, :n], AF.Relu)
                    mk.tt("pool" if ev % 2 else "dve", hid[:, fc, :n], r_[:, :n], r_[:, :n], ALU.mult)
                for c in range(8):
                    ps = k.PS[4 + c % 2]
                    for ft in range(32):
                        mk.mm(ps[:, :n], w2b[:, ft, c * 128:(c + 1) * 128], hid[:, ft, :n], start=(ft == 0), stop=(ft == 31))
                    mk.stt(h_[:, c, :n], ps[:, :n], k.modT[:, 40 + c, j:j + 1], h_[:, c, :n], ALU.mult, ALU.add)
                mk.dma(STQ, hv[:, :, t0:t0 + n], h_[:, :, :n])


_CACHE = {}


def host_consts(cfg):
    i = np.arange(64)
    kk, ff = i[:, None], i[None, :]
    masks = np.stack([(kk <= ff), (kk >= ff), (kk > ff), (kk < ff)], axis=1).astype(np.float32)
    rows = cfg.T // 64
    row_id = np.repeat(np.arange(rows), 64).astype(np.float32)
    col_id = np.tile(np.arange(64), rows).astype(np.float32)
    inv = (10000.0 ** (-np.arange(0, 32, 2, dtype=np.float32) / 32.0)).astype(np.float32)
    ang = np.stack([row_id[:, None] * inv, col_id[:, None] * inv], axis=1).astype(np.float32)
    rope = np.stack([np.cos(ang), np.sin(ang)], axis=1).astype(np.float32)
    return {"c_ident": np.eye(128, dtype=np.float32), "c_masks": np.ascontiguousarray(masks), "c_rope": np.ascontiguousarray(rope)}


def fm_cols():
    offs = np.cumsum([0, 512, 512, 512, 512, 8, 8, 512, 512, 512, 512, 512, 512, 512, 512, 128, 128, 4096])
    (dq, dk, dv, dg, dbeta, dalpha, hq, hff, hfb, hi, hgate, lx, lg, aq, ak, av, mg) = offs[:17]
    fm = []
    for base in (dq, dk, dv, dg, hq, hff, hfb, hgate, lx, lg):
        fm.append(np.arange(base, base + 512))
    fm.append(np.arange(mg, mg + 4096))
    fm = np.concatenate(fm)
    tm = np.concatenate([np.arange(hi, hi + 512), np.arange(aq, aq + 512), np.arange(ak, ak + 128), np.arange(av, av + 128),
                         np.arange(dbeta, dbeta + 8), np.arange(dalpha, dalpha + 8)])
    assert fm.size == NFM * 128 and tm.size == NTM
    return fm, tm


def pk(w):
    r = w.shape[0] // 128
    return np.ascontiguousarray(w.reshape(r, 128, *w.shape[1:]).swapaxes(0, 1))


def host_weights(inp, L):
    f = np.float32
    fm, tm = fm_cols()
    W = {}
    W["mod_w"] = np.stack([pk(inp["mod_w"][l]) for l in range(L)])
    W["mod_b"] = np.stack([np.ascontiguousarray(inp["mod_b"][l].reshape(48, 128).T) for l in range(L)])
    W["n1w"] = np.stack([np.ascontiguousarray(inp["norm1_w"][l].reshape(8, 128).T) for l in range(L)])
    W["n2w"] = np.stack([np.ascontiguousarray(inp["norm2_w"][l].reshape(8, 128).T) for l in range(L)])
    wfm = []
    for l in range(L):
        w = inp["w_in"][l][:, fm]
        w = w.reshape(8, 128, NFM, 128).transpose(2, 1, 0, 3)
        wfm.append(np.ascontiguousarray(w))
    W["w_fm"] = np.stack(wfm)
    W["w_tm"] = np.stack([pk(np.ascontiguousarray(inp["w_in"][l][:, tm])) for l in range(L)])
    W["dn_cw"] = np.stack([np.ascontiguousarray(inp["dn_conv_w"][l].reshape(4, 12, 128).transpose(2, 1, 0)) for l in range(L)])
    W["dn_ab"] = np.stack([np.broadcast_to(np.stack([inp["dn_a_log"][l].reshape(8), inp["dn_dt_bias"][l].reshape(8)])[None], (64, 2, 8)).copy() for l in range(L)])
    W["dn_nw"] = np.ascontiguousarray(inp["dn_norm_w"][:L].reshape(L, 128, 1))
    W["hg_nw"] = np.ascontiguousarray(inp["hg_norm_w"][:L].reshape(L, 128, 1))
    W["hg_lb"] = np.ascontiguousarray(inp["hg_lower_bounds"][:, :L].reshape(2, L, 4, 128).transpose(3, 0, 1, 2))
    W["lru_cw"] = np.stack([np.ascontiguousarray(inp["lru_conv_w"][l].reshape(4, 4, 128).transpose(2, 1, 0)) for l in range(L)])
    W["lru_cb"] = np.stack([np.ascontiguousarray(inp["lru_conv_b"][l].reshape(4, 128).T) for l in range(L)])
    W["lru_wa"] = np.stack([np.ascontiguousarray(inp["lru_w_a"][l].transpose(2, 0, 1, 3)) for l in range(L)])
    W["lru_wx"] = np.stack([np.ascontiguousarray(inp["lru_w_x"][l].transpose(2, 0, 1, 3)) for l in range(L)])
    W["lru_v"] = np.stack([np.ascontiguousarray(np.stack([inp["lru_b_a"][l], inp["lru_b_x"][l], inp["lru_lambda"][l]]).reshape(3, 2, 4, 128).transpose(3, 0, 1, 2)) for l in range(L)])
    W["att_nw"] = np.stack([np.broadcast_to(np.stack([inp["att_q_norm_w"][l], inp["att_k_norm_w"][l]])[None], (128, 2, 64)).copy() for l in range(L)])
    W["w_br"] = np.stack([np.ascontiguousarray(inp["w_branch"][l].reshape(4, 4, 128, D).transpose(2, 0, 1, 3)) for l in range(L)])
    W["w_out"] = np.stack([pk(inp["w_out"][l]) for l in range(L)])
    W["w1"] = np.stack([pk(inp["mlp_w1"][l]) for l in range(L)])
    W["w2"] = np.stack([pk(inp["mlp_w2"][l]) for l in range(L)])
    return {k_: np.ascontiguousarray(v, dtype=f) for k_, v in W.items()}


def run(inp, cfg, n_cores):
    key = (cfg.T, cfg.C, cfg.NB, cfg.L, cfg.debug)
    if key not in _CACHE:
        _CACHE[key] = build(cfg)
    nc, mk = _CACHE[key]
    inp = {k_: np.asarray(v, dtype=np.float32) for k_, v in inp.items()}
    W = host_weights(inp, cfg.L)
    W.update(host_consts(cfg))
    in_maps = []
    NB = cfg.NB
    for ci in range(n_cores):
        bs = slice(ci * NB, (ci + 1) * NB)
        m = dict(W)
        m["x_in"] = np.ascontiguousarray(np.concatenate([inp["ctx"][bs], inp["x"][bs]], axis=1))
        cv = np.concatenate([inp["c"][bs], inp["c_ctx"][None]], axis=0)
        m["cT"] = np.ascontiguousarray(cv.T.reshape(8, 128, NB + 1).transpose(1, 0, 2))
        in_maps.append(m)
    res = run_bass_kernel_spmd(nc, in_maps, core_ids=list(range(n_cores)))
    return res


def kernel(**inputs):
    cfg = Cfg(T=4096, C=256, NB=2, L=4)
    res = run(inputs, cfg, 8)
    return np.concatenate([r["out"] for r in res.results], axis=0).astype(np.float32)
```
